# Optimizing a Trainium2 kernel written in Bass

```python
import math
import jax, jax.numpy as jnp
from jax import lax
import numpy as np

D_MODEL = 2048
BATCH = 2
SEQ = 8192
DEPTH = 1

CHUNK = 64
Q_BLOCK = 128
EPS = 1e-6
NEG_INF = -1e30

MLA_HEADS = 16
MLA_Q_RANK = 768
MLA_KV_RANK = 512
MLA_NOPE = 128
MLA_ROPE = 64
MLA_V = 128
MLA_QK = MLA_NOPE + MLA_ROPE
ROPE_THETA = 10000.0

DIFF_HEADS = 8
DIFF_HEAD_DIM = 128
DIFF_V = 2 * DIFF_HEAD_DIM

N_BRANCH = 2
MLA_MIX = MLA_HEADS * MLA_V
DIFF_MIX = DIFF_HEADS * DIFF_V

D_FF = -(-8 * D_MODEL // (3 * 256)) * 256

IN_SIZES = (MLA_Q_RANK, MLA_KV_RANK, MLA_ROPE,
            DIFF_HEADS * 2 * DIFF_HEAD_DIM, DIFF_HEADS * 2 * DIFF_HEAD_DIM, DIFF_MIX,
            N_BRANCH * D_MODEL)
D_IN = sum(IN_SIZES)

kernel_name = "hybrid_mla_diffattn_gated_block"


def rmsnorm(x, g):
    xf = x.astype(jnp.float32)
    y = xf * lax.rsqrt(jnp.mean(xf * xf, axis=-1, keepdims=True) + EPS)
    return (y * g.astype(jnp.float32)).astype(x.dtype)


def rope_tables(pos, dim):
    inv = ROPE_THETA ** (-jnp.arange(0, dim, 2, dtype=jnp.float32) / dim)
    ang = pos.astype(jnp.float32)[:, None] * inv[None, :]
    return jnp.cos(ang), jnp.sin(ang)


def apply_rope(x, cos, sin):
    half = x.shape[-1] // 2
    xf = x.astype(jnp.float32)
    x1, x2 = xf[..., :half], xf[..., half:]
    c, s = cos[:, None, :], sin[:, None, :]
    return jnp.concatenate([x1 * c - x2 * s, x2 * c + x1 * s], axis=-1).astype(x.dtype)


def to_blocks(a):
    b, s = a.shape[:2]
    return jnp.moveaxis(a.reshape((b, s // Q_BLOCK, Q_BLOCK) + a.shape[2:]), 1, 0)


def from_blocks(a):
    nb, b, q = a.shape[:3]
    return jnp.moveaxis(a, 0, 1).reshape((b, nb * q) + a.shape[3:])


def chunk_allowed(q_pos, k_pos):
    return (k_pos[None, :] // CHUNK) <= (q_pos[:, None] // CHUNK)


def mla_branch(c_q, c_kv, k_pe, q_norm_g, w_uq, kv_norm_g, w_ukv, cos, sin):
    b, s, _ = c_q.shape
    q = (rmsnorm(c_q, q_norm_g) @ w_uq).reshape(b, s, MLA_HEADS, MLA_QK)
    q_nope = q[..., :MLA_NOPE]
    q_rope = apply_rope(q[..., MLA_NOPE:], cos, sin)
    kv = (rmsnorm(c_kv, kv_norm_g) @ w_ukv).reshape(b, s, MLA_HEADS, MLA_NOPE + MLA_V)
    k_nope, v = kv[..., :MLA_NOPE], kv[..., MLA_NOPE:]
    k_rope = apply_rope(k_pe[:, :, None, :], cos, sin)[:, :, 0, :]
    k_pos = jnp.arange(s)
    scale = 1.0 / math.sqrt(MLA_QK)

    def blk(args):
        qn, qr, bi = args
        q_pos = bi * Q_BLOCK + jnp.arange(Q_BLOCK)
        sc = (jnp.einsum('bqhd,bkhd->bhqk', qn, k_nope).astype(jnp.float32)
              + jnp.einsum('bqhr,bkr->bhqk', qr, k_rope).astype(jnp.float32)) * scale
        sc = jnp.where(chunk_allowed(q_pos, k_pos)[None, None], sc, NEG_INF)
        p = jax.nn.softmax(sc, axis=-1).astype(v.dtype)
        return jnp.einsum('bhqk,bkhd->bqhd', p, v)

    nb = s // Q_BLOCK
    o = lax.map(blk, (to_blocks(q_nope), to_blocks(q_rope), jnp.arange(nb)))
    return from_blocks(o).reshape(b, s, MLA_MIX)


def diff_branch(dq, dk, dv, lq1, lk1, lq2, lk2, sub_g, lambda_init, slopes):
    b, s, _ = dq.shape
    q = dq.reshape(b, s, DIFF_HEADS, 2, DIFF_HEAD_DIM)
    k = dk.reshape(b, s, DIFF_HEADS, 2, DIFF_HEAD_DIM)
    v = dv.reshape(b, s, DIFF_HEADS, DIFF_V)
    lam = (jnp.exp(jnp.sum(lq1.astype(jnp.float32) * lk1.astype(jnp.float32)))
           - jnp.exp(jnp.sum(lq2.astype(jnp.float32) * lk2.astype(jnp.float32)))
           + lambda_init)
    k_pos = jnp.arange(s)
    scale = 1.0 / math.sqrt(DIFF_HEAD_DIM)

    def blk(args):
        qb, bi = args
        q_pos = bi * Q_BLOCK + jnp.arange(Q_BLOCK)
        dist = jnp.abs(q_pos[:, None] - k_pos[None, :]).astype(jnp.float32)
        alibi = -slopes[:, None, None] * dist[None]
        sc = jnp.einsum('bqhmd,bkhmd->bhmqk', qb, k).astype(jnp.float32) * scale
        sc = sc + alibi[None, :, None]
        sc = jnp.where(chunk_allowed(q_pos, k_pos)[None, None, None], sc, NEG_INF)
        p = jax.nn.softmax(sc, axis=-1)
        a = (p[:, :, 0] - lam * p[:, :, 1]).astype(v.dtype)
        return jnp.einsum('bhqk,bkhd->bqhd', a, v)

    nb = s // Q_BLOCK
    o = from_blocks(lax.map(blk, (to_blocks(q), jnp.arange(nb))))
    o = rmsnorm(o, sub_g) * (1.0 - lambda_init)
    return o.reshape(b, s, DIFF_MIX)


def setup_inputs(seed: int = 0) -> dict:
    key = jax.random.key(seed)
    ks = jax.random.split(key, 24)

    def w(k, shape, fan_in):
        return jax.random.normal(k, shape, jnp.float32) * (fan_in ** -0.5)

    def gain(k, shape):
        return 1.0 + 0.02 * jax.random.normal(k, shape, jnp.float32)

    L = DEPTH
    return {
        "x": jax.random.normal(ks[0], (BATCH, SEQ, D_MODEL), jnp.float32),
        "attn_norm_g": gain(ks[1], (L, D_MODEL)),
        "w_in": w(ks[2], (L, D_MODEL, D_IN), D_MODEL),
        "b_gate": 0.01 * jax.random.normal(ks[3], (L, N_BRANCH * D_MODEL), jnp.float32),
        "q_norm_g": gain(ks[4], (L, MLA_Q_RANK)),
        "w_uq": w(ks[5], (L, MLA_Q_RANK, MLA_HEADS * MLA_QK), MLA_Q_RANK),
        "kv_norm_g": gain(ks[6], (L, MLA_KV_RANK)),
        "w_ukv": w(ks[7], (L, MLA_KV_RANK, MLA_HEADS * (MLA_NOPE + MLA_V)), MLA_KV_RANK),
        "lambda_q1": 0.1 * jax.random.normal(ks[8], (L, DIFF_HEAD_DIM), jnp.float32),
        "lambda_k1": 0.1 * jax.random.normal(ks[9], (L, DIFF_HEAD_DIM), jnp.float32),
        "lambda_q2": 0.1 * jax.random.normal(ks[10], (L, DIFF_HEAD_DIM), jnp.float32),
        "lambda_k2": 0.1 * jax.random.normal(ks[11], (L, DIFF_HEAD_DIM), jnp.float32),
        "diff_norm_g": gain(ks[12], (L, DIFF_V)),
        "w_mla_proj": w(ks[13], (L, MLA_MIX, D_MODEL), MLA_MIX),
        "w_diff_proj": w(ks[14], (L, DIFF_MIX, D_MODEL), DIFF_MIX),
        "w_out": w(ks[15], (L, D_MODEL, D_MODEL), D_MODEL),
        "ffn_norm_g": gain(ks[16], (L, D_MODEL)),
        "w_ffn_gate": w(ks[17], (L, D_MODEL, D_FF), D_MODEL),
        "w_ffn_up": w(ks[18], (L, D_MODEL, D_FF), D_MODEL),
        "w_ffn_down": w(ks[19], (L, D_FF, D_MODEL), D_FF),
        "final_norm_g": gain(ks[20], (D_MODEL,)),
    }


def reference(x, attn_norm_g, w_in, b_gate, q_norm_g, w_uq, kv_norm_g, w_ukv,
              lambda_q1, lambda_k1, lambda_q2, lambda_k2, diff_norm_g,
              w_mla_proj, w_diff_proj, w_out, ffn_norm_g, w_ffn_gate, w_ffn_up,
              w_ffn_down, final_norm_g):
    b, s, d = x.shape
    pos = jnp.arange(s)
    cos, sin = rope_tables(pos, MLA_ROPE)
    slopes = jnp.exp2(-8.0 * jnp.arange(1, DIFF_HEADS + 1, dtype=jnp.float32) / DIFF_HEADS)
    split_at = np.cumsum(IN_SIZES)[:-1].tolist()

    for i in range(DEPTH):
        lambda_init = 0.8 - 0.6 * math.exp(-0.3 * i)
        h = rmsnorm(x, attn_norm_g[i])
        proj = h @ w_in[i]
        c_q, c_kv, k_pe, dq, dk, dv, gate_logits = jnp.split(proj, split_at, axis=-1)

        y_mla = mla_branch(c_q, c_kv, k_pe, q_norm_g[i], w_uq[i], kv_norm_g[i], w_ukv[i], cos, sin)
        y_diff = diff_branch(dq, dk, dv, lambda_q1[i], lambda_k1[i], lambda_q2[i], lambda_k2[i],
                             diff_norm_g[i], lambda_init, slopes)

        g = jax.nn.sigmoid((gate_logits + b_gate[i]).astype(jnp.float32)).astype(x.dtype)
        g = g.reshape(b, s, N_BRANCH, d)
        merged = g[:, :, 0] * (y_mla @ w_mla_proj[i]) + g[:, :, 1] * (y_diff @ w_diff_proj[i])
        x = x + merged @ w_out[i]

        h2 = rmsnorm(x, ffn_norm_g[i])
        x = x + (jax.nn.silu(h2 @ w_ffn_gate[i]) * (h2 @ w_ffn_up[i])) @ w_ffn_down[i]

    return rmsnorm(x, final_norm_g)
```

```python
import math
from contextlib import ExitStack

import numpy as np
import concourse.bass as bass
import concourse.mybir as mybir
from concourse.bass_utils import run_bass_kernel_spmd

F32 = mybir.dt.float32
BF16 = mybir.dt.bfloat16
AF = mybir.ActivationFunctionType
ALU = mybir.AluOpType

D = 2048
NH_M = 16
NH_D = 8
QR = 768
KVR = 512
DFF = 5632
EPS = 1e-6
OFF_CQ, OFF_CKV, OFF_KPE, OFF_DQ, OFF_DK, OFF_DV, OFF_G = 0, 768, 1280, 1344, 3392, 5440, 7488
D_IN = 11584
LAMBDA_INIT = 0.8 - 0.6 * math.exp(-0.3 * 0)
ENGS = ["pe", "act", "dve", "pool", "sp"]
NDELTA = 66


class Res:
    __slots__ = ("w", "r", "dsem")

    def __init__(self):
        self.w = {}
        self.r = {}
        self.dsem = None


class Tl:
    __slots__ = ("t", "res")

    def __init__(self, t):
        self.t = t
        self.res = Res()


class Prog:
    def __init__(self, nc, n_dma_sems=80):
        self.nc = nc
        self.top = ExitStack()
        self.semobj = {}
        for e in ENGS:
            self.semobj["e:" + e] = self.top.enter_context(nc.semaphore("sem_" + e))
        self.cnt = {e: 0 for e in ENGS}
        self.dkeys = []
        self.dval = {}
        for i in range(n_dma_sems):
            k = "d:%d" % i
            self.semobj[k] = self.top.enter_context(nc.semaphore("semd_%d" % i))
            self.dkeys.append(k)
            self.dval[k] = 0
        self.known = {e: {} for e in ENGS}
        self.n_inst = 0
        self.uid = 0

    def begin(self):
        self.stack = ExitStack()
        self.ops = {e: [] for e in ENGS}
        self.free_d = list(self.dkeys)
        self.used_d = []
        self.rr = 0

    def sb(self, shape, dtype):
        self.uid += 1
        return Tl(self.stack.enter_context(self.nc.sbuf_tensor("sb%d" % self.uid, list(shape), dtype)))

    def ps(self, shape=(128, 512), dtype=F32):
        self.uid += 1
        return Tl(self.stack.enter_context(self.nc.psum_tensor("ps%d" % self.uid, list(shape), dtype)))

    def _need(self, eng, toks, waits):
        kn = self.known[eng]
        for k, v in toks.items():
            if eng == "pe" and k == "e:pe":
                continue
            if kn.get(k, 0) >= v:
                continue
            if waits.get(k, 0) < v:
                waits[k] = v

    def _deps(self, eng, reads, writes):
        waits = {}
        for r in reads:
            self._need(eng, r.res.w, waits)
        for w in writes:
            self._need(eng, w.res.w, waits)
            self._need(eng, w.res.r, waits)
        kn = self.known[eng]
        for k, v in waits.items():
            kn[k] = v
        return list(waits.items())

    def op(self, eng, fn, reads=(), writes=()):
        waits = self._deps(eng, reads, writes)
        self.cnt[eng] += 1
        key = "e:" + eng
        val = self.cnt[eng]
        for r in reads:
            r.res.r[key] = val
        for w in writes:
            w.res.w[key] = val
        self.ops[eng].append((waits, fn, (key, 1)))

    def dma(self, eng, out, in_, reads=(), writes=(), sem=None):
        waits = self._deps(eng, reads, writes)
        r = sem.res
        if r.dsem is None:
            r.dsem = self.free_d.pop() if eng != "pool" else self.free_d.pop(0)
            self.used_d.append(r.dsem)
        key = r.dsem
        self.dval[key] += 16
        val = self.dval[key]
        for x in reads:
            x.res.r[key] = val
        for x in writes:
            x.res.w[key] = val
        self.ops[eng].append((waits, lambda e: e.dma_start(out=out, in_=in_), (key, 16)))

    def end(self):
        for e in ENGS:
            waits = []
            for x in ENGS:
                k = "e:" + x
                if self.cnt[x] > self.known[e].get(k, 0):
                    waits.append((k, self.cnt[x]))
                    self.known[e][k] = self.cnt[x]
            if e == "sp":
                for k in self.used_d:
                    if self.dval[k] > self.known[e].get(k, 0):
                        waits.append((k, self.dval[k]))
            if waits:
                self.ops[e].append((waits, None, None))
        for e in ENGS:
            for k in self.used_d:
                self.known[e][k] = self.dval[k]
        ops = self.ops
        semobj = self.semobj

        def mk(name):
            def body(e):
                for waits, fn, inc in ops[name]:
                    for k, v in waits:
                        e.wait_ge(semobj[k], v)
                    if fn is not None:
                        ins = fn(e)
                        if inc is not None:
                            ins.then_inc(semobj[inc[0]], inc[1])
            return body

        for name in ENGS:
            self.n_inst += len(ops[name])
        with self.nc.Block() as block:
            block.tensor(mk("pe"))
            block.scalar(mk("act"))
            block.vector(mk("dve"))
            block.gpsimd(mk("pool"))
            block.sync(mk("sp"))
        self.stack.close()
        self.ops = None

    def evac_eng(self):
        self.rr += 1
        return "act" if (self.rr & 1) else "dve"


def copy_op(P, eng, out_ap, in_ap, reads, writes):
    if eng == "act":
        P.op("act", lambda e: e.activation(out=out_ap, in_=in_ap, func=AF.Copy), reads, writes)
    else:
        P.op("dve", lambda e: e.tensor_copy(out=out_ap, in_=in_ap), reads, writes)


def norm_phase(P, src, dst, gcol_dram, N, out_dtype, nfeat=D, T=256):
    KC = nfeat // 128
    P.begin()
    xt = [P.sb([128, KC, T], F32) for _ in range(2)]
    sq = [P.sb([128, KC, T], BF16) for _ in range(2)]
    st = [P.sb([128, KC, T], out_dtype) for _ in range(2)]
    ln = [P.sb([128, T], F32) for _ in range(2)]
    rs = [P.sb([128, T], F32) for _ in range(2)]
    ones = P.sb([128, 128], BF16)
    g = P.sb([128, KC], F32)
    epsc = P.sb([128, 1], F32)
    ps = [P.ps() for _ in range(2)]
    P.op("dve", lambda e: e.memset(ones.t[:], 1.0), (), (ones,))
    P.op("dve", lambda e: e.memset(epsc.t[:], EPS), (), (epsc,))
    P.dma("sp", g.t[:], gcol_dram, (), (g,), sem=g)
    srcv = src.rearrange("(kc p) n -> p kc n", p=128)
    dstv = dst.rearrange("(kc p) n -> p kc n", p=128)
    nt = N // T

    def load(i):
        s = xt[i % 2]
        P.dma("sp", s.t[:], srcv[:, :, i * T:(i + 1) * T], (), (s,), sem=s)

    load(0)
    for i in range(nt):
        if i + 1 < nt:
            load(i + 1)
        x, q, o, l, r, p = xt[i % 2], sq[i % 2], st[i % 2], ln[i % 2], rs[i % 2], ps[i % 2]
        P.op("act", lambda e, x=x, q=q: e.activation(out=q.t[:], in_=x.t[:], func=AF.Square), (x,), (q,))
        for kc in range(KC):
            P.op("pe", lambda e, p=p, q=q, kc=kc: e.matmul(p.t[:, 0:T], lhsT=ones.t[:], rhs=q.t[:, kc, :],
                                                          start=(kc == 0), stop=(kc == KC - 1)),
                 (ones, q), (p,))
        P.op("act", lambda e, p=p, l=l: e.activation(out=l.t[:], in_=p.t[:, 0:T], func=AF.Ln,
                                                     scale=1.0 / nfeat, bias=epsc.t[:]), (p, epsc), (l,))
        P.op("act", lambda e, l=l, r=r: e.activation(out=r.t[:], in_=l.t[:], func=AF.Exp, scale=-0.5), (l,), (r,))
        for kc in range(KC):
            P.op("dve", lambda e, o=o, x=x, r=r, kc=kc: e.scalar_tensor_tensor(
                out=o.t[:, kc, :], in0=x.t[:, kc, :], scalar=g.t[:, kc:kc + 1], in1=r.t[:],
                op0=ALU.mult, op1=ALU.mult), (x, r, g), (o,))
        P.dma("sp", dstv[:, :, i * T:(i + 1) * T], o.t[:], (o,), (), sem=o)
    P.end()


def gemm_phase(P, act_srcs, KC, N, blocks, T=512, CB=512, setup=None, npsum=6):
    P.begin()
    wt = [P.sb([128, KC, CB], BF16) for _ in range(2)]
    at = [P.sb([128, KC, T], BF16) for _ in range(2)]
    pss = [P.ps() for _ in range(npsum)]
    ctx = setup(P) if setup is not None else None
    nt = N // T
    seq = [(b, t) for b in range(len(blocks)) for t in range(nt)]
    psi = [0]

    def load_w(b):
        w = wt[b % 2]
        for (wap, kc0, nkc, off, ncols) in blocks[b]["wsegs"]:
            if isinstance(wap, int):
                P.op("dve", lambda e, w=w, kc0=kc0, nkc=nkc, off=off, ncols=ncols, so=wap: e.tensor_copy(
                    out=w.t[:, kc0:kc0 + nkc, off:off + ncols], in_=w.t[:, kc0:kc0 + nkc, so:so + ncols]), (w,), (w,))
                continue
            P.dma("pool", w.t[:, kc0:kc0 + nkc, off:off + ncols],
                  wap.rearrange("(kc p) n -> p kc n", p=128), (), (w,), sem=w)

    def load_a(j):
        a = at[j % 2]
        t = seq[j][1]
        for (aap, kc0, nkc) in act_srcs:
            P.dma("sp", a.t[:, kc0:kc0 + nkc, :],
                  aap.rearrange("(kc p) n -> p kc n", p=128)[:, :, t * T:(t + 1) * T], (), (a,), sem=a)

    load_w(0)
    load_a(0)
    for j, (b, t) in enumerate(seq):
        if t == 0 and b + 1 < len(blocks):
            load_w(b + 1)
        if j + 1 < len(seq):
            load_a(j + 1)
        w = wt[b % 2]
        a = at[j % 2]
        for grp in blocks[b]["groups"]:
            accs = grp["accs"]
            if accs[0][0] == "fm":
                pts = []
                for (kind, off, ncols, klo, khi) in accs:
                    p = pss[psi[0] % npsum]
                    psi[0] += 1
                    for kc in range(klo, khi):
                        P.op("pe", lambda e, p=p, w=w, a=a, kc=kc, off=off, ncols=ncols, klo=klo, khi=khi:
                             e.matmul(p.t[0:ncols, 0:T], lhsT=w.t[:, kc, off:off + ncols], rhs=a.t[:, kc, :],
                                      start=(kc == klo), stop=(kc == khi - 1)), (w, a), (p,))
                    pts.append(p)
                grp["epi"](P, ctx, t * T, T, pts, grp)
            else:
                (kind, off, ncols, klo, khi) = accs[0]
                for ts in range(T // 128):
                    p = pss[psi[0] % npsum]
                    psi[0] += 1
                    for kc in range(klo, khi):
                        P.op("pe", lambda e, p=p, w=w, a=a, kc=kc, off=off, ncols=ncols, klo=klo, khi=khi, ts=ts:
                             e.matmul(p.t[:, 0:ncols], lhsT=a.t[:, kc, ts * 128:(ts + 1) * 128],
                                      rhs=w.t[:, kc, off:off + ncols],
                                      start=(kc == klo), stop=(kc == khi - 1)), (w, a), (p,))
                    grp["epi"](P, ctx, t * T + ts * 128, 128, [p], grp)
    P.end()


class Stager:
    def __init__(self, P, shape, dtype, n=4):
        self.tl = [P.sb(shape, dtype) for _ in range(n)]
        self.i = 0

    def get(self):
        t = self.tl[self.i % len(self.tl)]
        self.i += 1
        return t


def epi_store_fm(dst, row_of):
    def epi(P, ctx, tok0, ntok, pts, grp):
        p = pts[0]
        ncols = grp["accs"][0][2]
        s = ctx["stg"].get()
        copy_op(P, P.evac_eng(), s.t[0:ncols, 0:ntok], p.t[0:ncols, 0:ntok], (p,), (s,))
        r0 = grp["row0"]
        P.dma("sp", dst[r0:r0 + ncols, tok0:tok0 + ntok], s.t[0:ncols, 0:ntok], (s,), (), sem=s)
    return epi


def epi_store_fm32(dst):
    def epi(P, ctx, tok0, ntok, pts, grp):
        p = pts[0]
        ncols = grp["accs"][0][2]
        s = ctx["stg32"].get()
        copy_op(P, P.evac_eng(), s.t[0:ncols, 0:ntok], p.t[0:ncols, 0:ntok], (p,), (s,))
        r0 = grp["row0"]
        P.dma("sp", dst[r0:r0 + ncols, tok0:tok0 + ntok], s.t[0:ncols, 0:ntok], (s,), (), sem=s)
    return epi


def epi_store_tm(dst):
    def epi(P, ctx, tok0, ntok, pts, grp):
        p = pts[0]
        ncols = grp["accs"][0][2]
        s = ctx["stg"].get()
        copy_op(P, P.evac_eng(), s.t[:, 0:ncols], p.t[:, 0:ncols], (p,), (s,))
        c0 = grp["col0"]
        P.dma("sp", dst[tok0:tok0 + 128, c0:c0 + ncols], s.t[:, 0:ncols], (s,), (), sem=s)
    return epi


def setup_stg(dtype=BF16, n=4):
    def setup(P):
        return {"stg": Stager(P, [128, 512], dtype, n)}
    return setup


def epi_gate(dst):
    def epi(P, ctx, tok0, ntok, pts, grp):
        p = pts[0]
        s = ctx["stg"].get()
        bcol = grp["bcol"]
        bg = ctx["bg"]
        P.op("act", lambda e: e.activation(out=s.t[:, 0:ntok], in_=p.t[:, 0:ntok], func=AF.Sigmoid,
                                           bias=bg.t[:, bcol:bcol + 1]), (p, bg), (s,))
        r0 = grp["row0"]
        P.dma("sp", dst[r0:r0 + 128, tok0:tok0 + ntok], s.t[:, 0:ntok], (s,), (), sem=s)
    return epi


def epi_rope(dst, cos_d, sin_d):
    def epi(P, ctx, tok0, ntok, pts, grp):
        pa, pb = pts
        k = ctx["cs_i"]
        ctx["cs_i"] += 1
        ct, sn = ctx["cos"][k % 2], ctx["sin"][k % 2]
        P.dma("sp", ct.t[0:64, 0:ntok], cos_d[:, tok0:tok0 + ntok], (), (ct,), sem=ct)
        P.dma("sp", sn.t[0:64, 0:ntok], sin_d[:, tok0:tok0 + ntok], (), (sn,), sem=sn)
        t1, t2 = ctx["rt1"][k % 2], ctx["rt2"][k % 2]
        P.op("dve", lambda e: e.tensor_tensor(out=t1.t[0:64, 0:ntok], in0=pa.t[0:64, 0:ntok],
                                              in1=ct.t[0:64, 0:ntok], op=ALU.mult), (pa, ct), (t1,))
        P.op("dve", lambda e: e.tensor_tensor(out=t2.t[0:64, 0:ntok], in0=pb.t[0:64, 0:ntok],
                                              in1=sn.t[0:64, 0:ntok], op=ALU.mult), (pb, sn), (t2,))
        s = ctx["stg"].get()
        P.op("dve", lambda e: e.tensor_tensor(out=s.t[0:64, 0:ntok], in0=t1.t[0:64, 0:ntok],
                                               in1=t2.t[0:64, 0:ntok], op=ALU.add), (t1, t2), (s,))
        r0 = grp["row0"]
        P.dma("sp", dst[r0:r0 + 64, tok0:tok0 + ntok], s.t[0:64, 0:ntok], (s,), (), sem=s)
    return epi


def setup_rope(extra=None):
    def setup(P):
        ctx = {"stg": Stager(P, [128, 512], BF16, 4), "stg32": Stager(P, [128, 512], F32, 4), "cs_i": 0,
               "cos": [P.sb([128, 512], F32) for _ in range(2)],
               "sin": [P.sb([128, 512], F32) for _ in range(2)],
               "rt1": [P.sb([128, 512], F32) for _ in range(2)],
               "rt2": [P.sb([128, 512], F32) for _ in range(2)]}
        if extra is not None:
            extra(P, ctx)
        return ctx
    return setup


def epi_resid(dst, resid):
    def epi(P, ctx, tok0, ntok, pts, grp):
        p = pts[0]
        r0 = grp["row0"]
        k = ctx["r_i"]
        ctx["r_i"] += 1
        rt = ctx["rt"][k % 3]
        P.dma("sp", rt.t[:, 0:ntok], resid[r0:r0 + 128, tok0:tok0 + ntok], (), (rt,), sem=rt)
        s = ctx["stg"].get()
        P.op("dve", lambda e: e.tensor_tensor(out=s.t[:, 0:ntok], in0=p.t[:, 0:ntok], in1=rt.t[:, 0:ntok],
                                              op=ALU.add), (p, rt), (s,))
        P.dma("sp", dst[r0:r0 + 128, tok0:tok0 + ntok], s.t[:, 0:ntok], (s,), (), sem=s)
    return epi


def setup_resid(P):
    return {"stg": Stager(P, [128, 512], F32, 3), "r_i": 0, "rt": [P.sb([128, 512], F32) for _ in range(3)]}


def epi_merge(dst, gT):
    def epi(P, ctx, tok0, ntok, pts, grp):
        pa, pb = pts
        r0 = grp["row0"]
        k = ctx["r_i"]
        ctx["r_i"] += 1
        g0, g1 = ctx["g0"][k % 2], ctx["g1"][k % 2]
        t1, t2 = ctx["t1"][k % 2], ctx["t2"][k % 2]
        P.dma("sp", g0.t[:, 0:ntok], gT[r0:r0 + 128, tok0:tok0 + ntok], (), (g0,), sem=g0)
        P.dma("sp", g1.t[:, 0:ntok], gT[D + r0:D + r0 + 128, tok0:tok0 + ntok], (), (g1,), sem=g1)
        P.op("dve", lambda e: e.tensor_tensor(out=t1.t[:, 0:ntok], in0=pa.t[:, 0:ntok], in1=g0.t[:, 0:ntok],
                                              op=ALU.mult), (pa, g0), (t1,))
        P.op("dve", lambda e: e.tensor_tensor(out=t2.t[:, 0:ntok], in0=pb.t[:, 0:ntok], in1=g1.t[:, 0:ntok],
                                              op=ALU.mult), (pb, g1), (t2,))
        s = ctx["stg"].get()
        P.op("dve", lambda e: e.tensor_tensor(out=s.t[:, 0:ntok], in0=t1.t[:, 0:ntok], in1=t2.t[:, 0:ntok],
                                               op=ALU.add), (t1, t2), (s,))
        P.dma("sp", dst[r0:r0 + 128, tok0:tok0 + ntok], s.t[:, 0:ntok], (s,), (), sem=s)
    return epi


def setup_merge(P):
    return {"stg": Stager(P, [128, 512], BF16, 3), "r_i": 0,
            "g0": [P.sb([128, 512], F32) for _ in range(2)], "g1": [P.sb([128, 512], F32) for _ in range(2)],
            "t1": [P.sb([128, 512], F32) for _ in range(2)], "t2": [P.sb([128, 512], F32) for _ in range(2)]}


def epi_swiglu(dst):
    def epi(P, ctx, tok0, ntok, pts, grp):
        pg, pu = pts
        r0 = grp["row0"]
        k = ctx["r_i"]
        ctx["r_i"] += 1
        t1 = ctx["t1"][k % 3]
        P.op("act", lambda e: e.activation(out=t1.t[:, 0:ntok], in_=pg.t[:, 0:ntok], func=AF.Silu), (pg,), (t1,))
        s = ctx["stg"].get()
        P.op("dve", lambda e: e.tensor_tensor(out=s.t[:, 0:ntok], in0=pu.t[:, 0:ntok], in1=t1.t[:, 0:ntok],
                                              op=ALU.mult), (pu, t1), (s,))
        P.dma("sp", dst[r0:r0 + 128, tok0:tok0 + ntok], s.t[:, 0:ntok], (s,), (), sem=s)
    return epi


def setup_swiglu(P):
    return {"stg": Stager(P, [128, 512], BF16, 3), "r_i": 0, "t1": [P.sb([128, 512], F32) for _ in range(3)]}


def latent_blocks(w_in, off, nlat, rawdst, rope=None):
    ncl = nlat * 128
    wsegs = [(w_in[:, off:off + ncl], 0, 16, 0, ncl)]
    groups = [dict(accs=[("fm", m * 128, 128, 0, 16)], row0=m * 128, epi=epi_store_fm32(rawdst)) for m in range(nlat)]
    CB = ncl
    if rope is not None:
        (rdst, cos_d, sin_d, roff) = rope
        wsegs.append((w_in[:, roff:roff + 64], 0, 16, ncl, 64))
        wsegs.append((ncl + 32, 0, 16, ncl + 64, 32))
        wsegs.append((ncl, 0, 16, ncl + 96, 32))
        groups.append(dict(accs=[("fm", ncl, 64, 0, 16), ("fm", ncl + 64, 64, 0, 16)], row0=0,
                           epi=epi_rope(rdst, cos_d, sin_d)))
        CB = ncl + 128
    return [dict(wsegs=wsegs, groups=groups)], setup_rope(), CB


def attn_phase(P, S, So, units, consts, kind):
    nq = So // 256
    nkt_all = S // 128
    dvv = 128 if kind == "mla" else 256
    P.begin()
    kn = [P.sb([128, S], BF16) for _ in range(2)]
    qn = [P.sb([128, So], BF16) for _ in range(2)]
    vt = [P.sb([128, nkt_all, dvv + 1], BF16) for _ in range(2)]
    ident = P.sb([128, 128], BF16)
    P.dma("pool", ident.t[:], consts["ident"], (), (ident,), sem=ident)
    for v in vt:
        P.op("pool", lambda e, v=v: e.memset(v.t[:, :, dvv:dvv + 1], 1.0), (), (v,))
    NPT = 6
    pt = [P.sb([128, 256], BF16) for _ in range(NPT)]
    nS = 3
    sbank = [P.ps() for _ in range(nS)]
    tpb = P.ps([128, 1024], BF16)
    tpr = [Res(), Res()]
    if kind == "mla":
        kr = P.sb([64, S], BF16)
        qr = [P.sb([64, So], BF16) for _ in range(2)]
        P.dma("sp", kr.t[:], consts["krT"], (), (kr,), sem=kr)
        msk = P.sb([128, 8, 256], F32)
        P.dma("sp", msk.t[:], consts["mask_mla"], (), (msk,), sem=msk)
        obank = [P.ps() for _ in range(2)]
        ybf = [P.sb([128, 128], BF16) for _ in range(4)]
        ystg = [P.sb([128, 256], BF16) for _ in range(2)]
        rsm = [P.sb([128, 1], F32) for _ in range(4)]
    else:
        mskd = [P.sb([128, 8, 256], F32) for _ in range(2)]
        abias = P.sb([128, NH_D * 80], F32)
        P.dma("sp", abias.t[:], consts["abias"], (), (abias,), sem=abias)
        obank = [P.ps() for _ in range(4)]
        o1n = [P.sb([128, 2, 256], F32) for _ in range(nq)]
        ofull = [P.sb([128, 256], F32) for _ in range(2)]
        junk = P.sb([128, 256], F32)
        ybf = [P.sb([128, 256], BF16) for _ in range(2)]
        ystg = [P.sb([128, 2, 256], BF16) for _ in range(2)]
        rsm = [P.sb([128, 1], F32) for _ in range(4)]
        r2l = [P.sb([128, 1], F32) for _ in range(2)]
        ssq = [P.sb([128, 1], F32) for _ in range(2)]
        lnv = [P.sb([128, 1], F32) for _ in range(2)]
        rstd = [P.sb([128, 1], F32) for _ in range(2)]
        epsc = P.sb([128, 1], F32)
        P.op("dve", lambda e: e.memset(epsc.t[:], EPS), (), (epsc,))
        gsub = P.sb([128, 256], F32)
        P.dma("sp", gsub.t[:], consts["gsub"], (), (gsub,), sem=gsub)
        P.op("act", lambda e: e.activation(out=gsub.t[:], in_=gsub.t[:], func=AF.Copy, scale=1.0 - LAMBDA_INIT),
             (gsub,), (gsub,))
        lt = [P.sb([128, 128], F32) for _ in range(4)]
        for i, nm in enumerate(["lq1", "lk1", "lq2", "lk2"]):
            P.dma("sp", lt[i].t[:], consts[nm], (), (lt[i],), sem=lt[i])
        pr = [P.sb([128, 128], F32) for _ in range(2)]
        sm = [P.sb([128, 1], F32) for _ in range(2)]
        ex = [P.sb([128, 1], F32) for _ in range(2)]
        neglam = P.sb([128, 1], F32)
        for i in range(2):
            P.op("dve", lambda e, i=i: e.tensor_tensor(out=pr[i].t[:], in0=lt[2 * i].t[:], in1=lt[2 * i + 1].t[:],
                                                       op=ALU.mult), (lt[2 * i], lt[2 * i + 1]), (pr[i],))
            P.op("act", lambda e, i=i: e.activation(out=junk.t[:, 0:128], in_=pr[i].t[:], func=AF.Copy,
                                                    accum_out=sm[i].t[:]), (pr[i],), (junk, sm[i]))
            P.op("act", lambda e, i=i: e.activation(out=ex[i].t[:], in_=sm[i].t[:], func=AF.Exp), (sm[i],), (ex[i],))
        P.op("dve", lambda e: e.tensor_tensor(out=neglam.t[:], in0=ex[1].t[:], in1=ex[0].t[:], op=ALU.subtract),
             (ex[0], ex[1]), (neglam,))
        P.op("dve", lambda e: e.tensor_scalar(out=neglam.t[:], in0=neglam.t[:], scalar1=-LAMBDA_INIT, scalar2=None,
                                              op0=ALU.add), (neglam,), (neglam,))

    vloaded = {}

    def load_unit(u):
        un = units[u]
        b = u % 2
        P.dma("sp", kn[b].t[:], un["kT"], (), (kn[b],), sem=kn[b])
        P.dma("sp", qn[b].t[:], un["qT"], (), (qn[b],), sem=qn[b])
        if kind == "mla":
            P.dma("sp", qr[b].t[:], un["qrT"], (), (qr[b],), sem=qr[b])
        vk = un["vkey"]
        if vk not in vloaded:
            vb = len(vloaded) % 2
            vloaded[vk] = vb
            vv = un["v"].rearrange("(t p) c -> p t c", p=128)
            step = 16
            for t0 in range(0, nkt_all, step):
                t1 = min(nkt_all, t0 + step)
                P.dma("sp", vt[vb].t[:, t0:t1, 0:dvv], vv[:, t0:t1, :], (), (vt[vb],), sem=vt[vb])
            if kind == "diff":
                P.dma("sp", mskd[vb].t[:], consts["mask_diff"][un["h"]], (), (mskd[vb],), sem=mskd[vb])

    pairs = []
    for u in range(len(units)):
        for i in range(nq):
            nk = 8 * (i + 1)
            for kt in range(nk):
                pairs.append((u, i, kt, nk))
    LA = 2
    cnt = {"s": 0, "p": 0, "e": 0}
    pend = {}

    def do_qk(n):
        (u, i, kt, nk) = pairs[n]
        un = units[u]
        b = u % 2
        sp_ = sbank[cnt["s"] % nS]
        cnt["s"] += 1
        q0 = i * 256
        k0 = kt * 128
        if kind == "mla":
            P.op("pe", lambda e: e.matmul(sp_.t[:, 0:256], lhsT=kn[b].t[:, k0:k0 + 128], rhs=qn[b].t[:, q0:q0 + 256],
                                          start=True, stop=False), (kn[b], qn[b]), (sp_,))
            P.op("pe", lambda e: e.matmul(sp_.t[:, 0:256], lhsT=kr.t[0:64, k0:k0 + 128],
                                          rhs=qr[b].t[0:64, q0:q0 + 256], start=False, stop=True),
                 (kr, qr[b]), (sp_,))
        else:
            P.op("pe", lambda e: e.matmul(sp_.t[:, 0:256], lhsT=kn[b].t[:, k0:k0 + 128], rhs=qn[b].t[:, q0:q0 + 256],
                                          start=True, stop=True), (kn[b], qn[b]), (sp_,))
        p = pt[cnt["p"] % NPT]
        cnt["p"] += 1
        sc = un["scale"]
        if kind == "mla":
            P.op("act", lambda e: e.activation(out=p.t[:], in_=sp_.t[:, 0:256], func=AF.Exp, scale=sc), (sp_,), (p,))
        else:
            h = un["h"]
            dst = kt - 8 * i
            if h == 0:
                for j in range(2):
                    bi = h * 80 + dst - 2 * j + 66
                    P.op("act", lambda e, j=j, bi=bi: e.activation(
                        out=p.t[:, j * 128:(j + 1) * 128], in_=sp_.t[:, j * 128:(j + 1) * 128], func=AF.Exp,
                        scale=sc, bias=abias.t[:, bi:bi + 1]), (sp_, abias), (p,))
            else:
                bi = h * 80 + dst + 66
                P.op("act", lambda e: e.activation(out=p.t[:], in_=sp_.t[:, 0:256], func=AF.Exp, scale=sc,
                                                   bias=abias.t[:, bi:bi + 1]), (sp_, abias), (p,))
        if kt >= nk - 8:
            kk = kt - (nk - 8)
            if kind == "mla":
                P.op("dve", lambda e: e.tensor_tensor(out=p.t[:], in0=p.t[:], in1=msk.t[:, kk, :], op=ALU.mult),
                     (p, msk), (p,))
            else:
                mk_ = mskd[vloaded[un["vkey"]]]
                P.op("dve", lambda e: e.tensor_tensor(out=p.t[:], in0=p.t[:], in1=mk_.t[:, kk, :], op=ALU.mult),
                     (p, mk_), (p,))
        pend[n] = p

    def do_pv(n):
        (u, i, kt, nk) = pairs[n]
        un = units[u]
        p = pend.pop(n)
        if i == 0 and kt == 0 and u + 1 < len(units):
            load_unit(u + 1)
        v = vt[vloaded[un["vkey"]]]
        ei = u * nq + i
        if kind == "mla":
            ob = obank[ei % 2]
            for qs in range(2):
                P.op("pe", lambda e, qs=qs: e.matmul(ob.t[:, qs * 129:(qs + 1) * 129], lhsT=p.t[:, qs * 128:(qs + 1) * 128],
                                                     rhs=v.t[:, kt, :], start=(kt == 0 and qs == 0),
                                                     stop=(kt == nk - 1), skip_group_check=True), (p, v), (ob,))
            if kt == nk - 1:
                ys = ystg[ei % 2]
                tr = tpr[ei % 2]
                for qs in range(2):
                    r = rsm[cnt["e"] % 4]
                    yb = ybf[cnt["e"] % 4]
                    cnt["e"] += 1
                    P.op("dve", lambda e, r=r, qs=qs: e.reciprocal(out=r.t[:], in_=ob.t[:, qs * 129 + 128:qs * 129 + 129]),
                         (ob,), (r,))
                    P.op("dve", lambda e, r=r, yb=yb, qs=qs: e.tensor_scalar(
                        out=yb.t[:], in0=ob.t[:, qs * 129:qs * 129 + 128], scalar1=r.t[:], scalar2=None,
                        op0=ALU.mult), (ob, r), (yb,))
                    c0 = (ei % 2) * 256 + qs * 128
                    P.op("pe", lambda e, yb=yb, c0=c0: e.transpose(tpb.t[:, c0:c0 + 128], yb.t[:], ident.t[:]),
                         (yb, ident), (_R(tr),))
                c0 = (ei % 2) * 256
                copy_op(P, "dve", ys.t[:], tpb.t[:, c0:c0 + 256], (_R(tr),), (ys,))
                h = un["h"]
                P.dma("sp", consts["yT"][h * 128:(h + 1) * 128, i * 256:(i + 1) * 256], ys.t[:], (ys,), (), sem=ys)
        else:
            m = un["m"]
            obs = [obank[(ei % 2) * 2 + qs] for qs in range(2)]
            for qs in range(2):
                P.op("pe", lambda e, qs=qs: e.matmul(obs[qs].t[:, 0:257], lhsT=p.t[:, qs * 128:(qs + 1) * 128],
                                                     rhs=v.t[:, kt, :], start=(kt == 0), stop=(kt == nk - 1)),
                     (p, v), (obs[qs],))
            if kt == nk - 1:
                h = un["h"]
                for qs in range(2):
                    ob = obs[qs]
                    r = rsm[cnt["e"] % 4]
                    cnt["e"] += 1
                    P.op("dve", lambda e, r=r, ob=ob: e.reciprocal(out=r.t[:], in_=ob.t[:, 256:257]), (ob,), (r,))
                    if m == 0:
                        P.op("dve", lambda e, r=r, ob=ob, qs=qs: e.tensor_scalar(
                            out=o1n[i].t[:, qs, :], in0=ob.t[:, 0:256], scalar1=r.t[:], scalar2=None, op0=ALU.mult),
                            (ob, r), (o1n[i],))
                    else:
                        k2 = cnt["e"] % 2
                        rl, of, sq_, ln_, rs_, yb = r2l[k2], ofull[k2], ssq[k2], lnv[k2], rstd[k2], ybf[k2]
                        P.op("dve", lambda e, r=r, rl=rl: e.tensor_tensor(out=rl.t[:], in0=r.t[:], in1=neglam.t[:],
                                                                          op=ALU.mult), (r, neglam), (rl,))
                        P.op("dve", lambda e, rl=rl, of=of, ob=ob, qs=qs: e.scalar_tensor_tensor(
                            out=of.t[:], in0=ob.t[:, 0:256], scalar=rl.t[:], in1=o1n[i].t[:, qs, :],
                            op0=ALU.mult, op1=ALU.add), (ob, rl, o1n[i]), (of,))
                        P.op("act", lambda e, of=of, sq_=sq_: e.activation(out=junk.t[:], in_=of.t[:], func=AF.Square,
                                                                           accum_out=sq_.t[:]), (of,), (junk, sq_))
                        P.op("act", lambda e, sq_=sq_, ln_=ln_: e.activation(out=ln_.t[:], in_=sq_.t[:], func=AF.Ln,
                                                                             scale=1.0 / 256, bias=epsc.t[:]),
                             (sq_, epsc), (ln_,))
                        P.op("act", lambda e, ln_=ln_, rs_=rs_: e.activation(out=rs_.t[:], in_=ln_.t[:], func=AF.Exp,
                                                                             scale=-0.5), (ln_,), (rs_,))
                        P.op("dve", lambda e, yb=yb, of=of, rs_=rs_: e.scalar_tensor_tensor(
                            out=yb.t[:], in0=of.t[:], scalar=rs_.t[:], in1=gsub.t[:], op0=ALU.mult, op1=ALU.mult),
                            (of, rs_, gsub), (yb,))
                        tr = tpr[ei % 2]
                        for c in range(2):
                            c0 = (ei % 2) * 512 + c * 256 + qs * 128
                            P.op("pe", lambda e, yb=yb, c=c, c0=c0: e.transpose(
                                tpb.t[:, c0:c0 + 128], yb.t[:, c * 128:(c + 1) * 128], ident.t[:]),
                                (yb, ident), (_R(tr),))
                if m == 1:
                    ys = ystg[ei % 2]
                    tr = tpr[ei % 2]
                    c0 = (ei % 2) * 512
                    copy_op(P, "dve", ys.t[:, :, :], tpb.t[:, c0:c0 + 512].rearrange("p (c q) -> p c q", c=2),
                            (_R(tr),), (ys,))
                    P.dma("sp", consts["yT"][h * 256:(h + 1) * 256, i * 256:(i + 1) * 256].rearrange(
                        "(c p) q -> p c q", p=128), ys.t[:, :, :], (ys,), (), sem=ys)

    load_unit(0)
    for n in range(len(pairs) + LA):
        if n < len(pairs):
            do_qk(n)
        if n >= LA:
            do_pv(n - LA)
    P.end()


class _R:
    __slots__ = ("res",)

    def __init__(self, res):
        self.res = res


def build_program(S, debug=False, upto=99):
    So = S // 4
    To = min(512, So)
    nc = bass.Bass("TRN2", target_bir_lowering=False)
    P = Prog(nc)

    def din(name, shape, dt=F32):
        return nc.dram_tensor(name, list(shape), dt, kind="ExternalInput").ap()

    def scr(name, shape, dt):
        kind = "ExternalOutput" if debug else "Internal"
        return nc.dram_tensor(name, list(shape), dt, kind=kind).ap()

    xall = din("xall", [D, S])
    xown = din("xown", [D, So])
    w_in = din("w_in", [D, D_IN])
    w_uq = din("w_uq", [QR, NH_M * 192])
    w_ukv = din("w_ukv", [KVR, NH_M * 256])
    w_mla = din("w_mla_proj", [D, D])
    w_dif = din("w_diff_proj", [D, D])
    w_out = din("w_out", [D, D])
    w_fg = din("w_ffn_gate", [D, DFF])
    w_fu = din("w_ffn_up", [D, DFF])
    w_fd = din("w_ffn_down", [DFF, D])
    g_attn = din("g_attn", [128, 16])
    g_q = din("g_q", [128, 6])
    g_kv = din("g_kv", [128, 4])
    g_ffn = din("g_ffn", [128, 16])
    g_fin = din("g_fin", [128, 16])
    b_gate = din("b_gate", [128, 32])
    gsub = din("gsub", [128, 256])
    lq1 = din("lq1", [128, 128])
    lk1 = din("lk1", [128, 128])
    lq2 = din("lq2", [128, 128])
    lk2 = din("lk2", [128, 128])
    cos_all = din("cos_all", [64, S])
    sin_all = din("sin_all", [64, S])
    cos_own = din("cos_own", [64, So])
    sin_own = din("sin_own", [64, So])
    abias = din("abias", [128, NH_D * 80])
    mask_mla = din("mask_mla", [128, 8, 256])
    mask_diff = din("mask_diff", [NH_D, 128, 8, 256])
    ident = din("ident", [128, 128])

    hT_all = scr("hT_all", [D, S], BF16)
    hT_own = scr("hT_own", [D, So], BF16)
    dkT = scr("dkT", [D, S], BF16)
    dv = scr("dv", [S, D], BF16)
    ckvnT = scr("ckvnT", [KVR, S], BF16)
    ckvraw = scr("ckvraw", [KVR, S], F32)
    cqraw = scr("cqraw", [QR, So], F32)
    krT = scr("krT", [64, S], BF16)
    knT = scr("knT", [D, S], BF16)
    vm = scr("vm", [S, D], BF16)
    dqT = scr("dqT", [D, So], BF16)
    gT = scr("gT", [2 * D, So], F32)
    cqnT = scr("cqnT", [QR, So], BF16)
    qnT = scr("qnT", [D, So], BF16)
    qrT = scr("qrT", [NH_M * 64, So], BF16)
    ymT = scr("ymT", [D, So], BF16)
    ydT = scr("ydT", [D, So], BF16)
    mT = scr("mT", [D, So], BF16)
    x1T = scr("x1T", [D, So], F32)
    h2T = scr("h2T", [D, So], BF16)
    aT = scr("aT", [DFF, So], BF16)
    x2T = scr("x2T", [D, So], F32)
    outT = nc.dram_tensor("outT", [D, So], F32, kind="ExternalOutput").ap()

    def done():
        P.top.close()
        return nc, P

    norm_phase(P, xall, hT_all, g_attn, S, BF16)
    norm_phase(P, xown, hT_own, g_attn, So, BF16)
    if upto <= 0:
        return done()

    blocks = []
    for cb in range(4):
        c0 = OFF_DK + cb * 512
        blocks.append(dict(
            wsegs=[(w_in[:, c0:c0 + 512], 0, 16, 0, 512)],
            groups=[dict(accs=[("fm", m * 128, 128, 0, 16)], row0=cb * 512 + m * 128,
                         epi=epi_store_fm(dkT, None)) for m in range(4)]))
    for cb in range(4):
        c0 = OFF_DV + cb * 512
        blocks.append(dict(
            wsegs=[(w_in[:, c0:c0 + 512], 0, 16, 0, 512)],
            groups=[dict(accs=[("tm", 0, 512, 0, 16)], col0=cb * 512, epi=epi_store_tm(dv))]))
    gemm_phase(P, [(hT_all, 0, 16)], 16, S, blocks, setup=setup_stg())

    if upto <= 0.3:
        return done()
    blocks, setup, CB = latent_blocks(w_in, OFF_CKV, 4, ckvraw, rope=(krT, cos_all, sin_all, OFF_KPE))
    gemm_phase(P, [(hT_all, 0, 16)], 16, S, blocks, CB=CB, setup=setup)
    norm_phase(P, ckvraw, ckvnT, g_kv, S, BF16, nfeat=KVR)

    if upto <= 0.6:
        return done()
    blocks = []
    for hb in range(NH_M // 2):
        groups = []
        for hh in range(2):
            h = hb * 2 + hh
            groups.append(dict(accs=[("fm", hh * 256, 128, 0, 4)], row0=h * 128, epi=epi_store_fm(knT, None)))
            groups.append(dict(accs=[("tm", hh * 256 + 128, 128, 0, 4)], col0=h * 128, epi=epi_store_tm(vm)))
        blocks.append(dict(wsegs=[(w_ukv[:, hb * 512:(hb + 1) * 512], 0, 4, 0, 512)], groups=groups))
    gemm_phase(P, [(ckvnT, 0, 4)], 4, S, blocks, setup=setup_stg())
    if upto <= 1:
        return done()

    blocks = []
    for cb in range(4):
        c0 = OFF_DQ + cb * 512
        blocks.append(dict(
            wsegs=[(w_in[:, c0:c0 + 512], 0, 16, 0, 512)],
            groups=[dict(accs=[("fm", m * 128, 128, 0, 16)], row0=cb * 512 + m * 128,
                         epi=epi_store_fm(dqT, None)) for m in range(4)]))
    gemm_phase(P, [(hT_own, 0, 16)], 16, So, blocks, T=To, setup=setup_stg())

    blocks = []
    for cb in range(8):
        c0 = OFF_G + cb * 512
        blocks.append(dict(
            wsegs=[(w_in[:, c0:c0 + 512], 0, 16, 0, 512)],
            groups=[dict(accs=[("fm", m * 128, 128, 0, 16)], row0=cb * 512 + m * 128, bcol=cb * 4 + m,
                         epi=epi_gate(gT)) for m in range(4)]))

    def setup_gate(P):
        ctx = {"stg": Stager(P, [128, 512], F32, 4), "bg": P.sb([128, 32], F32)}
        P.dma("sp", ctx["bg"].t[:], b_gate, (), (ctx["bg"],), sem=ctx["bg"])
        return ctx
    gemm_phase(P, [(hT_own, 0, 16)], 16, So, blocks, T=To, setup=setup_gate)

    blocks, setup, CB = latent_blocks(w_in, OFF_CQ, 6, cqraw)
    gemm_phase(P, [(hT_own, 0, 16)], 16, So, blocks, T=To, CB=CB, setup=setup)
    norm_phase(P, cqraw, cqnT, g_q, So, BF16, nfeat=QR)

    blocks = []
    for hb in range(NH_M // 2):
        wsegs, groups = [], []
        for hh in range(2):
            h = hb * 2 + hh
            o = hh * 256
            wsegs.append((w_uq[:, h * 192:h * 192 + 192], 0, 6, o, 192))
            wsegs.append((o + 160, 0, 6, o + 192, 32))
            wsegs.append((o + 128, 0, 6, o + 224, 32))
            groups.append(dict(accs=[("fm", o, 128, 0, 6)], row0=h * 128, epi=epi_store_fm(qnT, None)))
            groups.append(dict(accs=[("fm", o + 128, 64, 0, 6), ("fm", o + 192, 64, 0, 6)], row0=h * 64,
                               epi=epi_rope(qrT, cos_own, sin_own)))
        blocks.append(dict(wsegs=wsegs, groups=groups))
    gemm_phase(P, [(cqnT, 0, 6)], 6, So, blocks, T=To, setup=setup_rope())
    if upto <= 2:
        return done()

    consts = dict(ident=ident, krT=krT, mask_mla=mask_mla, yT=ymT)
    units = [dict(kT=knT[h * 128:(h + 1) * 128, :], qT=qnT[h * 128:(h + 1) * 128, :],
                  qrT=qrT[h * 64:(h + 1) * 64, :], v=vm[:, h * 128:(h + 1) * 128], vkey=h, h=h, m=0,
                  scale=1.0 / math.sqrt(192.0)) for h in range(NH_M)]
    attn_phase(P, S, So, units, consts, "mla")
    if upto <= 3:
        return done()

    consts = dict(ident=ident, abias=abias, mask_diff=mask_diff, yT=ydT, gsub=gsub, lq1=lq1, lk1=lk1, lq2=lq2, lk2=lk2)
    units = []
    for h in range(NH_D):
        for m in range(2):
            r0 = (h * 2 + m) * 128
            units.append(dict(kT=dkT[r0:r0 + 128, :], qT=dqT[r0:r0 + 128, :], v=dv[:, h * 256:(h + 1) * 256],
                              vkey=h, h=h, m=m, scale=1.0 / math.sqrt(128.0)))
    attn_phase(P, S, So, units, consts, "diff")
    if upto <= 4:
        return done()

    blocks = []
    for cb in range(8):
        c0 = cb * 256
        blocks.append(dict(
            wsegs=[(w_mla[:, c0:c0 + 256], 0, 16, 0, 256), (w_dif[:, c0:c0 + 256], 16, 16, 0, 256)],
            groups=[dict(accs=[("fm", m * 128, 128, 0, 16), ("fm", m * 128, 128, 16, 32)], row0=c0 + m * 128,
                         epi=epi_merge(mT, gT)) for m in range(2)]))
    gemm_phase(P, [(ymT, 0, 16), (ydT, 16, 16)], 32, So, blocks, T=To, CB=256, setup=setup_merge)

    blocks = []
    for cb in range(4):
        c0 = cb * 512
        blocks.append(dict(
            wsegs=[(w_out[:, c0:c0 + 512], 0, 16, 0, 512)],
            groups=[dict(accs=[("fm", m * 128, 128, 0, 16)], row0=c0 + m * 128, epi=epi_resid(x1T, xown))
                    for m in range(4)]))
    gemm_phase(P, [(mT, 0, 16)], 16, So, blocks, T=To, setup=setup_resid)
    if upto <= 5:
        return done()

    norm_phase(P, x1T, h2T, g_ffn, So, BF16)
    blocks = []
    for cb in range(DFF // 256):
        c0 = cb * 256
        blocks.append(dict(
            wsegs=[(w_fg[:, c0:c0 + 256], 0, 16, 0, 256), (w_fu[:, c0:c0 + 256], 0, 16, 256, 256)],
            groups=[dict(accs=[("fm", m * 128, 128, 0, 16), ("fm", 256 + m * 128, 128, 0, 16)], row0=c0 + m * 128,
                         epi=epi_swiglu(aT)) for m in range(2)]))
    gemm_phase(P, [(h2T, 0, 16)], 16, So, blocks, T=To, setup=setup_swiglu)
    blocks = []
    for cb in range(8):
        c0 = cb * 256
        blocks.append(dict(
            wsegs=[(w_fd[:, c0:c0 + 256], 0, 44, 0, 256)],
            groups=[dict(accs=[("fm", m * 128, 128, 0, 44)], row0=c0 + m * 128, epi=epi_resid(x2T, x1T))
                    for m in range(2)]))
    gemm_phase(P, [(aT, 0, 44)], 44, So, blocks, T=To, CB=256, setup=setup_resid)

    norm_phase(P, x2T, outT, g_fin, So, F32)
    return done()


def _col(v, n):
    return np.ascontiguousarray(np.asarray(v, np.float32).reshape(n, 128).T)


def _rep(v):
    v = np.asarray(v, np.float32).reshape(1, -1)
    return np.ascontiguousarray(np.broadcast_to(v, (128, v.shape[1])))


def position_tables(S):
    inv = (10000.0 ** (-np.arange(0, 64, 2, dtype=np.float32) / np.float32(64))).astype(np.float32)
    ang = np.arange(S, dtype=np.float32)[:, None] * inv[None, :]
    c, s = np.cos(ang).astype(np.float32).T, np.sin(ang).astype(np.float32).T
    cos2 = np.ascontiguousarray(np.concatenate([c, c], 0))
    sin2 = np.ascontiguousarray(np.concatenate([-s, s], 0))
    slopes = np.exp2(-8.0 * np.arange(1, NH_D + 1, dtype=np.float32) / NH_D).astype(np.float64)
    per_core = []
    p = np.arange(128)[:, None, None]
    for c_ in range(4):
        ab = np.zeros((128, NH_D, 80), np.float32)
        for h in range(NH_D):
            clampv = slopes[h] * (127 if h == 0 else 255)
            di = np.arange(80)[None, :]
            val = slopes[h] * ((di - 66 - 2 * c_) * 128 + np.arange(128)[:, None])
            ab[:, h, :] = np.minimum(val, clampv)
        kk = np.arange(8)[None, :, None]
        col = np.arange(256)[None, None, :]
        kr = kk * 128 + p
        qr = c_ * 256 + col
        allowed = (kr // 64) <= (qr // 64)
        mm = allowed.astype(np.float32)
        md = np.zeros((NH_D, 128, 8, 256), np.float32)
        for h in range(NH_D):
            corr = np.where(kr > qr, np.exp(-2.0 * slopes[h] * np.maximum(kr - qr, 0)), 1.0)
            md[h] = (allowed * corr).astype(np.float32)
        per_core.append(dict(abias=np.ascontiguousarray(ab.reshape(128, NH_D * 80)), mask_mla=np.ascontiguousarray(mm),
                             mask_diff=md))
    return cos2, sin2, per_core


_CACHE = {}


def kernel(x, attn_norm_g, w_in, b_gate, q_norm_g, w_uq, kv_norm_g, w_ukv,
           lambda_q1, lambda_k1, lambda_q2, lambda_k2, diff_norm_g,
           w_mla_proj, w_diff_proj, w_out, ffn_norm_g, w_ffn_gate, w_ffn_up,
           w_ffn_down, final_norm_g, _debug=False, _upto=99):
    x = np.asarray(x, np.float32)
    B, S, _ = x.shape
    So = S // 4
    nq = So // 256
    key = (S, _debug, _upto)
    if key not in _CACHE:
        _CACHE[key] = build_program(S, debug=_debug, upto=_upto)
    nc, P = _CACHE[key]
    cos2, sin2, per_core = position_tables(S)
    f = lambda a: np.ascontiguousarray(np.asarray(a, np.float32))
    shared = dict(
        w_in=f(w_in[0]), w_uq=f(w_uq[0]), w_ukv=f(w_ukv[0]), w_mla_proj=f(w_mla_proj[0]),
        w_diff_proj=f(w_diff_proj[0]), w_out=f(w_out[0]), w_ffn_gate=f(w_ffn_gate[0]), w_ffn_up=f(w_ffn_up[0]),
        w_ffn_down=f(w_ffn_down[0]),
        g_attn=_col(attn_norm_g[0], 16), g_q=_col(q_norm_g[0], 6), g_kv=_col(kv_norm_g[0], 4),
        g_ffn=_col(ffn_norm_g[0], 16), g_fin=_col(final_norm_g, 16), b_gate=_col(b_gate[0], 32),
        gsub=_rep(diff_norm_g[0]), lq1=_rep(lambda_q1[0]), lk1=_rep(lambda_k1[0]), lq2=_rep(lambda_q2[0]),
        lk2=_rep(lambda_k2[0]), cos_all=cos2, sin_all=sin2, ident=np.eye(128, dtype=np.float32))
    in_maps = []
    own_idx = []
    for core in range(8):
        b, c = core // 4, core % 4
        idx = np.concatenate([np.arange((4 * i + c) * 256, (4 * i + c + 1) * 256) for i in range(nq)])
        own_idx.append(idx)
        xT = np.ascontiguousarray(x[b].T)
        m = dict(shared)
        m.update(xall=xT, xown=np.ascontiguousarray(xT[:, idx]),
                 cos_own=np.ascontiguousarray(cos2[:, idx]), sin_own=np.ascontiguousarray(sin2[:, idx]),
                 abias=per_core[c]["abias"], mask_mla=per_core[c]["mask_mla"], mask_diff=per_core[c]["mask_diff"])
        in_maps.append(m)
    res = run_bass_kernel_spmd(nc, in_maps, core_ids=list(range(8)))
    out = np.empty((B, S, D), np.float32)
    for core in range(8):
        b = core // 4
        out[b, own_idx[core], :] = np.asarray(res.results[core]["outT"], np.float32).T
    if _debug:
        return out, res.results
    return out
```

```python
import math
from contextlib import ExitStack

import numpy as np
import concourse.bass as bass
import concourse.mybir as mybir
from concourse.bass_utils import run_bass_kernel_spmd

F32 = mybir.dt.float32
BF16 = mybir.dt.bfloat16
AF = mybir.ActivationFunctionType
ALU = mybir.AluOpType

D = 2048
NH_M = 16
NH_D = 8
QR = 768
KVR = 512
DFF = 5632
EPS = 1e-6
OFF_CQ, OFF_CKV, OFF_KPE, OFF_DQ, OFF_DK, OFF_DV, OFF_G = 0, 768, 1280, 1344, 3392, 5440, 7488
D_IN = 11584
LAMBDA_INIT = 0.8 - 0.6 * math.exp(-0.3 * 0)
ENGS = ["pe", "act", "dve", "pool", "sp"]
NDELTA = 66


class Res:
    __slots__ = ("w", "r", "dsem")

    def __init__(self):
        self.w = {}
        self.r = {}
        self.dsem = None


class Tl:
    __slots__ = ("t", "res")

    def __init__(self, t):
        self.t = t
        self.res = Res()


class Prog:
    def __init__(self, nc, n_dma_sems=80):
        self.nc = nc
        self.top = ExitStack()
        self.semobj = {}
        for e in ENGS:
            self.semobj["e:" + e] = self.top.enter_context(nc.semaphore("sem_" + e))
        self.cnt = {e: 0 for e in ENGS}
        self.dkeys = []
        self.dval = {}
        for i in range(n_dma_sems):
            k = "d:%d" % i
            self.semobj[k] = self.top.enter_context(nc.semaphore("semd_%d" % i))
            self.dkeys.append(k)
            self.dval[k] = 0
        self.known = {e: {} for e in ENGS}
        self.n_inst = 0
        self.uid = 0

    def begin(self):
        self.stack = ExitStack()
        self.ops = {e: [] for e in ENGS}
        self.free_d = list(self.dkeys)
        self.used_d = []
        self.rr = 0

    def sb(self, shape, dtype):
        self.uid += 1
        return Tl(self.stack.enter_context(self.nc.sbuf_tensor("sb%d" % self.uid, list(shape), dtype)))

    def ps(self, shape=(128, 512), dtype=F32):
        self.uid += 1
        return Tl(self.stack.enter_context(self.nc.psum_tensor("ps%d" % self.uid, list(shape), dtype)))

    def _need(self, eng, toks, waits):
        kn = self.known[eng]
        for k, v in toks.items():
            if eng == "pe" and k == "e:pe":
                continue
            if kn.get(k, 0) >= v:
                continue
            if waits.get(k, 0) < v:
                waits[k] = v

    def _deps(self, eng, reads, writes):
        waits = {}
        for r in reads:
            self._need(eng, r.res.w, waits)
        for w in writes:
            self._need(eng, w.res.w, waits)
            self._need(eng, w.res.r, waits)
        kn = self.known[eng]
        for k, v in waits.items():
            kn[k] = v
        return list(waits.items())

    def op(self, eng, fn, reads=(), writes=()):
        waits = self._deps(eng, reads, writes)
        self.cnt[eng] += 1
        key = "e:" + eng
        val = self.cnt[eng]
        for r in reads:
            r.res.r[key] = val
        for w in writes:
            w.res.w[key] = val
        self.ops[eng].append((waits, fn, (key, 1)))

    def dma(self, eng, out, in_, reads=(), writes=(), sem=None):
        waits = self._deps(eng, reads, writes)
        r = sem.res
        if r.dsem is None:
            r.dsem = self.free_d.pop() if eng != "pool" else self.free_d.pop(0)
            self.used_d.append(r.dsem)
        key = r.dsem
        self.dval[key] += 16
        val = self.dval[key]
        for x in reads:
            x.res.r[key] = val
        for x in writes:
            x.res.w[key] = val
        self.ops[eng].append((waits, lambda e: e.dma_start(out=out, in_=in_), (key, 16)))

    def end(self):
        for e in ENGS:
            waits = []
            for x in ENGS:
                k = "e:" + x
                if self.cnt[x] > self.known[e].get(k, 0):
                    waits.append((k, self.cnt[x]))
                    self.known[e][k] = self.cnt[x]
            if e == "sp":
                for k in self.used_d:
                    if self.dval[k] > self.known[e].get(k, 0):
                        waits.append((k, self.dval[k]))
            if waits:
                self.ops[e].append((waits, None, None))
        for e in ENGS:
            for k in self.used_d:
                self.known[e][k] = self.dval[k]
        ops = self.ops
        semobj = self.semobj

        def mk(name):
            def body(e):
                for waits, fn, inc in ops[name]:
                    for k, v in waits:
                        e.wait_ge(semobj[k], v)
                    if fn is not None:
                        ins = fn(e)
                        if inc is not None:
                            ins.then_inc(semobj[inc[0]], inc[1])
            return body

        for name in ENGS:
            self.n_inst += len(ops[name])
        with self.nc.Block() as block:
            block.tensor(mk("pe"))
            block.scalar(mk("act"))
            block.vector(mk("dve"))
            block.gpsimd(mk("pool"))
            block.sync(mk("sp"))
        self.stack.close()
        self.ops = None

    def evac_eng(self):
        self.rr += 1
        return "act" if (self.rr & 1) else "dve"


def copy_op(P, eng, out_ap, in_ap, reads, writes):
    if eng == "act":
        P.op("act", lambda e: e.activation(out=out_ap, in_=in_ap, func=AF.Copy), reads, writes)
    else:
        P.op("dve", lambda e: e.tensor_copy(out=out_ap, in_=in_ap), reads, writes)


def norm_phase(P, src, dst, gcol_dram, N, out_dtype, nfeat=D, T=256):
    KC = nfeat // 128
    P.begin()
    xt = [P.sb([128, KC, T], F32) for _ in range(2)]
    sq = [P.sb([128, KC, T], BF16) for _ in range(2)]
    st = [P.sb([128, KC, T], out_dtype) for _ in range(2)]
    ln = [P.sb([128, T], F32) for _ in range(2)]
    rs = [P.sb([128, T], F32) for _ in range(2)]
    ones = P.sb([128, 128], BF16)
    g = P.sb([128, KC], F32)
    epsc = P.sb([128, 1], F32)
    ps = [P.ps() for _ in range(2)]
    P.op("dve", lambda e: e.memset(ones.t[:], 1.0), (), (ones,))
    P.op("dve", lambda e: e.memset(epsc.t[:], EPS), (), (epsc,))
    P.dma("sp", g.t[:], gcol_dram, (), (g,), sem=g)
    srcv = src.rearrange("(kc p) n -> p kc n", p=128)
    dstv = dst.rearrange("(kc p) n -> p kc n", p=128)
    nt = N // T

    def load(i):
        s = xt[i % 2]
        P.dma("sp", s.t[:], srcv[:, :, i * T:(i + 1) * T], (), (s,), sem=s)

    load(0)
    for i in range(nt):
        if i + 1 < nt:
            load(i + 1)
        x, q, o, l, r, p = xt[i % 2], sq[i % 2], st[i % 2], ln[i % 2], rs[i % 2], ps[i % 2]
        P.op("act", lambda e, x=x, q=q: e.activation(out=q.t[:], in_=x.t[:], func=AF.Square), (x,), (q,))
        for kc in range(KC):
            P.op("pe", lambda e, p=p, q=q, kc=kc: e.matmul(p.t[:, 0:T], lhsT=ones.t[:], rhs=q.t[:, kc, :],
                                                          start=(kc == 0), stop=(kc == KC - 1)),
                 (ones, q), (p,))
        P.op("act", lambda e, p=p, l=l: e.activation(out=l.t[:], in_=p.t[:, 0:T], func=AF.Ln,
                                                     scale=1.0 / nfeat, bias=epsc.t[:]), (p, epsc), (l,))
        P.op("act", lambda e, l=l, r=r: e.activation(out=r.t[:], in_=l.t[:], func=AF.Exp, scale=-0.5), (l,), (r,))
        for kc in range(KC):
            P.op("dve", lambda e, o=o, x=x, r=r, kc=kc: e.scalar_tensor_tensor(
                out=o.t[:, kc, :], in0=x.t[:, kc, :], scalar=g.t[:, kc:kc + 1], in1=r.t[:],
                op0=ALU.mult, op1=ALU.mult), (x, r, g), (o,))
        P.dma("sp", dstv[:, :, i * T:(i + 1) * T], o.t[:], (o,), (), sem=o)
    P.end()


def gemm_phase(P, act_srcs, KC, N, blocks, T=512, CB=512, setup=None, npsum=6, nwbuf=2):
    SEG = 512
    P.begin()
    nseg = (CB + SEG - 1) // SEG
    wt = [[P.sb([128, KC, min(SEG, CB - s * SEG)], BF16) for s in range(nseg)] for _ in range(nwbuf)]
    at = [P.sb([128, KC, T], BF16) for _ in range(2)]
    pss = [P.ps() for _ in range(npsum)]
    ctx = setup(P) if setup is not None else None
    nt = N // T
    seq = [(b, t) for b in range(len(blocks)) for t in range(nt)]
    psi = [0]

    def load_w(b):
        w = wt[b % nwbuf]
        for (wap, kc0, nkc, off, ncols) in blocks[b]["wsegs"]:
            if isinstance(wap, int):
                sg, so, ss = off // SEG, off % SEG, wap % SEG
                assert wap // SEG == sg
                ws = w[sg]
                P.op("dve", lambda e, ws=ws, kc0=kc0, nkc=nkc, so=so, ss=ss, ncols=ncols: e.tensor_copy(
                    out=ws.t[:, kc0:kc0 + nkc, so:so + ncols], in_=ws.t[:, kc0:kc0 + nkc, ss:ss + ncols]), (ws,), (ws,))
                continue
            c = 0
            while c < ncols:
                o = off + c
                sg, so = o // SEG, o % SEG
                n = min(ncols - c, SEG - so)
                ws = w[sg]
                P.dma("pool", ws.t[:, kc0:kc0 + nkc, so:so + n],
                      wap[:, c:c + n].rearrange("(kc p) n -> p kc n", p=128), (), (ws,), sem=ws)
                c += n

    def load_a(j):
        a = at[j % 2]
        t = seq[j][1]
        for (aap, kc0, nkc) in act_srcs:
            P.dma("sp", a.t[:, kc0:kc0 + nkc, :],
                  aap.rearrange("(kc p) n -> p kc n", p=128)[:, :, t * T:(t + 1) * T], (), (a,), sem=a)

    load_w(0)
    load_a(0)
    for j, (b, t) in enumerate(seq):
        if t == 0:
            if nwbuf == 2 and b + 1 < len(blocks):
                load_w(b + 1)
            if nwbuf == 1 and b > 0:
                load_w(b)
        if j + 1 < len(seq):
            load_a(j + 1)
        w = wt[b % nwbuf]
        a = at[j % 2]
        for grp in blocks[b]["groups"]:
            accs = grp["accs"]
            if accs[0][0] == "fm":
                pts = []
                for (kind, off, ncols, klo, khi) in accs:
                    p = pss[psi[0] % npsum]
                    psi[0] += 1
                    ws, so = w[off // SEG], off % SEG
                    for kc in range(klo, khi):
                        P.op("pe", lambda e, p=p, ws=ws, a=a, kc=kc, so=so, ncols=ncols, klo=klo, khi=khi:
                             e.matmul(p.t[0:ncols, 0:T], lhsT=ws.t[:, kc, so:so + ncols], rhs=a.t[:, kc, :],
                                      start=(kc == klo), stop=(kc == khi - 1)), (ws, a), (p,))
                    pts.append(p)
                grp["epi"](P, ctx, t * T, T, pts, grp)
            else:
                (kind, off, ncols, klo, khi) = accs[0]
                ws, so = w[off // SEG], off % SEG
                for ts in range(T // 128):
                    p = pss[psi[0] % npsum]
                    psi[0] += 1
                    for kc in range(klo, khi):
                        P.op("pe", lambda e, p=p, ws=ws, a=a, kc=kc, so=so, ncols=ncols, klo=klo, khi=khi, ts=ts:
                             e.matmul(p.t[:, 0:ncols], lhsT=a.t[:, kc, ts * 128:(ts + 1) * 128],
                                      rhs=ws.t[:, kc, so:so + ncols],
                                      start=(kc == klo), stop=(kc == khi - 1)), (ws, a), (p,))
                    grp["epi"](P, ctx, t * T + ts * 128, 128, [p], grp)
    P.end()


class Stager:
    def __init__(self, P, shape, dtype, n=4):
        self.tl = [P.sb(shape, dtype) for _ in range(n)]
        self.i = 0

    def get(self):
        t = self.tl[self.i % len(self.tl)]
        self.i += 1
        return t


def epi_store_fm(dst, row_of):
    def epi(P, ctx, tok0, ntok, pts, grp):
        p = pts[0]
        ncols = grp["accs"][0][2]
        s = ctx["stg"].get()
        copy_op(P, P.evac_eng(), s.t[0:ncols, 0:ntok], p.t[0:ncols, 0:ntok], (p,), (s,))
        r0 = grp["row0"]
        P.dma("sp", dst[r0:r0 + ncols, tok0:tok0 + ntok], s.t[0:ncols, 0:ntok], (s,), (), sem=s)
    return epi


def epi_store_fm32(dst):
    def epi(P, ctx, tok0, ntok, pts, grp):
        p = pts[0]
        ncols = grp["accs"][0][2]
        s = ctx["stg32"].get()
        copy_op(P, P.evac_eng(), s.t[0:ncols, 0:ntok], p.t[0:ncols, 0:ntok], (p,), (s,))
        r0 = grp["row0"]
        P.dma("sp", dst[r0:r0 + ncols, tok0:tok0 + ntok], s.t[0:ncols, 0:ntok], (s,), (), sem=s)
    return epi


def epi_store_tm(dst):
    def epi(P, ctx, tok0, ntok, pts, grp):
        p = pts[0]
        ncols = grp["accs"][0][2]
        s = ctx["stg"].get()
        copy_op(P, P.evac_eng(), s.t[:, 0:ncols], p.t[:, 0:ncols], (p,), (s,))
        c0 = grp["col0"]
        P.dma("sp", dst[tok0:tok0 + 128, c0:c0 + ncols], s.t[:, 0:ncols], (s,), (), sem=s)
    return epi


def setup_stg(dtype=BF16, n=4):
    def setup(P):
        return {"stg": Stager(P, [128, 512], dtype, n)}
    return setup


def epi_gate(dst):
    def epi(P, ctx, tok0, ntok, pts, grp):
        p = pts[0]
        s = ctx["stg"].get()
        bcol = grp["bcol"]
        bg = ctx["bg"]
        P.op("act", lambda e: e.activation(out=s.t[:, 0:ntok], in_=p.t[:, 0:ntok], func=AF.Sigmoid,
                                           bias=bg.t[:, bcol:bcol + 1]), (p, bg), (s,))
        r0 = grp["row0"]
        P.dma("sp", dst[r0:r0 + 128, tok0:tok0 + ntok], s.t[:, 0:ntok], (s,), (), sem=s)
    return epi


def epi_rope(dst, cos_d, sin_d):
    def epi(P, ctx, tok0, ntok, pts, grp):
        pa, pb = pts
        k = ctx["cs_i"]
        ctx["cs_i"] += 1
        ct, sn = ctx["cos"][k % 2], ctx["sin"][k % 2]
        P.dma("sp", ct.t[0:64, 0:ntok], cos_d[:, tok0:tok0 + ntok], (), (ct,), sem=ct)
        P.dma("sp", sn.t[0:64, 0:ntok], sin_d[:, tok0:tok0 + ntok], (), (sn,), sem=sn)
        t1, t2 = ctx["rt1"][k % 2], ctx["rt2"][k % 2]
        P.op("dve", lambda e: e.tensor_tensor(out=t1.t[0:64, 0:ntok], in0=pa.t[0:64, 0:ntok],
                                              in1=ct.t[0:64, 0:ntok], op=ALU.mult), (pa, ct), (t1,))
        P.op("dve", lambda e: e.tensor_tensor(out=t2.t[0:64, 0:ntok], in0=pb.t[0:64, 0:ntok],
                                              in1=sn.t[0:64, 0:ntok], op=ALU.mult), (pb, sn), (t2,))
        s = ctx["stg"].get()
        P.op("dve", lambda e: e.tensor_tensor(out=s.t[0:64, 0:ntok], in0=t1.t[0:64, 0:ntok],
                                               in1=t2.t[0:64, 0:ntok], op=ALU.add), (t1, t2), (s,))
        r0 = grp["row0"]
        P.dma("sp", dst[r0:r0 + 64, tok0:tok0 + ntok], s.t[0:64, 0:ntok], (s,), (), sem=s)
    return epi


def setup_rope(extra=None):
    def setup(P):
        ctx = {"stg": Stager(P, [128, 512], BF16, 4), "stg32": Stager(P, [128, 512], F32, 4), "cs_i": 0,
               "cos": [P.sb([128, 512], F32) for _ in range(2)],
               "sin": [P.sb([128, 512], F32) for _ in range(2)],
               "rt1": [P.sb([128, 512], F32) for _ in range(2)],
               "rt2": [P.sb([128, 512], F32) for _ in range(2)]}
        if extra is not None:
            extra(P, ctx)
        return ctx
    return setup


def epi_resid(dst, resid):
    def epi(P, ctx, tok0, ntok, pts, grp):
        p = pts[0]
        r0 = grp["row0"]
        k = ctx["r_i"]
        ctx["r_i"] += 1
        rt = ctx["rt"][k % 3]
        P.dma("sp", rt.t[:, 0:ntok], resid[r0:r0 + 128, tok0:tok0 + ntok], (), (rt,), sem=rt)
        s = ctx["stg"].get()
        P.op("dve", lambda e: e.tensor_tensor(out=s.t[:, 0:ntok], in0=p.t[:, 0:ntok], in1=rt.t[:, 0:ntok],
                                              op=ALU.add), (p, rt), (s,))
        P.dma("sp", dst[r0:r0 + 128, tok0:tok0 + ntok], s.t[:, 0:ntok], (s,), (), sem=s)
    return epi


def setup_resid(P):
    return {"stg": Stager(P, [128, 512], F32, 3), "r_i": 0, "rt": [P.sb([128, 512], F32) for _ in range(3)]}


def epi_merge(dst, gT):
    def epi(P, ctx, tok0, ntok, pts, grp):
        pa, pb = pts
        r0 = grp["row0"]
        k = ctx["r_i"]
        ctx["r_i"] += 1
        g0, g1 = ctx["g0"][k % 2], ctx["g1"][k % 2]
        t1, t2 = ctx["t1"][k % 2], ctx["t2"][k % 2]
        P.dma("sp", g0.t[:, 0:ntok], gT[r0:r0 + 128, tok0:tok0 + ntok], (), (g0,), sem=g0)
        P.dma("sp", g1.t[:, 0:ntok], gT[D + r0:D + r0 + 128, tok0:tok0 + ntok], (), (g1,), sem=g1)
        P.op("dve", lambda e: e.tensor_tensor(out=t1.t[:, 0:ntok], in0=pa.t[:, 0:ntok], in1=g0.t[:, 0:ntok],
                                              op=ALU.mult), (pa, g0), (t1,))
        P.op("dve", lambda e: e.tensor_tensor(out=t2.t[:, 0:ntok], in0=pb.t[:, 0:ntok], in1=g1.t[:, 0:ntok],
                                              op=ALU.mult), (pb, g1), (t2,))
        s = ctx["stg"].get()
        P.op("dve", lambda e: e.tensor_tensor(out=s.t[:, 0:ntok], in0=t1.t[:, 0:ntok], in1=t2.t[:, 0:ntok],
                                               op=ALU.add), (t1, t2), (s,))
        P.dma("sp", dst[r0:r0 + 128, tok0:tok0 + ntok], s.t[:, 0:ntok], (s,), (), sem=s)
    return epi


def setup_merge(P):
    return {"stg": Stager(P, [128, 512], BF16, 3), "r_i": 0,
            "g0": [P.sb([128, 512], F32) for _ in range(2)], "g1": [P.sb([128, 512], F32) for _ in range(2)],
            "t1": [P.sb([128, 512], F32) for _ in range(2)], "t2": [P.sb([128, 512], F32) for _ in range(2)]}


def epi_swiglu(dst):
    def epi(P, ctx, tok0, ntok, pts, grp):
        pg, pu = pts
        r0 = grp["row0"]
        k = ctx["r_i"]
        ctx["r_i"] += 1
        t1 = ctx["t1"][k % 3]
        P.op("act", lambda e: e.activation(out=t1.t[:, 0:ntok], in_=pg.t[:, 0:ntok], func=AF.Silu), (pg,), (t1,))
        s = ctx["stg"].get()
        P.op("dve", lambda e: e.tensor_tensor(out=s.t[:, 0:ntok], in0=pu.t[:, 0:ntok], in1=t1.t[:, 0:ntok],
                                              op=ALU.mult), (pu, t1), (s,))
        P.dma("sp", dst[r0:r0 + 128, tok0:tok0 + ntok], s.t[:, 0:ntok], (s,), (), sem=s)
    return epi


def setup_swiglu(P):
    return {"stg": Stager(P, [128, 512], BF16, 3), "r_i": 0, "t1": [P.sb([128, 512], F32) for _ in range(3)]}


def latent_blocks(w_in, off, nlat, rawdst, rope=None):
    ncl = nlat * 128
    wsegs = [(w_in[:, off:off + ncl], 0, 16, 0, ncl)]
    groups = [dict(accs=[("fm", m * 128, 128, 0, 16)], row0=m * 128, epi=epi_store_fm32(rawdst)) for m in range(nlat)]
    CB = ncl
    if rope is not None:
        (rdst, cos_d, sin_d, roff) = rope
        wsegs.append((w_in[:, roff:roff + 64], 0, 16, ncl, 64))
        wsegs.append((ncl + 32, 0, 16, ncl + 64, 32))
        wsegs.append((ncl, 0, 16, ncl + 96, 32))
        groups.append(dict(accs=[("fm", ncl, 64, 0, 16), ("fm", ncl + 64, 64, 0, 16)], row0=0,
                           epi=epi_rope(rdst, cos_d, sin_d)))
        CB = ncl + 128
    return [dict(wsegs=wsegs, groups=groups)], setup_rope(), CB


def attn_phase(P, S, So, units, consts, kind):
    nq = So // 256
    nkt_all = S // 128
    dvv = 128 if kind == "mla" else 256
    P.begin()
    kn = [P.sb([128, S], BF16) for _ in range(2)]
    qn = [P.sb([128, So], BF16) for _ in range(2)]
    vt = [P.sb([128, nkt_all, dvv + 1], BF16) for _ in range(2)]
    ident = P.sb([128, 128], BF16)
    P.dma("pool", ident.t[:], consts["ident"], (), (ident,), sem=ident)
    for v in vt:
        P.op("pool", lambda e, v=v: e.memset(v.t[:, :, dvv:dvv + 1], 1.0), (), (v,))
    nS = 5 if kind == "mla" else 3
    LA = nS - 2
    NPT = LA + 3
    pt = [P.sb([128, 512], BF16) for _ in range(NPT)]
    sbank = [P.ps() for _ in range(nS)]
    tpb = P.ps([128, 1024], BF16)
    tpr = [Res(), Res()]
    if kind == "mla":
        kr = P.sb([64, S], BF16)
        qr = [P.sb([64, So], BF16) for _ in range(2)]
        P.dma("sp", kr.t[:], consts["krT"], (), (kr,), sem=kr)
        msk = P.sb([128, 2048], F32)
        P.dma("sp", msk.t[:], consts["mask_mla"].rearrange("p a b -> p (a b)"), (), (msk,), sem=msk)
        obank = [P.ps() for _ in range(2)]
        ybf = [P.sb([128, 128], BF16) for _ in range(4)]
        ystg = [P.sb([128, 256], BF16) for _ in range(2)]
        rsm = [P.sb([128, 1], F32) for _ in range(4)]
    else:
        mskd = [P.sb([128, 2048], F32) for _ in range(2)]
        abias = P.sb([128, NH_D * 80], F32)
        P.dma("sp", abias.t[:], consts["abias"], (), (abias,), sem=abias)
        obank = [P.ps() for _ in range(4)]
        o1n = [P.sb([128, 2, 256], F32) for _ in range(nq)]
        ofull = [P.sb([128, 256], F32) for _ in range(2)]
        junk = P.sb([128, 256], F32)
        ybf = [P.sb([128, 256], BF16) for _ in range(2)]
        ystg = [P.sb([128, 2, 256], BF16) for _ in range(2)]
        rsm = [P.sb([128, 1], F32) for _ in range(4)]
        r2l = [P.sb([128, 1], F32) for _ in range(2)]
        ssq = [P.sb([128, 1], F32) for _ in range(2)]
        lnv = [P.sb([128, 1], F32) for _ in range(2)]
        rstd = [P.sb([128, 1], F32) for _ in range(2)]
        epsc = P.sb([128, 1], F32)
        P.op("dve", lambda e: e.memset(epsc.t[:], EPS), (), (epsc,))
        gsub = P.sb([128, 256], F32)
        P.dma("sp", gsub.t[:], consts["gsub"], (), (gsub,), sem=gsub)
        P.op("act", lambda e: e.activation(out=gsub.t[:], in_=gsub.t[:], func=AF.Copy, scale=1.0 - LAMBDA_INIT),
             (gsub,), (gsub,))
        lt = [P.sb([128, 128], F32) for _ in range(4)]
        for i, nm in enumerate(["lq1", "lk1", "lq2", "lk2"]):
            P.dma("sp", lt[i].t[:], consts[nm], (), (lt[i],), sem=lt[i])
        pr = [P.sb([128, 128], F32) for _ in range(2)]
        sm = [P.sb([128, 1], F32) for _ in range(2)]
        ex = [P.sb([128, 1], F32) for _ in range(2)]
        neglam = P.sb([128, 1], F32)
        for i in range(2):
            P.op("dve", lambda e, i=i: e.tensor_tensor(out=pr[i].t[:], in0=lt[2 * i].t[:], in1=lt[2 * i + 1].t[:],
                                                       op=ALU.mult), (lt[2 * i], lt[2 * i + 1]), (pr[i],))
            P.op("act", lambda e, i=i: e.activation(out=junk.t[:, 0:128], in_=pr[i].t[:], func=AF.Copy,
                                                    accum_out=sm[i].t[:]), (pr[i],), (junk, sm[i]))
            P.op("act", lambda e, i=i: e.activation(out=ex[i].t[:], in_=sm[i].t[:], func=AF.Exp), (sm[i],), (ex[i],))
        P.op("dve", lambda e: e.tensor_tensor(out=neglam.t[:], in0=ex[1].t[:], in1=ex[0].t[:], op=ALU.subtract),
             (ex[0], ex[1]), (neglam,))
        P.op("dve", lambda e: e.tensor_scalar(out=neglam.t[:], in0=neglam.t[:], scalar1=-LAMBDA_INIT, scalar2=None,
                                              op0=ALU.add), (neglam,), (neglam,))

    vloaded = {}

    def load_unit(u):
        un = units[u]
        b = u % 2
        P.dma("sp", kn[b].t[:], un["kT"], (), (kn[b],), sem=kn[b])
        P.dma("sp", qn[b].t[:], un["qT"], (), (qn[b],), sem=qn[b])
        if kind == "mla":
            P.dma("sp", qr[b].t[:], un["qrT"], (), (qr[b],), sem=qr[b])
        vk = un["vkey"]
        if vk not in vloaded:
            vb = len(vloaded) % 2
            vloaded[vk] = vb
            vv = un["v"].rearrange("(t p) c -> p t c", p=128)
            step = 16
            for t0 in range(0, nkt_all, step):
                t1 = min(nkt_all, t0 + step)
                P.dma("sp", vt[vb].t[:, t0:t1, 0:dvv], vv[:, t0:t1, :], (), (vt[vb],), sem=vt[vb])
            if kind == "diff":
                P.dma("sp", mskd[vb].t[:], consts["mask_diff"][un["h"]].rearrange("p a b -> p (a b)"), (),
                      (mskd[vb],), sem=mskd[vb])

    pairs = []
    for u in range(len(units)):
        for i in range(nq):
            nk2 = 4 * (i + 1)
            for kt2 in range(nk2):
                pairs.append((u, i, kt2, nk2))
    cnt = {"s": 0, "p": 0, "e": 0}
    pend = {}

    def do_qk(n):
        (u, i, kt2, nk2) = pairs[n]
        un = units[u]
        b = u % 2
        sp_ = sbank[cnt["s"] % nS]
        cnt["s"] += 1
        q0 = i * 256
        for j in range(2):
            k0 = (2 * kt2 + j) * 128
            if kind == "mla":
                P.op("pe", lambda e, j=j, k0=k0: e.matmul(
                    sp_.t[:, j * 256:(j + 1) * 256], lhsT=kn[b].t[:, k0:k0 + 128], rhs=qn[b].t[:, q0:q0 + 256],
                    start=True, stop=False), (kn[b], qn[b]), (sp_,))
                P.op("pe", lambda e, j=j, k0=k0: e.matmul(
                    sp_.t[:, j * 256:(j + 1) * 256], lhsT=kr.t[0:64, k0:k0 + 128], rhs=qr[b].t[0:64, q0:q0 + 256],
                    start=False, stop=True), (kr, qr[b]), (sp_,))
            else:
                P.op("pe", lambda e, j=j, k0=k0: e.matmul(
                    sp_.t[:, j * 256:(j + 1) * 256], lhsT=kn[b].t[:, k0:k0 + 128], rhs=qn[b].t[:, q0:q0 + 256],
                    start=True, stop=True), (kn[b], qn[b]), (sp_,))
        p = pt[cnt["p"] % NPT]
        cnt["p"] += 1
        sc = un["scale"]
        if kind == "mla":
            P.op("act", lambda e: e.activation(out=p.t[:], in_=sp_.t[:, 0:512], func=AF.Exp, scale=sc), (sp_,), (p,))
        else:
            h = un["h"]
            for j in range(2):
                dst = 2 * kt2 + j - 8 * i
                if h == 0:
                    for jq in range(2):
                        bi = h * 80 + dst - 2 * jq + 66
                        c0 = j * 256 + jq * 128
                        P.op("act", lambda e, bi=bi, c0=c0: e.activation(
                            out=p.t[:, c0:c0 + 128], in_=sp_.t[:, c0:c0 + 128], func=AF.Exp,
                            scale=sc, bias=abias.t[:, bi:bi + 1]), (sp_, abias), (p,))
                else:
                    bi = h * 80 + dst + 66
                    P.op("act", lambda e, bi=bi, j=j: e.activation(
                        out=p.t[:, j * 256:(j + 1) * 256], in_=sp_.t[:, j * 256:(j + 1) * 256], func=AF.Exp,
                        scale=sc, bias=abias.t[:, bi:bi + 1]), (sp_, abias), (p,))
        if kt2 >= nk2 - 4:
            kk = 2 * (kt2 - (nk2 - 4))
            mk_ = msk if kind == "mla" else mskd[vloaded[un["vkey"]]]
            P.op("dve", lambda e: e.tensor_tensor(out=p.t[:], in0=p.t[:], in1=mk_.t[:, kk * 256:(kk + 2) * 256],
                                                  op=ALU.mult), (p, mk_), (p,))
        pend[n] = p

    def do_pv(n):
        (u, i, kt2, nk2) = pairs[n]
        un = units[u]
        p = pend.pop(n)
        if i == 0 and kt2 == 0 and u + 1 < len(units):
            load_unit(u + 1)
        v = vt[vloaded[un["vkey"]]]
        ei = u * nq + i
        last = (kt2 == nk2 - 1)
        if kind == "mla":
            ob = obank[ei % 2]
            for j in range(2):
                for qs in range(2):
                    P.op("pe", lambda e, qs=qs, j=j: e.matmul(
                        ob.t[:, qs * 129:(qs + 1) * 129], lhsT=p.t[:, j * 256 + qs * 128:j * 256 + (qs + 1) * 128],
                        rhs=v.t[:, 2 * kt2 + j, :], start=(kt2 == 0 and j == 0 and qs == 0),
                        stop=(last and j == 1), skip_group_check=True), (p, v), (ob,))
            if last:
                ys = ystg[ei % 2]
                tr = tpr[ei % 2]
                for qs in range(2):
                    r = rsm[cnt["e"] % 4]
                    yb = ybf[cnt["e"] % 4]
                    cnt["e"] += 1
                    P.op("dve", lambda e, r=r, qs=qs: e.reciprocal(out=r.t[:], in_=ob.t[:, qs * 129 + 128:qs * 129 + 129]),
                         (ob,), (r,))
                    P.op("dve", lambda e, r=r, yb=yb, qs=qs: e.tensor_scalar(
                        out=yb.t[:], in0=ob.t[:, qs * 129:qs * 129 + 128], scalar1=r.t[:], scalar2=None,
                        op0=ALU.mult), (ob, r), (yb,))
                    c0 = (ei % 2) * 256 + qs * 128
                    P.op("pe", lambda e, yb=yb, c0=c0: e.transpose(tpb.t[:, c0:c0 + 128], yb.t[:], ident.t[:]),
                         (yb, ident), (_R(tr),))
                c0 = (ei % 2) * 256
                copy_op(P, "dve", ys.t[:], tpb.t[:, c0:c0 + 256], (_R(tr),), (ys,))
                h = un["h"]
                P.dma("sp", consts["yT"][h * 128:(h + 1) * 128, i * 256:(i + 1) * 256], ys.t[:], (ys,), (), sem=ys)
        else:
            m = un["m"]
            obs = [obank[(ei % 2) * 2 + qs] for qs in range(2)]
            for j in range(2):
                for qs in range(2):
                    P.op("pe", lambda e, qs=qs, j=j: e.matmul(
                        obs[qs].t[:, 0:257], lhsT=p.t[:, j * 256 + qs * 128:j * 256 + (qs + 1) * 128],
                        rhs=v.t[:, 2 * kt2 + j, :], start=(kt2 == 0 and j == 0), stop=(last and j == 1)),
                        (p, v), (obs[qs],))
            if last:
                h = un["h"]
                for qs in range(2):
                    ob = obs[qs]
                    r = rsm[cnt["e"] % 4]
                    cnt["e"] += 1
                    P.op("dve", lambda e, r=r, ob=ob: e.reciprocal(out=r.t[:], in_=ob.t[:, 256:257]), (ob,), (r,))
                    if m == 0:
                        P.op("dve", lambda e, r=r, ob=ob, qs=qs: e.tensor_scalar(
                            out=o1n[i].t[:, qs, :], in0=ob.t[:, 0:256], scalar1=r.t[:], scalar2=None, op0=ALU.mult),
                            (ob, r), (o1n[i],))
                    else:
                        k2 = cnt["e"] % 2
                        rl, of, sq_, ln_, rs_, yb = r2l[k2], ofull[k2], ssq[k2], lnv[k2], rstd[k2], ybf[k2]
                        P.op("dve", lambda e, r=r, rl=rl: e.tensor_tensor(out=rl.t[:], in0=r.t[:], in1=neglam.t[:],
                                                                          op=ALU.mult), (r, neglam), (rl,))
                        P.op("dve", lambda e, rl=rl, of=of, ob=ob, qs=qs: e.scalar_tensor_tensor(
                            out=of.t[:], in0=ob.t[:, 0:256], scalar=rl.t[:], in1=o1n[i].t[:, qs, :],
                            op0=ALU.mult, op1=ALU.add), (ob, rl, o1n[i]), (of,))
                        P.op("act", lambda e, of=of, sq_=sq_: e.activation(out=junk.t[:], in_=of.t[:], func=AF.Square,
                                                                           accum_out=sq_.t[:]), (of,), (junk, sq_))
                        P.op("act", lambda e, sq_=sq_, ln_=ln_: e.activation(out=ln_.t[:], in_=sq_.t[:], func=AF.Ln,
                                                                             scale=1.0 / 256, bias=epsc.t[:]),
                             (sq_, epsc), (ln_,))
                        P.op("act", lambda e, ln_=ln_, rs_=rs_: e.activation(out=rs_.t[:], in_=ln_.t[:], func=AF.Exp,
                                                                             scale=-0.5), (ln_,), (rs_,))
                        P.op("dve", lambda e, yb=yb, of=of, rs_=rs_: e.scalar_tensor_tensor(
                            out=yb.t[:], in0=of.t[:], scalar=rs_.t[:], in1=gsub.t[:], op0=ALU.mult, op1=ALU.mult),
                            (of, rs_, gsub), (yb,))
                        tr = tpr[ei % 2]
                        for c in range(2):
                            c0 = (ei % 2) * 512 + c * 256 + qs * 128
                            P.op("pe", lambda e, yb=yb, c=c, c0=c0: e.transpose(
                                tpb.t[:, c0:c0 + 128], yb.t[:, c * 128:(c + 1) * 128], ident.t[:]),
                                (yb, ident), (_R(tr),))
                if m == 1:
                    ys = ystg[ei % 2]
                    tr = tpr[ei % 2]
                    c0 = (ei % 2) * 512
                    copy_op(P, "dve", ys.t[:, :, :], tpb.t[:, c0:c0 + 512].rearrange("p (c q) -> p c q", c=2),
                            (_R(tr),), (ys,))
                    P.dma("sp", consts["yT"][h * 256:(h + 1) * 256, i * 256:(i + 1) * 256].rearrange(
                        "(c p) q -> p c q", p=128), ys.t[:, :, :], (ys,), (), sem=ys)

    load_unit(0)
    for n in range(len(pairs) + LA):
        if n < len(pairs):
            do_qk(n)
        if n >= LA:
            do_pv(n - LA)
    P.end()


class _R:
    __slots__ = ("res",)

    def __init__(self, res):
        self.res = res


def build_program(S, debug=False, upto=99):
    So = S // 4
    To = min(512, So)
    nc = bass.Bass("TRN2", target_bir_lowering=False)
    P = Prog(nc)

    def din(name, shape, dt=F32):
        return nc.dram_tensor(name, list(shape), dt, kind="ExternalInput").ap()

    def scr(name, shape, dt):
        kind = "ExternalOutput" if debug else "Internal"
        return nc.dram_tensor(name, list(shape), dt, kind=kind).ap()

    xall = din("xall", [D, S])
    xown = din("xown", [D, So])
    w_in = din("w_in", [D, D_IN])
    w_uq = din("w_uq", [QR, NH_M * 192])
    w_ukv = din("w_ukv", [KVR, NH_M * 256])
    w_mla = din("w_mla_proj", [D, D])
    w_dif = din("w_diff_proj", [D, D])
    w_out = din("w_out", [D, D])
    w_fg = din("w_ffn_gate", [D, DFF])
    w_fu = din("w_ffn_up", [D, DFF])
    w_fd = din("w_ffn_down", [DFF, D])
    g_attn = din("g_attn", [128, 16])
    g_q = din("g_q", [128, 6])
    g_kv = din("g_kv", [128, 4])
    g_ffn = din("g_ffn", [128, 16])
    g_fin = din("g_fin", [128, 16])
    b_gate = din("b_gate", [128, 32])
    gsub = din("gsub", [128, 256])
    lq1 = din("lq1", [128, 128])
    lk1 = din("lk1", [128, 128])
    lq2 = din("lq2", [128, 128])
    lk2 = din("lk2", [128, 128])
    cos_all = din("cos_all", [64, S])
    sin_all = din("sin_all", [64, S])
    cos_own = din("cos_own", [64, So])
    sin_own = din("sin_own", [64, So])
    abias = din("abias", [128, NH_D * 80])
    mask_mla = din("mask_mla", [128, 8, 256])
    mask_diff = din("mask_diff", [NH_D, 128, 8, 256])
    ident = din("ident", [128, 128])

    hT_all = scr("hT_all", [D, S], BF16)
    hT_own = scr("hT_own", [D, So], BF16)
    dkT = scr("dkT", [D, S], BF16)
    dv = scr("dv", [S, D], BF16)
    ckvnT = scr("ckvnT", [KVR, S], BF16)
    ckvraw = scr("ckvraw", [KVR, S], F32)
    cqraw = scr("cqraw", [QR, So], F32)
    krT = scr("krT", [64, S], BF16)
    knT = scr("knT", [D, S], BF16)
    vm = scr("vm", [S, D], BF16)
    dqT = scr("dqT", [D, So], BF16)
    gT = scr("gT", [2 * D, So], F32)
    cqnT = scr("cqnT", [QR, So], BF16)
    qnT = scr("qnT", [D, So], BF16)
    qrT = scr("qrT", [NH_M * 64, So], BF16)
    ymT = scr("ymT", [D, So], BF16)
    ydT = scr("ydT", [D, So], BF16)
    mT = scr("mT", [D, So], BF16)
    x1T = scr("x1T", [D, So], F32)
    h2T = scr("h2T", [D, So], BF16)
    aT = scr("aT", [DFF, So], BF16)
    x2T = scr("x2T", [D, So], F32)
    outT = nc.dram_tensor("outT", [D, So], F32, kind="ExternalOutput").ap()

    def done():
        P.top.close()
        return nc, P

    norm_phase(P, xall, hT_all, g_attn, S, BF16)
    norm_phase(P, xown, hT_own, g_attn, So, BF16)
    if upto <= 0:
        return done()

    blocks = [dict(wsegs=[(w_in[:, OFF_DK:OFF_DK + 2048], 0, 16, 0, 2048)],
                   groups=[dict(accs=[("fm", m * 128, 128, 0, 16)], row0=m * 128, epi=epi_store_fm(dkT, None))
                           for m in range(16)]),
              dict(wsegs=[(w_in[:, OFF_DV:OFF_DV + 2048], 0, 16, 0, 2048)],
                   groups=[dict(accs=[("tm", cb * 512, 512, 0, 16)], col0=cb * 512, epi=epi_store_tm(dv))
                           for cb in range(4)])]
    gemm_phase(P, [(hT_all, 0, 16)], 16, S, blocks, CB=2048, setup=setup_stg())
    if upto <= 0.3:
        return done()
    blocks, setup, CB = latent_blocks(w_in, OFF_CKV, 4, ckvraw, rope=(krT, cos_all, sin_all, OFF_KPE))
    gemm_phase(P, [(hT_all, 0, 16)], 16, S, blocks, CB=CB, setup=setup)
    norm_phase(P, ckvraw, ckvnT, g_kv, S, BF16, nfeat=KVR)

    if upto <= 0.6:
        return done()
    groups = []
    for h in range(NH_M):
        groups.append(dict(accs=[("fm", h * 256, 128, 0, 4)], row0=h * 128, epi=epi_store_fm(knT, None)))
        groups.append(dict(accs=[("tm", h * 256 + 128, 128, 0, 4)], col0=h * 128, epi=epi_store_tm(vm)))
    blocks = [dict(wsegs=[(w_ukv[:, :], 0, 4, 0, 4096)], groups=groups)]
    gemm_phase(P, [(ckvnT, 0, 4)], 4, S, blocks, CB=4096, nwbuf=1, setup=setup_stg())
    if upto <= 1:
        return done()

    blocks = [dict(wsegs=[(w_in[:, OFF_DQ:OFF_DQ + 2048], 0, 16, 0, 2048)],
                   groups=[dict(accs=[("fm", m * 128, 128, 0, 16)], row0=m * 128, epi=epi_store_fm(dqT, None))
                           for m in range(16)])]
    gemm_phase(P, [(hT_own, 0, 16)], 16, So, blocks, T=To, CB=2048, nwbuf=1, setup=setup_stg())

    blocks = []
    for cb in range(2):
        c0 = OFF_G + cb * 2048
        blocks.append(dict(
            wsegs=[(w_in[:, c0:c0 + 2048], 0, 16, 0, 2048)],
            groups=[dict(accs=[("fm", m * 128, 128, 0, 16)], row0=cb * 2048 + m * 128, bcol=cb * 16 + m,
                         epi=epi_gate(gT)) for m in range(16)]))

    def setup_gate(P):
        ctx = {"stg": Stager(P, [128, 512], F32, 4), "bg": P.sb([128, 32], F32)}
        P.dma("sp", ctx["bg"].t[:], b_gate, (), (ctx["bg"],), sem=ctx["bg"])
        return ctx
    gemm_phase(P, [(hT_own, 0, 16)], 16, So, blocks, T=To, CB=2048, setup=setup_gate)

    blocks, setup, CB = latent_blocks(w_in, OFF_CQ, 6, cqraw)
    gemm_phase(P, [(hT_own, 0, 16)], 16, So, blocks, T=To, CB=CB, setup=setup)
    norm_phase(P, cqraw, cqnT, g_q, So, BF16, nfeat=QR)

    wsegs, groups = [], []
    for h in range(NH_M):
        o = h * 256
        wsegs.append((w_uq[:, h * 192:h * 192 + 192], 0, 6, o, 192))
        wsegs.append((o + 160, 0, 6, o + 192, 32))
        wsegs.append((o + 128, 0, 6, o + 224, 32))
        groups.append(dict(accs=[("fm", o, 128, 0, 6)], row0=h * 128, epi=epi_store_fm(qnT, None)))
        groups.append(dict(accs=[("fm", o + 128, 64, 0, 6), ("fm", o + 192, 64, 0, 6)], row0=h * 64,
                           epi=epi_rope(qrT, cos_own, sin_own)))
    blocks = [dict(wsegs=wsegs, groups=groups)]
    gemm_phase(P, [(cqnT, 0, 6)], 6, So, blocks, T=To, CB=4096, nwbuf=1, setup=setup_rope())
    if upto <= 2:
        return done()

    consts = dict(ident=ident, krT=krT, mask_mla=mask_mla, yT=ymT)
    units = [dict(kT=knT[h * 128:(h + 1) * 128, :], qT=qnT[h * 128:(h + 1) * 128, :],
                  qrT=qrT[h * 64:(h + 1) * 64, :], v=vm[:, h * 128:(h + 1) * 128], vkey=h, h=h, m=0,
                  scale=1.0 / math.sqrt(192.0)) for h in range(NH_M)]
    attn_phase(P, S, So, units, consts, "mla")
    if upto <= 3:
        return done()

    consts = dict(ident=ident, abias=abias, mask_diff=mask_diff, yT=ydT, gsub=gsub, lq1=lq1, lk1=lk1, lq2=lq2, lk2=lk2)
    units = []
    for h in range(NH_D):
        for m in range(2):
            r0 = (h * 2 + m) * 128
            units.append(dict(kT=dkT[r0:r0 + 128, :], qT=dqT[r0:r0 + 128, :], v=dv[:, h * 256:(h + 1) * 256],
                              vkey=h, h=h, m=m, scale=1.0 / math.sqrt(128.0)))
    attn_phase(P, S, So, units, consts, "diff")
    if upto <= 4:
        return done()

    blocks = []
    for cb in range(2):
        c0 = cb * 1024
        blocks.append(dict(
            wsegs=[(w_mla[:, c0:c0 + 1024], 0, 16, 0, 1024), (w_dif[:, c0:c0 + 1024], 16, 16, 0, 1024)],
            groups=[dict(accs=[("fm", m * 128, 128, 0, 16), ("fm", m * 128, 128, 16, 32)], row0=c0 + m * 128,
                         epi=epi_merge(mT, gT)) for m in range(8)]))
    gemm_phase(P, [(ymT, 0, 16), (ydT, 16, 16)], 32, So, blocks, T=To, CB=1024, nwbuf=1, setup=setup_merge)

    blocks = [dict(wsegs=[(w_out[:, :], 0, 16, 0, 2048)],
                   groups=[dict(accs=[("fm", m * 128, 128, 0, 16)], row0=m * 128, epi=epi_resid(x1T, xown))
                           for m in range(16)])]
    gemm_phase(P, [(mT, 0, 16)], 16, So, blocks, T=To, CB=2048, nwbuf=1, setup=setup_resid)
    if upto <= 5:
        return done()

    norm_phase(P, x1T, h2T, g_ffn, So, BF16)
    blocks = []
    for c0 in range(0, DFF, 1024):
        n = min(1024, DFF - c0)
        blocks.append(dict(
            wsegs=[(w_fg[:, c0:c0 + n], 0, 16, 0, n), (w_fu[:, c0:c0 + n], 0, 16, 1024, n)],
            groups=[dict(accs=[("fm", m * 128, 128, 0, 16), ("fm", 1024 + m * 128, 128, 0, 16)], row0=c0 + m * 128,
                         epi=epi_swiglu(aT)) for m in range(n // 128)]))
    gemm_phase(P, [(h2T, 0, 16)], 16, So, blocks, T=To, CB=2048, setup=setup_swiglu)
    blocks = []
    for cb in range(2):
        c0 = cb * 1024
        blocks.append(dict(
            wsegs=[(w_fd[:, c0:c0 + 1024], 0, 44, 0, 1024)],
            groups=[dict(accs=[("fm", m * 128, 128, 0, 44)], row0=c0 + m * 128, epi=epi_resid(x2T, x1T))
                    for m in range(8)]))
    gemm_phase(P, [(aT, 0, 44)], 44, So, blocks, T=To, CB=1024, nwbuf=1, setup=setup_resid)

    norm_phase(P, x2T, outT, g_fin, So, F32)
    return done()


def _col(v, n):
    return np.ascontiguousarray(np.asarray(v, np.float32).reshape(n, 128).T)


def _rep(v):
    v = np.asarray(v, np.float32).reshape(1, -1)
    return np.ascontiguousarray(np.broadcast_to(v, (128, v.shape[1])))


def position_tables(S):
    inv = (10000.0 ** (-np.arange(0, 64, 2, dtype=np.float32) / np.float32(64))).astype(np.float32)
    ang = np.arange(S, dtype=np.float32)[:, None] * inv[None, :]
    c, s = np.cos(ang).astype(np.float32).T, np.sin(ang).astype(np.float32).T
    cos2 = np.ascontiguousarray(np.concatenate([c, c], 0))
    sin2 = np.ascontiguousarray(np.concatenate([-s, s], 0))
    slopes = np.exp2(-8.0 * np.arange(1, NH_D + 1, dtype=np.float32) / NH_D).astype(np.float64)
    per_core = []
    p = np.arange(128)[:, None, None]
    for c_ in range(4):
        ab = np.zeros((128, NH_D, 80), np.float32)
        for h in range(NH_D):
            clampv = slopes[h] * (127 if h == 0 else 255)
            di = np.arange(80)[None, :]
            val = slopes[h] * ((di - 66 - 2 * c_) * 128 + np.arange(128)[:, None])
            ab[:, h, :] = np.minimum(val, clampv)
        kk = np.arange(8)[None, :, None]
        col = np.arange(256)[None, None, :]
        kr = kk * 128 + p
        qr = c_ * 256 + col
        allowed = (kr // 64) <= (qr // 64)
        mm = allowed.astype(np.float32)
        md = np.zeros((NH_D, 128, 8, 256), np.float32)
        for h in range(NH_D):
            corr = np.where(kr > qr, np.exp(-2.0 * slopes[h] * np.maximum(kr - qr, 0)), 1.0)
            md[h] = (allowed * corr).astype(np.float32)
        per_core.append(dict(abias=np.ascontiguousarray(ab.reshape(128, NH_D * 80)), mask_mla=np.ascontiguousarray(mm),
                             mask_diff=md))
    return cos2, sin2, per_core


_CACHE = {}


def kernel(x, attn_norm_g, w_in, b_gate, q_norm_g, w_uq, kv_norm_g, w_ukv,
           lambda_q1, lambda_k1, lambda_q2, lambda_k2, diff_norm_g,
           w_mla_proj, w_diff_proj, w_out, ffn_norm_g, w_ffn_gate, w_ffn_up,
           w_ffn_down, final_norm_g, _debug=False, _upto=99):
    x = np.asarray(x, np.float32)
    B, S, _ = x.shape
    So = S // 4
    nq = So // 256
    key = (S, _debug, _upto)
    if key not in _CACHE:
        _CACHE[key] = build_program(S, debug=_debug, upto=_upto)
    nc, P = _CACHE[key]
    cos2, sin2, per_core = position_tables(S)
    f = lambda a: np.ascontiguousarray(np.asarray(a, np.float32))
    shared = dict(
        w_in=f(w_in[0]), w_uq=f(w_uq[0]), w_ukv=f(w_ukv[0]), w_mla_proj=f(w_mla_proj[0]),
        w_diff_proj=f(w_diff_proj[0]), w_out=f(w_out[0]), w_ffn_gate=f(w_ffn_gate[0]), w_ffn_up=f(w_ffn_up[0]),
        w_ffn_down=f(w_ffn_down[0]),
        g_attn=_col(attn_norm_g[0], 16), g_q=_col(q_norm_g[0], 6), g_kv=_col(kv_norm_g[0], 4),
        g_ffn=_col(ffn_norm_g[0], 16), g_fin=_col(final_norm_g, 16), b_gate=_col(b_gate[0], 32),
        gsub=_rep(diff_norm_g[0]), lq1=_rep(lambda_q1[0]), lk1=_rep(lambda_k1[0]), lq2=_rep(lambda_q2[0]),
        lk2=_rep(lambda_k2[0]), cos_all=cos2, sin_all=sin2, ident=np.eye(128, dtype=np.float32))
    in_maps = []
    own_idx = []
    for core in range(8):
        b, c = core // 4, core % 4
        idx = np.concatenate([np.arange((4 * i + c) * 256, (4 * i + c + 1) * 256) for i in range(nq)])
        own_idx.append(idx)
        xT = np.ascontiguousarray(x[b].T)
        m = dict(shared)
        m.update(xall=xT, xown=np.ascontiguousarray(xT[:, idx]),
                 cos_own=np.ascontiguousarray(cos2[:, idx]), sin_own=np.ascontiguousarray(sin2[:, idx]),
                 abias=per_core[c]["abias"], mask_mla=per_core[c]["mask_mla"], mask_diff=per_core[c]["mask_diff"])
        in_maps.append(m)
    res = run_bass_kernel_spmd(nc, in_maps, core_ids=list(range(8)))
    out = np.empty((B, S, D), np.float32)
    for core in range(8):
        b = core // 4
        out[b, own_idx[core], :] = np.asarray(res.results[core]["outT"], np.float32).T
    if _debug:
        return out, res.results
    return out
```

```python
import math
from contextlib import ExitStack

import numpy as np
import concourse.bass as bass
import concourse.mybir as mybir
from concourse.bass_utils import run_bass_kernel_spmd

F32 = mybir.dt.float32
BF16 = mybir.dt.bfloat16
AF = mybir.ActivationFunctionType
ALU = mybir.AluOpType

D = 2048
NH_M = 16
NH_D = 8
QR = 768
KVR = 512
DFF = 5632
EPS = 1e-6
OFF_CQ, OFF_CKV, OFF_KPE, OFF_DQ, OFF_DK, OFF_DV, OFF_G = 0, 768, 1280, 1344, 3392, 5440, 7488
D_IN = 11584
LAMBDA_INIT = 0.8 - 0.6 * math.exp(-0.3 * 0)
ENGS = ["pe", "act", "dve", "pool", "sp"]
NDELTA = 66


class Res:
    __slots__ = ("w", "r", "dsem")

    def __init__(self):
        self.w = {}
        self.r = {}
        self.dsem = None


class Tl:
    __slots__ = ("t", "res")

    def __init__(self, t):
        self.t = t
        self.res = Res()


class Prog:
    def __init__(self, nc, n_dma_sems=80):
        self.nc = nc
        self.top = ExitStack()
        self.semobj = {}
        for e in ENGS:
            self.semobj["e:" + e] = self.top.enter_context(nc.semaphore("sem_" + e))
        self.cnt = {e: 0 for e in ENGS}
        self.dkeys = []
        self.dval = {}
        for i in range(n_dma_sems):
            k = "d:%d" % i
            self.semobj[k] = self.top.enter_context(nc.semaphore("semd_%d" % i))
            self.dkeys.append(k)
            self.dval[k] = 0
        self.known = {e: {} for e in ENGS}
        self.n_inst = 0
        self.uid = 0

    def begin(self):
        self.stack = ExitStack()
        self.ops = {e: [] for e in ENGS}
        self.free_d = list(self.dkeys)
        self.used_d = []
        self.rr = 0

    def sb(self, shape, dtype):
        self.uid += 1
        return Tl(self.stack.enter_context(self.nc.sbuf_tensor("sb%d" % self.uid, list(shape), dtype)))

    def ps(self, shape=(128, 512), dtype=F32):
        self.uid += 1
        return Tl(self.stack.enter_context(self.nc.psum_tensor("ps%d" % self.uid, list(shape), dtype)))

    def _need(self, eng, toks, waits):
        kn = self.known[eng]
        for k, v in toks.items():
            if eng == "pe" and k == "e:pe":
                continue
            if kn.get(k, 0) >= v:
                continue
            if waits.get(k, 0) < v:
                waits[k] = v

    def _deps(self, eng, reads, writes):
        waits = {}
        for r in reads:
            self._need(eng, r.res.w, waits)
        for w in writes:
            self._need(eng, w.res.w, waits)
            self._need(eng, w.res.r, waits)
        kn = self.known[eng]
        for k, v in waits.items():
            kn[k] = v
        return list(waits.items())

    def op(self, eng, fn, reads=(), writes=()):
        waits = self._deps(eng, reads, writes)
        self.cnt[eng] += 1
        key = "e:" + eng
        val = self.cnt[eng]
        for r in reads:
            r.res.r[key] = val
        for w in writes:
            w.res.w[key] = val
        self.ops[eng].append((waits, fn, (key, 1)))

    def dma(self, eng, out, in_, reads=(), writes=(), sem=None):
        waits = self._deps(eng, reads, writes)
        r = sem.res
        if r.dsem is None:
            r.dsem = self.free_d.pop() if eng != "pool" else self.free_d.pop(0)
            self.used_d.append(r.dsem)
        key = r.dsem
        self.dval[key] += 16
        val = self.dval[key]
        for x in reads:
            x.res.r[key] = val
        for x in writes:
            x.res.w[key] = val
        self.ops[eng].append((waits, lambda e: e.dma_start(out=out, in_=in_), (key, 16)))

    def end(self):
        for e in ENGS:
            waits = []
            for x in ENGS:
                k = "e:" + x
                if self.cnt[x] > self.known[e].get(k, 0):
                    waits.append((k, self.cnt[x]))
                    self.known[e][k] = self.cnt[x]
            if e == "sp":
                for k in self.used_d:
                    if self.dval[k] > self.known[e].get(k, 0):
                        waits.append((k, self.dval[k]))
            if waits:
                self.ops[e].append((waits, None, None))
        for e in ENGS:
            for k in self.used_d:
                self.known[e][k] = self.dval[k]
        ops = self.ops
        semobj = self.semobj

        def mk(name):
            def body(e):
                for waits, fn, inc in ops[name]:
                    for k, v in waits:
                        e.wait_ge(semobj[k], v)
                    if fn is not None:
                        ins = fn(e)
                        if inc is not None:
                            ins.then_inc(semobj[inc[0]], inc[1])
            return body

        for name in ENGS:
            self.n_inst += len(ops[name])
        with self.nc.Block() as block:
            block.tensor(mk("pe"))
            block.scalar(mk("act"))
            block.vector(mk("dve"))
            block.gpsimd(mk("pool"))
            block.sync(mk("sp"))
        self.stack.close()
        self.ops = None

    def evac_eng(self):
        self.rr += 1
        return "act" if (self.rr & 1) else "dve"


def copy_op(P, eng, out_ap, in_ap, reads, writes):
    if eng == "act":
        P.op("act", lambda e: e.activation(out=out_ap, in_=in_ap, func=AF.Copy), reads, writes)
    else:
        P.op("dve", lambda e: e.tensor_copy(out=out_ap, in_=in_ap), reads, writes)


def norm_phase(P, src, dst, gcol_dram, N, out_dtype, nfeat=D, T=256):
    KC = nfeat // 128
    P.begin()
    xt = [P.sb([128, KC, T], F32) for _ in range(2)]
    sq = [P.sb([128, KC, T], BF16) for _ in range(2)]
    st = [P.sb([128, KC, T], out_dtype) for _ in range(2)]
    ln = [P.sb([128, T], F32) for _ in range(2)]
    rs = [P.sb([128, T], F32) for _ in range(2)]
    ones = P.sb([128, 128], BF16)
    g = P.sb([128, KC], F32)
    epsc = P.sb([128, 1], F32)
    ps = [P.ps() for _ in range(2)]
    P.op("dve", lambda e: e.memset(ones.t[:], 1.0), (), (ones,))
    P.op("dve", lambda e: e.memset(epsc.t[:], EPS), (), (epsc,))
    P.dma("sp", g.t[:], gcol_dram, (), (g,), sem=g)
    srcv = src.rearrange("(kc p) n -> p kc n", p=128)
    dstv = dst.rearrange("(kc p) n -> p kc n", p=128)
    nt = N // T

    def load(i):
        s = xt[i % 2]
        P.dma("sp", s.t[:], srcv[:, :, i * T:(i + 1) * T], (), (s,), sem=s)

    load(0)
    for i in range(nt):
        if i + 1 < nt:
            load(i + 1)
        x, q, o, l, r, p = xt[i % 2], sq[i % 2], st[i % 2], ln[i % 2], rs[i % 2], ps[i % 2]
        P.op("act", lambda e, x=x, q=q: e.activation(out=q.t[:], in_=x.t[:], func=AF.Square), (x,), (q,))
        for kc in range(KC):
            P.op("pe", lambda e, p=p, q=q, kc=kc: e.matmul(p.t[:, 0:T], lhsT=ones.t[:], rhs=q.t[:, kc, :],
                                                          start=(kc == 0), stop=(kc == KC - 1)),
                 (ones, q), (p,))
        P.op("act", lambda e, p=p, l=l: e.activation(out=l.t[:], in_=p.t[:, 0:T], func=AF.Ln,
                                                     scale=1.0 / nfeat, bias=epsc.t[:]), (p, epsc), (l,))
        P.op("act", lambda e, l=l, r=r: e.activation(out=r.t[:], in_=l.t[:], func=AF.Exp, scale=-0.5), (l,), (r,))
        for kc in range(KC):
            P.op("dve", lambda e, o=o, x=x, r=r, kc=kc: e.scalar_tensor_tensor(
                out=o.t[:, kc, :], in0=x.t[:, kc, :], scalar=g.t[:, kc:kc + 1], in1=r.t[:],
                op0=ALU.mult, op1=ALU.mult), (x, r, g), (o,))
        P.dma("sp", dstv[:, :, i * T:(i + 1) * T], o.t[:], (o,), (), sem=o)
    P.end()


def gemm_phase(P, act_srcs, KC, N, blocks, T=512, CB=512, setup=None, npsum=6, nwbuf=2):
    SEG = 512
    P.begin()
    nseg = (CB + SEG - 1) // SEG
    wt = [[P.sb([128, KC, min(SEG, CB - s * SEG)], BF16) for s in range(nseg)] for _ in range(nwbuf)]
    at = [P.sb([128, KC, T], BF16) for _ in range(2)]
    pss = [P.ps() for _ in range(npsum)]
    ctx = setup(P) if setup is not None else None
    nt = N // T
    seq = [(b, t) for b in range(len(blocks)) for t in range(nt)]
    psi = [0]

    def load_w(b):
        w = wt[b % nwbuf]
        for (wap, kc0, nkc, off, ncols) in blocks[b]["wsegs"]:
            if isinstance(wap, int):
                sg, so, ss = off // SEG, off % SEG, wap % SEG
                assert wap // SEG == sg
                ws = w[sg]
                P.op("dve", lambda e, ws=ws, kc0=kc0, nkc=nkc, so=so, ss=ss, ncols=ncols: e.tensor_copy(
                    out=ws.t[:, kc0:kc0 + nkc, so:so + ncols], in_=ws.t[:, kc0:kc0 + nkc, ss:ss + ncols]), (ws,), (ws,))
                continue
            c = 0
            while c < ncols:
                o = off + c
                sg, so = o // SEG, o % SEG
                n = min(ncols - c, SEG - so)
                ws = w[sg]
                P.dma("pool", ws.t[:, kc0:kc0 + nkc, so:so + n],
                      wap[:, c:c + n].rearrange("(kc p) n -> p kc n", p=128), (), (ws,), sem=ws)
                c += n

    def load_a(j):
        a = at[j % 2]
        t = seq[j][1]
        for (aap, kc0, nkc) in act_srcs:
            P.dma("sp", a.t[:, kc0:kc0 + nkc, :],
                  aap.rearrange("(kc p) n -> p kc n", p=128)[:, :, t * T:(t + 1) * T], (), (a,), sem=a)

    load_w(0)
    load_a(0)
    for j, (b, t) in enumerate(seq):
        if t == 0:
            if nwbuf == 2 and b + 1 < len(blocks):
                load_w(b + 1)
            if nwbuf == 1 and b > 0:
                load_w(b)
        if j + 1 < len(seq):
            load_a(j + 1)
        w = wt[b % nwbuf]
        a = at[j % 2]
        for grp in blocks[b]["groups"]:
            accs = grp["accs"]
            if accs[0][0] == "fm":
                pts = []
                for (kind, off, ncols, klo, khi) in accs:
                    p = pss[psi[0] % npsum]
                    psi[0] += 1
                    ws, so = w[off // SEG], off % SEG
                    for kc in range(klo, khi):
                        P.op("pe", lambda e, p=p, ws=ws, a=a, kc=kc, so=so, ncols=ncols, klo=klo, khi=khi:
                             e.matmul(p.t[0:ncols, 0:T], lhsT=ws.t[:, kc, so:so + ncols], rhs=a.t[:, kc, :],
                                      start=(kc == klo), stop=(kc == khi - 1)), (ws, a), (p,))
                    pts.append(p)
                grp["epi"](P, ctx, t * T, T, pts, grp)
            else:
                (kind, off, ncols, klo, khi) = accs[0]
                ws, so = w[off // SEG], off % SEG
                for ts in range(T // 128):
                    p = pss[psi[0] % npsum]
                    psi[0] += 1
                    for kc in range(klo, khi):
                        P.op("pe", lambda e, p=p, ws=ws, a=a, kc=kc, so=so, ncols=ncols, klo=klo, khi=khi, ts=ts:
                             e.matmul(p.t[:, 0:ncols], lhsT=a.t[:, kc, ts * 128:(ts + 1) * 128],
                                      rhs=ws.t[:, kc, so:so + ncols],
                                      start=(kc == klo), stop=(kc == khi - 1)), (ws, a), (p,))
                    grp["epi"](P, ctx, t * T + ts * 128, 128, [p], grp)
    P.end()


class Stager:
    def __init__(self, P, shape, dtype, n=4):
        self.tl = [P.sb(shape, dtype) for _ in range(n)]
        self.i = 0

    def get(self):
        t = self.tl[self.i % len(self.tl)]
        self.i += 1
        return t


def epi_store_fm(dst, row_of):
    def epi(P, ctx, tok0, ntok, pts, grp):
        p = pts[0]
        ncols = grp["accs"][0][2]
        s = ctx["stg"].get()
        copy_op(P, P.evac_eng(), s.t[0:ncols, 0:ntok], p.t[0:ncols, 0:ntok], (p,), (s,))
        r0 = grp["row0"]
        P.dma("sp", dst[r0:r0 + ncols, tok0:tok0 + ntok], s.t[0:ncols, 0:ntok], (s,), (), sem=s)
    return epi


def epi_store_fm32(dst):
    def epi(P, ctx, tok0, ntok, pts, grp):
        p = pts[0]
        ncols = grp["accs"][0][2]
        s = ctx["stg32"].get()
        copy_op(P, P.evac_eng(), s.t[0:ncols, 0:ntok], p.t[0:ncols, 0:ntok], (p,), (s,))
        r0 = grp["row0"]
        P.dma("sp", dst[r0:r0 + ncols, tok0:tok0 + ntok], s.t[0:ncols, 0:ntok], (s,), (), sem=s)
    return epi


def epi_store_tm(dst):
    def epi(P, ctx, tok0, ntok, pts, grp):
        p = pts[0]
        ncols = grp["accs"][0][2]
        s = ctx["stg"].get()
        copy_op(P, P.evac_eng(), s.t[:, 0:ncols], p.t[:, 0:ncols], (p,), (s,))
        c0 = grp["col0"]
        P.dma("sp", dst[tok0:tok0 + 128, c0:c0 + ncols], s.t[:, 0:ncols], (s,), (), sem=s)
    return epi


def setup_stg(dtype=BF16, n=4):
    def setup(P):
        return {"stg": Stager(P, [128, 512], dtype, n)}
    return setup


def epi_gate(dst):
    def epi(P, ctx, tok0, ntok, pts, grp):
        p = pts[0]
        s = ctx["stg"].get()
        bcol = grp["bcol"]
        bg = ctx["bg"]
        P.op("act", lambda e: e.activation(out=s.t[:, 0:ntok], in_=p.t[:, 0:ntok], func=AF.Sigmoid,
                                           bias=bg.t[:, bcol:bcol + 1]), (p, bg), (s,))
        r0 = grp["row0"]
        P.dma("sp", dst[r0:r0 + 128, tok0:tok0 + ntok], s.t[:, 0:ntok], (s,), (), sem=s)
    return epi


def epi_rope(dst, cos_d, sin_d):
    def epi(P, ctx, tok0, ntok, pts, grp):
        pa, pb = pts
        k = ctx["cs_i"]
        ctx["cs_i"] += 1
        ct, sn = ctx["cos"][k % 2], ctx["sin"][k % 2]
        P.dma("sp", ct.t[0:64, 0:ntok], cos_d[:, tok0:tok0 + ntok], (), (ct,), sem=ct)
        P.dma("sp", sn.t[0:64, 0:ntok], sin_d[:, tok0:tok0 + ntok], (), (sn,), sem=sn)
        t1, t2 = ctx["rt1"][k % 2], ctx["rt2"][k % 2]
        P.op("dve", lambda e: e.tensor_tensor(out=t1.t[0:64, 0:ntok], in0=pa.t[0:64, 0:ntok],
                                              in1=ct.t[0:64, 0:ntok], op=ALU.mult), (pa, ct), (t1,))
        P.op("dve", lambda e: e.tensor_tensor(out=t2.t[0:64, 0:ntok], in0=pb.t[0:64, 0:ntok],
                                              in1=sn.t[0:64, 0:ntok], op=ALU.mult), (pb, sn), (t2,))
        s = ctx["stg"].get()
        P.op("dve", lambda e: e.tensor_tensor(out=s.t[0:64, 0:ntok], in0=t1.t[0:64, 0:ntok],
                                               in1=t2.t[0:64, 0:ntok], op=ALU.add), (t1, t2), (s,))
        r0 = grp["row0"]
        P.dma("sp", dst[r0:r0 + 64, tok0:tok0 + ntok], s.t[0:64, 0:ntok], (s,), (), sem=s)
    return epi


def setup_rope(extra=None):
    def setup(P):
        ctx = {"stg": Stager(P, [128, 512], BF16, 4), "stg32": Stager(P, [128, 512], F32, 4), "cs_i": 0,
               "cos": [P.sb([128, 512], F32) for _ in range(2)],
               "sin": [P.sb([128, 512], F32) for _ in range(2)],
               "rt1": [P.sb([128, 512], F32) for _ in range(2)],
               "rt2": [P.sb([128, 512], F32) for _ in range(2)]}
        if extra is not None:
            extra(P, ctx)
        return ctx
    return setup


def epi_resid(dst, resid):
    def epi(P, ctx, tok0, ntok, pts, grp):
        p = pts[0]
        r0 = grp["row0"]
        k = ctx["r_i"]
        ctx["r_i"] += 1
        rt = ctx["rt"][k % 3]
        P.dma("sp", rt.t[:, 0:ntok], resid[r0:r0 + 128, tok0:tok0 + ntok], (), (rt,), sem=rt)
        s = ctx["stg"].get()
        P.op("dve", lambda e: e.tensor_tensor(out=s.t[:, 0:ntok], in0=p.t[:, 0:ntok], in1=rt.t[:, 0:ntok],
                                              op=ALU.add), (p, rt), (s,))
        P.dma("sp", dst[r0:r0 + 128, tok0:tok0 + ntok], s.t[:, 0:ntok], (s,), (), sem=s)
    return epi


def setup_resid(P):
    return {"stg": Stager(P, [128, 512], F32, 3), "r_i": 0, "rt": [P.sb([128, 512], F32) for _ in range(3)]}


def epi_merge(dst, gT):
    def epi(P, ctx, tok0, ntok, pts, grp):
        pa, pb = pts
        r0 = grp["row0"]
        k = ctx["r_i"]
        ctx["r_i"] += 1
        g0, g1 = ctx["g0"][k % 2], ctx["g1"][k % 2]
        t1, t2 = ctx["t1"][k % 2], ctx["t2"][k % 2]
        P.dma("sp", g0.t[:, 0:ntok], gT[r0:r0 + 128, tok0:tok0 + ntok], (), (g0,), sem=g0)
        P.dma("sp", g1.t[:, 0:ntok], gT[D + r0:D + r0 + 128, tok0:tok0 + ntok], (), (g1,), sem=g1)
        P.op("dve", lambda e: e.tensor_tensor(out=t1.t[:, 0:ntok], in0=pa.t[:, 0:ntok], in1=g0.t[:, 0:ntok],
                                              op=ALU.mult), (pa, g0), (t1,))
        P.op("dve", lambda e: e.tensor_tensor(out=t2.t[:, 0:ntok], in0=pb.t[:, 0:ntok], in1=g1.t[:, 0:ntok],
                                              op=ALU.mult), (pb, g1), (t2,))
        s = ctx["stg"].get()
        P.op("dve", lambda e: e.tensor_tensor(out=s.t[:, 0:ntok], in0=t1.t[:, 0:ntok], in1=t2.t[:, 0:ntok],
                                               op=ALU.add), (t1, t2), (s,))
        P.dma("sp", dst[r0:r0 + 128, tok0:tok0 + ntok], s.t[:, 0:ntok], (s,), (), sem=s)
    return epi


def setup_merge(P):
    return {"stg": Stager(P, [128, 512], BF16, 3), "r_i": 0,
            "g0": [P.sb([128, 512], F32) for _ in range(2)], "g1": [P.sb([128, 512], F32) for _ in range(2)],
            "t1": [P.sb([128, 512], F32) for _ in range(2)], "t2": [P.sb([128, 512], F32) for _ in range(2)]}


def epi_swiglu(dst):
    def epi(P, ctx, tok0, ntok, pts, grp):
        pg, pu = pts
        r0 = grp["row0"]
        k = ctx["r_i"]
        ctx["r_i"] += 1
        t1 = ctx["t1"][k % 3]
        P.op("act", lambda e: e.activation(out=t1.t[:, 0:ntok], in_=pg.t[:, 0:ntok], func=AF.Silu), (pg,), (t1,))
        s = ctx["stg"].get()
        P.op("dve", lambda e: e.tensor_tensor(out=s.t[:, 0:ntok], in0=pu.t[:, 0:ntok], in1=t1.t[:, 0:ntok],
                                              op=ALU.mult), (pu, t1), (s,))
        P.dma("sp", dst[r0:r0 + 128, tok0:tok0 + ntok], s.t[:, 0:ntok], (s,), (), sem=s)
    return epi


def setup_swiglu(P):
    return {"stg": Stager(P, [128, 512], BF16, 3), "r_i": 0, "t1": [P.sb([128, 512], F32) for _ in range(3)]}


def latent_blocks(w_in, off, nlat, rawdst, rope=None):
    ncl = nlat * 128
    wsegs = [(w_in[:, off:off + ncl], 0, 16, 0, ncl)]
    groups = [dict(accs=[("fm", m * 128, 128, 0, 16)], row0=m * 128, epi=epi_store_fm32(rawdst)) for m in range(nlat)]
    CB = ncl
    if rope is not None:
        (rdst, cos_d, sin_d, roff) = rope
        wsegs.append((w_in[:, roff:roff + 64], 0, 16, ncl, 64))
        wsegs.append((ncl + 32, 0, 16, ncl + 64, 32))
        wsegs.append((ncl, 0, 16, ncl + 96, 32))
        groups.append(dict(accs=[("fm", ncl, 64, 0, 16), ("fm", ncl + 64, 64, 0, 16)], row0=0,
                           epi=epi_rope(rdst, cos_d, sin_d)))
        CB = ncl + 128
    return [dict(wsegs=wsegs, groups=groups)], setup_rope(), CB


def attn_phase(P, S, So, units, consts, kind):
    nq = So // 256
    nkt_all = S // 128
    dvv = 128 if kind == "mla" else 256
    P.begin()
    if kind == "mla":
        kn = [P.sb([128, S], BF16) for _ in range(2)]
        qn = [P.sb([128, So], BF16) for _ in range(2)]
    else:
        kn = [P.sb([128, 2, S], BF16) for _ in range(2)]
        qn = [P.sb([128, 2, So], BF16) for _ in range(2)]
    vt = [P.sb([128, nkt_all, 257], BF16) for _ in range(2)]
    ident = P.sb([128, 128], BF16)
    P.dma("pool", ident.t[:], consts["ident"], (), (ident,), sem=ident)
    for v in vt:
        if kind == "mla":
            P.op("pool", lambda e, v=v: e.memset(v.t[:, :, 128:257], 0.0), (), (v,))
        P.op("pool", lambda e, v=v: e.memset(v.t[:, :, dvv:dvv + 1], 1.0), (), (v,))
    nS = 5 if kind == "mla" else 3
    LA = nS - 2
    NPT = LA + 3
    pt = [P.sb([128, 512], BF16) for _ in range(NPT)]
    sbank = [P.ps() for _ in range(nS)]
    tpb = P.ps([128, 1024], BF16)
    tpr = [Res(), Res()]
    rsm = [P.sb([128, 1], F32) for _ in range(4)]
    if kind == "mla":
        kr = P.sb([64, S], BF16)
        qr = [P.sb([64, So], BF16) for _ in range(2)]
        P.dma("sp", kr.t[:], consts["krT"], (), (kr,), sem=kr)
        msk = P.sb([128, 2048], F32)
        P.dma("sp", msk.t[:], consts["mask_mla"].rearrange("p a b -> p (a b)"), (), (msk,), sem=msk)
        obank = [P.ps() for _ in range(2)]
        ybf = [P.sb([128, 128], BF16) for _ in range(4)]
        ystg = [P.sb([128, 256], BF16) for _ in range(2)]
    else:
        mskd = [P.sb([128, 2048], F32) for _ in range(2)]
        abias = P.sb([128, NH_D * 80], F32)
        P.dma("sp", abias.t[:], consts["abias"], (), (abias,), sem=abias)
        obank = [P.ps() for _ in range(4)]
        o1n = [P.sb([128, 256], F32) for _ in range(2)]
        ofull = [P.sb([128, 256], F32) for _ in range(2)]
        junk = P.sb([128, 256], F32)
        ybf = [P.sb([128, 256], BF16) for _ in range(2)]
        ystg = [P.sb([128, 2, 256], BF16) for _ in range(2)]
        r2l = [P.sb([128, 1], F32) for _ in range(2)]
        ssq = [P.sb([128, 1], F32) for _ in range(2)]
        lnv = [P.sb([128, 1], F32) for _ in range(2)]
        rstd = [P.sb([128, 1], F32) for _ in range(2)]
        epsc = P.sb([128, 1], F32)
        P.op("dve", lambda e: e.memset(epsc.t[:], EPS), (), (epsc,))
        gsub = P.sb([128, 256], F32)
        P.dma("sp", gsub.t[:], consts["gsub"], (), (gsub,), sem=gsub)
        P.op("act", lambda e: e.activation(out=gsub.t[:], in_=gsub.t[:], func=AF.Copy, scale=1.0 - LAMBDA_INIT),
             (gsub,), (gsub,))
        lt = [P.sb([128, 128], F32) for _ in range(4)]
        for i, nm in enumerate(["lq1", "lk1", "lq2", "lk2"]):
            P.dma("sp", lt[i].t[:], consts[nm], (), (lt[i],), sem=lt[i])
        pr = [P.sb([128, 128], F32) for _ in range(2)]
        sm = [P.sb([128, 1], F32) for _ in range(2)]
        ex = [P.sb([128, 1], F32) for _ in range(2)]
        neglam = P.sb([128, 1], F32)
        for i in range(2):
            P.op("dve", lambda e, i=i: e.tensor_tensor(out=pr[i].t[:], in0=lt[2 * i].t[:], in1=lt[2 * i + 1].t[:],
                                                       op=ALU.mult), (lt[2 * i], lt[2 * i + 1]), (pr[i],))
            P.op("act", lambda e, i=i: e.activation(out=junk.t[:, 0:128], in_=pr[i].t[:], func=AF.Copy,
                                                    accum_out=sm[i].t[:]), (pr[i],), (junk, sm[i]))
            P.op("act", lambda e, i=i: e.activation(out=ex[i].t[:], in_=sm[i].t[:], func=AF.Exp), (sm[i],), (ex[i],))
        P.op("dve", lambda e: e.tensor_tensor(out=neglam.t[:], in0=ex[1].t[:], in1=ex[0].t[:], op=ALU.subtract),
             (ex[0], ex[1]), (neglam,))
        P.op("dve", lambda e: e.tensor_scalar(out=neglam.t[:], in0=neglam.t[:], scalar1=-LAMBDA_INIT, scalar2=None,
                                              op0=ALU.add), (neglam,), (neglam,))

    vloaded = {}

    def load_unit(u):
        un = units[u]
        b = u % 2
        if kind == "mla":
            P.dma("sp", kn[b].t[:], un["kT"], (), (kn[b],), sem=kn[b])
            P.dma("sp", qn[b].t[:], un["qT"], (), (qn[b],), sem=qn[b])
            P.dma("sp", qr[b].t[:], un["qrT"], (), (qr[b],), sem=qr[b])
        else:
            for m in range(2):
                P.dma("sp", kn[b].t[:, m, :], un["kT"][m], (), (kn[b],), sem=kn[b])
                P.dma("sp", qn[b].t[:, m, :], un["qT"][m], (), (qn[b],), sem=qn[b])
        vk = un["vkey"]
        if vk not in vloaded:
            vb = len(vloaded) % 2
            vloaded[vk] = vb
            vv = un["v"].rearrange("(t p) c -> p t c", p=128)
            step = 16
            for t0 in range(0, nkt_all, step):
                t1 = min(nkt_all, t0 + step)
                P.dma("sp", vt[vb].t[:, t0:t1, 0:dvv], vv[:, t0:t1, :], (), (vt[vb],), sem=vt[vb])
            if kind == "diff":
                P.dma("sp", mskd[vb].t[:], consts["mask_diff"][un["h"]].rearrange("p a b -> p (a b)"), (),
                      (mskd[vb],), sem=mskd[vb])

    pairs = []
    for u in range(len(units)):
        for i in range(nq):
            if kind == "mla":
                ks = list(range(4 * (i + 1)))
            else:
                slope = 2.0 ** (-(units[u]["h"] + 1))
                ks = [kt for kt in range(8 * (i + 1)) if slope * ((8 * i - kt) * 128 - 127) < 140.0]
            for n_, k_ in enumerate(ks):
                pairs.append((u, i, k_, n_ == 0, n_ == len(ks) - 1))
    cnt = {"s": 0, "p": 0, "e": 0}
    pend = {}

    def do_qk(n):
        (u, i, kx, first, last) = pairs[n]
        un = units[u]
        b = u % 2
        sp_ = sbank[cnt["s"] % nS]
        cnt["s"] += 1
        q0 = i * 256
        p = pt[cnt["p"] % NPT]
        cnt["p"] += 1
        sc = un["scale"]
        if kind == "mla":
            nk2 = 4 * (i + 1)
            for j in range(2):
                k0 = (2 * kx + j) * 128
                P.op("pe", lambda e, j=j, k0=k0: e.matmul(
                    sp_.t[:, j * 256:(j + 1) * 256], lhsT=kn[b].t[:, k0:k0 + 128], rhs=qn[b].t[:, q0:q0 + 256],
                    start=True, stop=False), (kn[b], qn[b]), (sp_,))
                P.op("pe", lambda e, j=j, k0=k0: e.matmul(
                    sp_.t[:, j * 256:(j + 1) * 256], lhsT=kr.t[0:64, k0:k0 + 128], rhs=qr[b].t[0:64, q0:q0 + 256],
                    start=False, stop=True), (kr, qr[b]), (sp_,))
            P.op("act", lambda e: e.activation(out=p.t[:], in_=sp_.t[:, 0:512], func=AF.Exp, scale=sc), (sp_,), (p,))
            if kx >= nk2 - 4:
                kk = 2 * (kx - (nk2 - 4))
                P.op("dve", lambda e: e.tensor_tensor(out=p.t[:], in0=p.t[:], in1=msk.t[:, kk * 256:(kk + 2) * 256],
                                                      op=ALU.mult), (p, msk), (p,))
        else:
            h = un["h"]
            k0 = kx * 128
            for m in range(2):
                P.op("pe", lambda e, m=m: e.matmul(
                    sp_.t[:, m * 256:(m + 1) * 256], lhsT=kn[b].t[:, m, k0:k0 + 128], rhs=qn[b].t[:, m, q0:q0 + 256],
                    start=True, stop=True), (kn[b], qn[b]), (sp_,))
            dst = kx - 8 * i
            if h == 0:
                for jq in range(2):
                    bi = h * 80 + dst - 2 * jq + 66
                    c0 = jq * 128
                    P.op("act", lambda e, bi=bi, c0=c0: e.activation(
                        out=p.t[:, 0:512].rearrange("p (m q) -> p m q", m=2)[:, :, c0:c0 + 128],
                        in_=sp_.t[:, 0:512].rearrange("p (m q) -> p m q", m=2)[:, :, c0:c0 + 128], func=AF.Exp,
                        scale=sc, bias=abias.t[:, bi:bi + 1]), (sp_, abias), (p,))
            else:
                bi = h * 80 + dst + 66
                P.op("act", lambda e, bi=bi: e.activation(out=p.t[:], in_=sp_.t[:, 0:512], func=AF.Exp, scale=sc,
                                                          bias=abias.t[:, bi:bi + 1]), (sp_, abias), (p,))
            if kx >= 8 * i:
                kk = kx - 8 * i
                mk_ = mskd[vloaded[un["vkey"]]]
                for m in range(2):
                    P.op("dve", lambda e, m=m: e.tensor_tensor(
                        out=p.t[:, m * 256:(m + 1) * 256], in0=p.t[:, m * 256:(m + 1) * 256],
                        in1=mk_.t[:, kk * 256:(kk + 1) * 256], op=ALU.mult), (p, mk_), (p,))
        pend[n] = p

    def do_pv(n):
        (u, i, kx, first, last) = pairs[n]
        un = units[u]
        p = pend.pop(n)
        if i == 0 and first and u + 1 < len(units):
            load_unit(u + 1)
        v = vt[vloaded[un["vkey"]]]
        ei = u * nq + i
        h = un["h"]
        if kind == "mla":
            for j in range(2):
                for qs in range(2):
                    P.op("pe", lambda e, qs=qs, j=j: e.matmul(
                        obank[qs].t[:, 0:257], lhsT=p.t[:, j * 256 + qs * 128:j * 256 + (qs + 1) * 128],
                        rhs=v.t[:, 2 * kx + j, :], start=(first and j == 0), stop=(last and j == 1)),
                        (p, v), (obank[qs],))
            if last:
                ys = ystg[ei % 2]
                tr = tpr[ei % 2]
                for qs in range(2):
                    ob = obank[qs]
                    r = rsm[cnt["e"] % 4]
                    yb = ybf[cnt["e"] % 4]
                    cnt["e"] += 1
                    P.op("dve", lambda e, r=r, ob=ob: e.reciprocal(out=r.t[:], in_=ob.t[:, 128:129]), (ob,), (r,))
                    P.op("dve", lambda e, r=r, yb=yb, ob=ob: e.tensor_scalar(
                        out=yb.t[:], in0=ob.t[:, 0:128], scalar1=r.t[:], scalar2=None, op0=ALU.mult), (ob, r), (yb,))
                    c0 = (ei % 2) * 256 + qs * 128
                    P.op("pe", lambda e, yb=yb, c0=c0: e.transpose(tpb.t[:, c0:c0 + 128], yb.t[:], ident.t[:]),
                         (yb, ident), (_R(tr),))
                c0 = (ei % 2) * 256
                copy_op(P, "dve", ys.t[:], tpb.t[:, c0:c0 + 256], (_R(tr),), (ys,))
                P.dma("sp", consts["yT"][h * 128:(h + 1) * 128, i * 256:(i + 1) * 256], ys.t[:], (ys,), (), sem=ys)
        else:
            for m in range(2):
                for qs in range(2):
                    ob = obank[m * 2 + qs]
                    P.op("pe", lambda e, qs=qs, m=m, ob=ob: e.matmul(
                        ob.t[:, 0:257], lhsT=p.t[:, m * 256 + qs * 128:m * 256 + (qs + 1) * 128],
                        rhs=v.t[:, kx, :], start=first, stop=last), (p, v), (ob,))
            if last:
                ys = ystg[ei % 2]
                tr = tpr[ei % 2]
                for qs in range(2):
                    o1b, o2b = obank[qs], obank[2 + qs]
                    r1 = rsm[cnt["e"] % 4]
                    r2 = rsm[(cnt["e"] + 1) % 4]
                    cnt["e"] += 2
                    k2 = qs
                    t1, rl, of, sq_, ln_, rs_, yb = o1n[k2], r2l[k2], ofull[k2], ssq[k2], lnv[k2], rstd[k2], ybf[k2]
                    P.op("dve", lambda e, r1=r1, o1b=o1b: e.reciprocal(out=r1.t[:], in_=o1b.t[:, 256:257]), (o1b,), (r1,))
                    P.op("dve", lambda e, r1=r1, o1b=o1b, t1=t1: e.tensor_scalar(
                        out=t1.t[:], in0=o1b.t[:, 0:256], scalar1=r1.t[:], scalar2=None, op0=ALU.mult),
                        (o1b, r1), (t1,))
                    P.op("dve", lambda e, r2=r2, o2b=o2b: e.reciprocal(out=r2.t[:], in_=o2b.t[:, 256:257]), (o2b,), (r2,))
                    P.op("dve", lambda e, r2=r2, rl=rl: e.tensor_tensor(out=rl.t[:], in0=r2.t[:], in1=neglam.t[:],
                                                                        op=ALU.mult), (r2, neglam), (rl,))
                    P.op("dve", lambda e, rl=rl, of=of, o2b=o2b, t1=t1: e.scalar_tensor_tensor(
                        out=of.t[:], in0=o2b.t[:, 0:256], scalar=rl.t[:], in1=t1.t[:],
                        op0=ALU.mult, op1=ALU.add), (o2b, rl, t1), (of,))
                    P.op("act", lambda e, of=of, sq_=sq_: e.activation(out=junk.t[:], in_=of.t[:], func=AF.Square,
                                                                       accum_out=sq_.t[:]), (of,), (junk, sq_))
                    P.op("act", lambda e, sq_=sq_, ln_=ln_: e.activation(out=ln_.t[:], in_=sq_.t[:], func=AF.Ln,
                                                                         scale=1.0 / 256, bias=epsc.t[:]),
                         (sq_, epsc), (ln_,))
                    P.op("act", lambda e, ln_=ln_, rs_=rs_: e.activation(out=rs_.t[:], in_=ln_.t[:], func=AF.Exp,
                                                                         scale=-0.5), (ln_,), (rs_,))
                    P.op("dve", lambda e, yb=yb, of=of, rs_=rs_: e.scalar_tensor_tensor(
                        out=yb.t[:], in0=of.t[:], scalar=rs_.t[:], in1=gsub.t[:], op0=ALU.mult, op1=ALU.mult),
                        (of, rs_, gsub), (yb,))
                    for c in range(2):
                        c0 = (ei % 2) * 512 + c * 256 + qs * 128
                        P.op("pe", lambda e, yb=yb, c=c, c0=c0: e.transpose(
                            tpb.t[:, c0:c0 + 128], yb.t[:, c * 128:(c + 1) * 128], ident.t[:]),
                            (yb, ident), (_R(tr),))
                c0 = (ei % 2) * 512
                copy_op(P, "dve", ys.t[:, :, :], tpb.t[:, c0:c0 + 512].rearrange("p (c q) -> p c q", c=2),
                        (_R(tr),), (ys,))
                P.dma("sp", consts["yT"][h * 256:(h + 1) * 256, i * 256:(i + 1) * 256].rearrange(
                    "(c p) q -> p c q", p=128), ys.t[:, :, :], (ys,), (), sem=ys)

    load_unit(0)
    for n in range(len(pairs) + LA):
        if n < len(pairs):
            do_qk(n)
        if n >= LA:
            do_pv(n - LA)
    P.end()


class _R:
    __slots__ = ("res",)

    def __init__(self, res):
        self.res = res


def build_program(S, debug=False, upto=99):
    So = S // 4
    To = min(512, So)
    nc = bass.Bass("TRN2", target_bir_lowering=False)
    P = Prog(nc)

    def din(name, shape, dt=F32):
        return nc.dram_tensor(name, list(shape), dt, kind="ExternalInput").ap()

    def scr(name, shape, dt):
        kind = "ExternalOutput" if debug else "Internal"
        return nc.dram_tensor(name, list(shape), dt, kind=kind).ap()

    xall = din("xall", [D, S])
    xown = din("xown", [D, So])
    w_in = din("w_in", [D, D_IN])
    w_uq = din("w_uq", [QR, NH_M * 192])
    w_ukv = din("w_ukv", [KVR, NH_M * 256])
    w_mla = din("w_mla_proj", [D, D])
    w_dif = din("w_diff_proj", [D, D])
    w_out = din("w_out", [D, D])
    w_fg = din("w_ffn_gate", [D, DFF])
    w_fu = din("w_ffn_up", [D, DFF])
    w_fd = din("w_ffn_down", [DFF, D])
    g_attn = din("g_attn", [128, 16])
    g_q = din("g_q", [128, 6])
    g_kv = din("g_kv", [128, 4])
    g_ffn = din("g_ffn", [128, 16])
    g_fin = din("g_fin", [128, 16])
    b_gate = din("b_gate", [128, 32])
    gsub = din("gsub", [128, 256])
    lq1 = din("lq1", [128, 128])
    lk1 = din("lk1", [128, 128])
    lq2 = din("lq2", [128, 128])
    lk2 = din("lk2", [128, 128])
    cos_all = din("cos_all", [64, S])
    sin_all = din("sin_all", [64, S])
    cos_own = din("cos_own", [64, So])
    sin_own = din("sin_own", [64, So])
    abias = din("abias", [128, NH_D * 80])
    mask_mla = din("mask_mla", [128, 8, 256])
    mask_diff = din("mask_diff", [NH_D, 128, 8, 256])
    ident = din("ident", [128, 128])

    hT_all = scr("hT_all", [D, S], BF16)
    hT_own = scr("hT_own", [D, So], BF16)
    dkT = scr("dkT", [D, S], BF16)
    dv = scr("dv", [S, D], BF16)
    ckvnT = scr("ckvnT", [KVR, S], BF16)
    ckvraw = scr("ckvraw", [KVR, S], F32)
    cqraw = scr("cqraw", [QR, So], F32)
    krT = scr("krT", [64, S], BF16)
    knT = scr("knT", [D, S], BF16)
    vm = scr("vm", [S, D], BF16)
    dqT = scr("dqT", [D, So], BF16)
    gT = scr("gT", [2 * D, So], F32)
    cqnT = scr("cqnT", [QR, So], BF16)
    qnT = scr("qnT", [D, So], BF16)
    qrT = scr("qrT", [NH_M * 64, So], BF16)
    ymT = scr("ymT", [D, So], BF16)
    ydT = scr("ydT", [D, So], BF16)
    mT = scr("mT", [D, So], BF16)
    x1T = scr("x1T", [D, So], F32)
    h2T = scr("h2T", [D, So], BF16)
    aT = scr("aT", [DFF, So], BF16)
    x2T = scr("x2T", [D, So], F32)
    outT = nc.dram_tensor("outT", [D, So], F32, kind="ExternalOutput").ap()

    def done():
        P.top.close()
        return nc, P

    norm_phase(P, xall, hT_all, g_attn, S, BF16)
    norm_phase(P, xown, hT_own, g_attn, So, BF16)
    if upto <= 0:
        return done()

    blocks = [dict(wsegs=[(w_in[:, OFF_DK:OFF_DK + 2048], 0, 16, 0, 2048)],
                   groups=[dict(accs=[("fm", m * 128, 128, 0, 16)], row0=m * 128, epi=epi_store_fm(dkT, None))
                           for m in range(16)]),
              dict(wsegs=[(w_in[:, OFF_DV:OFF_DV + 2048], 0, 16, 0, 2048)],
                   groups=[dict(accs=[("tm", cb * 512, 512, 0, 16)], col0=cb * 512, epi=epi_store_tm(dv))
                           for cb in range(4)])]
    gemm_phase(P, [(hT_all, 0, 16)], 16, S, blocks, CB=2048, setup=setup_stg())
    if upto <= 0.3:
        return done()
    blocks, setup, CB = latent_blocks(w_in, OFF_CKV, 4, ckvraw, rope=(krT, cos_all, sin_all, OFF_KPE))
    gemm_phase(P, [(hT_all, 0, 16)], 16, S, blocks, CB=CB, setup=setup)
    norm_phase(P, ckvraw, ckvnT, g_kv, S, BF16, nfeat=KVR)

    if upto <= 0.6:
        return done()
    groups = []
    for h in range(NH_M):
        groups.append(dict(accs=[("fm", h * 256, 128, 0, 4)], row0=h * 128, epi=epi_store_fm(knT, None)))
        groups.append(dict(accs=[("tm", h * 256 + 128, 128, 0, 4)], col0=h * 128, epi=epi_store_tm(vm)))
    blocks = [dict(wsegs=[(w_ukv[:, :], 0, 4, 0, 4096)], groups=groups)]
    gemm_phase(P, [(ckvnT, 0, 4)], 4, S, blocks, CB=4096, nwbuf=1, setup=setup_stg())
    if upto <= 1:
        return done()

    blocks = [dict(wsegs=[(w_in[:, OFF_DQ:OFF_DQ + 2048], 0, 16, 0, 2048)],
                   groups=[dict(accs=[("fm", m * 128, 128, 0, 16)], row0=m * 128, epi=epi_store_fm(dqT, None))
                           for m in range(16)])]
    gemm_phase(P, [(hT_own, 0, 16)], 16, So, blocks, T=To, CB=2048, nwbuf=1, setup=setup_stg())

    blocks = []
    for cb in range(2):
        c0 = OFF_G + cb * 2048
        blocks.append(dict(
            wsegs=[(w_in[:, c0:c0 + 2048], 0, 16, 0, 2048)],
            groups=[dict(accs=[("fm", m * 128, 128, 0, 16)], row0=cb * 2048 + m * 128, bcol=cb * 16 + m,
                         epi=epi_gate(gT)) for m in range(16)]))

    def setup_gate(P):
        ctx = {"stg": Stager(P, [128, 512], F32, 4), "bg": P.sb([128, 32], F32)}
        P.dma("sp", ctx["bg"].t[:], b_gate, (), (ctx["bg"],), sem=ctx["bg"])
        return ctx
    gemm_phase(P, [(hT_own, 0, 16)], 16, So, blocks, T=To, CB=2048, setup=setup_gate)

    blocks, setup, CB = latent_blocks(w_in, OFF_CQ, 6, cqraw)
    gemm_phase(P, [(hT_own, 0, 16)], 16, So, blocks, T=To, CB=CB, setup=setup)
    norm_phase(P, cqraw, cqnT, g_q, So, BF16, nfeat=QR)

    wsegs, groups = [], []
    for h in range(NH_M):
        o = h * 256
        wsegs.append((w_uq[:, h * 192:h * 192 + 192], 0, 6, o, 192))
        wsegs.append((o + 160, 0, 6, o + 192, 32))
        wsegs.append((o + 128, 0, 6, o + 224, 32))
        groups.append(dict(accs=[("fm", o, 128, 0, 6)], row0=h * 128, epi=epi_store_fm(qnT, None)))
        groups.append(dict(accs=[("fm", o + 128, 64, 0, 6), ("fm", o + 192, 64, 0, 6)], row0=h * 64,
                           epi=epi_rope(qrT, cos_own, sin_own)))
    blocks = [dict(wsegs=wsegs, groups=groups)]
    gemm_phase(P, [(cqnT, 0, 6)], 6, So, blocks, T=To, CB=4096, nwbuf=1, setup=setup_rope())
    if upto <= 2:
        return done()

    consts = dict(ident=ident, krT=krT, mask_mla=mask_mla, yT=ymT)
    units = [dict(kT=knT[h * 128:(h + 1) * 128, :], qT=qnT[h * 128:(h + 1) * 128, :],
                  qrT=qrT[h * 64:(h + 1) * 64, :], v=vm[:, h * 128:(h + 1) * 128], vkey=h, h=h, m=0,
                  scale=1.0 / math.sqrt(192.0)) for h in range(NH_M)]
    attn_phase(P, S, So, units, consts, "mla")
    if upto <= 3:
        return done()

    consts = dict(ident=ident, abias=abias, mask_diff=mask_diff, yT=ydT, gsub=gsub, lq1=lq1, lk1=lk1, lq2=lq2, lk2=lk2)
    units = []
    for h in range(NH_D):
        rows = [(h * 2 + m) * 128 for m in range(2)]
        units.append(dict(kT=[dkT[r:r + 128, :] for r in rows], qT=[dqT[r:r + 128, :] for r in rows],
                          v=dv[:, h * 256:(h + 1) * 256], vkey=h, h=h, scale=1.0 / math.sqrt(128.0)))
    attn_phase(P, S, So, units, consts, "diff")
    if upto <= 4:
        return done()

    blocks = []
    for cb in range(2):
        c0 = cb * 1024
        blocks.append(dict(
            wsegs=[(w_mla[:, c0:c0 + 1024], 0, 16, 0, 1024), (w_dif[:, c0:c0 + 1024], 16, 16, 0, 1024)],
            groups=[dict(accs=[("fm", m * 128, 128, 0, 16), ("fm", m * 128, 128, 16, 32)], row0=c0 + m * 128,
                         epi=epi_merge(mT, gT)) for m in range(8)]))
    gemm_phase(P, [(ymT, 0, 16), (ydT, 16, 16)], 32, So, blocks, T=To, CB=1024, nwbuf=1, setup=setup_merge)

    blocks = [dict(wsegs=[(w_out[:, :], 0, 16, 0, 2048)],
                   groups=[dict(accs=[("fm", m * 128, 128, 0, 16)], row0=m * 128, epi=epi_resid(x1T, xown))
                           for m in range(16)])]
    gemm_phase(P, [(mT, 0, 16)], 16, So, blocks, T=To, CB=2048, nwbuf=1, setup=setup_resid)
    if upto <= 5:
        return done()

    norm_phase(P, x1T, h2T, g_ffn, So, BF16)
    blocks = []
    for c0 in range(0, DFF, 1024):
        n = min(1024, DFF - c0)
        blocks.append(dict(
            wsegs=[(w_fg[:, c0:c0 + n], 0, 16, 0, n), (w_fu[:, c0:c0 + n], 0, 16, 1024, n)],
            groups=[dict(accs=[("fm", m * 128, 128, 0, 16), ("fm", 1024 + m * 128, 128, 0, 16)], row0=c0 + m * 128,
                         epi=epi_swiglu(aT)) for m in range(n // 128)]))
    gemm_phase(P, [(h2T, 0, 16)], 16, So, blocks, T=To, CB=2048, setup=setup_swiglu)
    blocks = []
    for cb in range(2):
        c0 = cb * 1024
        blocks.append(dict(
            wsegs=[(w_fd[:, c0:c0 + 1024], 0, 44, 0, 1024)],
            groups=[dict(accs=[("fm", m * 128, 128, 0, 44)], row0=c0 + m * 128, epi=epi_resid(x2T, x1T))
                    for m in range(8)]))
    gemm_phase(P, [(aT, 0, 44)], 44, So, blocks, T=To, CB=1024, nwbuf=1, setup=setup_resid)

    norm_phase(P, x2T, outT, g_fin, So, F32)
    return done()


def _col(v, n):
    return np.ascontiguousarray(np.asarray(v, np.float32).reshape(n, 128).T)


def _rep(v):
    v = np.asarray(v, np.float32).reshape(1, -1)
    return np.ascontiguousarray(np.broadcast_to(v, (128, v.shape[1])))


def position_tables(S):
    inv = (10000.0 ** (-np.arange(0, 64, 2, dtype=np.float32) / np.float32(64))).astype(np.float32)
    ang = np.arange(S, dtype=np.float32)[:, None] * inv[None, :]
    c, s = np.cos(ang).astype(np.float32).T, np.sin(ang).astype(np.float32).T
    cos2 = np.ascontiguousarray(np.concatenate([c, c], 0))
    sin2 = np.ascontiguousarray(np.concatenate([-s, s], 0))
    slopes = np.exp2(-8.0 * np.arange(1, NH_D + 1, dtype=np.float32) / NH_D).astype(np.float64)
    per_core = []
    p = np.arange(128)[:, None, None]
    for c_ in range(4):
        ab = np.zeros((128, NH_D, 80), np.float32)
        for h in range(NH_D):
            clampv = slopes[h] * (127 if h == 0 else 255)
            di = np.arange(80)[None, :]
            val = slopes[h] * ((di - 66 - 2 * c_) * 128 + np.arange(128)[:, None])
            ab[:, h, :] = np.minimum(val, clampv)
        kk = np.arange(8)[None, :, None]
        col = np.arange(256)[None, None, :]
        kr = kk * 128 + p
        qr = c_ * 256 + col
        allowed = (kr // 64) <= (qr // 64)
        mm = allowed.astype(np.float32)
        md = np.zeros((NH_D, 128, 8, 256), np.float32)
        for h in range(NH_D):
            corr = np.where(kr > qr, np.exp(-2.0 * slopes[h] * np.maximum(kr - qr, 0)), 1.0)
            md[h] = (allowed * corr).astype(np.float32)
        per_core.append(dict(abias=np.ascontiguousarray(ab.reshape(128, NH_D * 80)), mask_mla=np.ascontiguousarray(mm),
                             mask_diff=md))
    return cos2, sin2, per_core


_CACHE = {}


def kernel(x, attn_norm_g, w_in, b_gate, q_norm_g, w_uq, kv_norm_g, w_ukv,
           lambda_q1, lambda_k1, lambda_q2, lambda_k2, diff_norm_g,
           w_mla_proj, w_diff_proj, w_out, ffn_norm_g, w_ffn_gate, w_ffn_up,
           w_ffn_down, final_norm_g, _debug=False, _upto=99):
    x = np.asarray(x, np.float32)
    B, S, _ = x.shape
    So = S // 4
    nq = So // 256
    key = (S, _debug, _upto)
    if key not in _CACHE:
        _CACHE[key] = build_program(S, debug=_debug, upto=_upto)
    nc, P = _CACHE[key]
    cos2, sin2, per_core = position_tables(S)
    f = lambda a: np.ascontiguousarray(np.asarray(a, np.float32))
    shared = dict(
        w_in=f(w_in[0]), w_uq=f(w_uq[0]), w_ukv=f(w_ukv[0]), w_mla_proj=f(w_mla_proj[0]),
        w_diff_proj=f(w_diff_proj[0]), w_out=f(w_out[0]), w_ffn_gate=f(w_ffn_gate[0]), w_ffn_up=f(w_ffn_up[0]),
        w_ffn_down=f(w_ffn_down[0]),
        g_attn=_col(attn_norm_g[0], 16), g_q=_col(q_norm_g[0], 6), g_kv=_col(kv_norm_g[0], 4),
        g_ffn=_col(ffn_norm_g[0], 16), g_fin=_col(final_norm_g, 16), b_gate=_col(b_gate[0], 32),
        gsub=_rep(diff_norm_g[0]), lq1=_rep(lambda_q1[0]), lk1=_rep(lambda_k1[0]), lq2=_rep(lambda_q2[0]),
        lk2=_rep(lambda_k2[0]), cos_all=cos2, sin_all=sin2, ident=np.eye(128, dtype=np.float32))
    in_maps = []
    own_idx = []
    for core in range(8):
        b, c = core // 4, core % 4
        idx = np.concatenate([np.arange((4 * i + c) * 256, (4 * i + c + 1) * 256) for i in range(nq)])
        own_idx.append(idx)
        xT = np.ascontiguousarray(x[b].T)
        m = dict(shared)
        m.update(xall=xT, xown=np.ascontiguousarray(xT[:, idx]),
                 cos_own=np.ascontiguousarray(cos2[:, idx]), sin_own=np.ascontiguousarray(sin2[:, idx]),
                 abias=per_core[c]["abias"], mask_mla=per_core[c]["mask_mla"], mask_diff=per_core[c]["mask_diff"])
        in_maps.append(m)
    res = run_bass_kernel_spmd(nc, in_maps, core_ids=list(range(8)))
    out = np.empty((B, S, D), np.float32)
    for core in range(8):
        b = core // 4
        out[b, own_idx[core], :] = np.asarray(res.results[core]["outT"], np.float32).T
    if _debug:
        return out, res.results
    return out
```

```python
import math
from contextlib import ExitStack

import numpy as np
import concourse.bass as bass
import concourse.mybir as mybir
from concourse.bass_utils import run_bass_kernel_spmd

F32 = mybir.dt.float32
BF16 = mybir.dt.bfloat16
AF = mybir.ActivationFunctionType
ALU = mybir.AluOpType

D = 2048
NH_M = 16
NH_D = 8
QR = 768
KVR = 512
DFF = 5632
EPS = 1e-6
OFF_CQ, OFF_CKV, OFF_KPE, OFF_DQ, OFF_DK, OFF_DV, OFF_G = 0, 768, 1280, 1344, 3392, 5440, 7488
D_IN = 11584
LAMBDA_INIT = 0.8 - 0.6 * math.exp(-0.3 * 0)
ENGS = ["pe", "act", "dve", "pool", "sp"]
NDELTA = 66


class Res:
    __slots__ = ("w", "r", "dsem")

    def __init__(self):
        self.w = {}
        self.r = {}
        self.dsem = None


class Tl:
    __slots__ = ("t", "res")

    def __init__(self, t):
        self.t = t
        self.res = Res()


class Prog:
    def __init__(self, nc, n_dma_sems=80):
        self.nc = nc
        self.top = ExitStack()
        self.semobj = {}
        for e in ENGS:
            self.semobj["e:" + e] = self.top.enter_context(nc.semaphore("sem_" + e))
        self.cnt = {e: 0 for e in ENGS}
        self.dkeys = []
        self.dval = {}
        for i in range(n_dma_sems):
            k = "d:%d" % i
            self.semobj[k] = self.top.enter_context(nc.semaphore("semd_%d" % i))
            self.dkeys.append(k)
            self.dval[k] = 0
        self.known = {e: {} for e in ENGS}
        self.n_inst = 0
        self.uid = 0

    def begin(self):
        self.stack = ExitStack()
        self.ops = {e: [] for e in ENGS}
        self.free_d = list(self.dkeys)
        self.used_d = []
        self.rr = 0

    def sb(self, shape, dtype):
        self.uid += 1
        return Tl(self.stack.enter_context(self.nc.sbuf_tensor("sb%d" % self.uid, list(shape), dtype)))

    def ps(self, shape=(128, 512), dtype=F32):
        self.uid += 1
        return Tl(self.stack.enter_context(self.nc.psum_tensor("ps%d" % self.uid, list(shape), dtype)))

    def _need(self, eng, toks, waits):
        kn = self.known[eng]
        for k, v in toks.items():
            if eng == "pe" and k == "e:pe":
                continue
            if kn.get(k, 0) >= v:
                continue
            if waits.get(k, 0) < v:
                waits[k] = v

    def _deps(self, eng, reads, writes):
        waits = {}
        for r in reads:
            self._need(eng, r.res.w, waits)
        for w in writes:
            self._need(eng, w.res.w, waits)
            self._need(eng, w.res.r, waits)
        kn = self.known[eng]
        for k, v in waits.items():
            kn[k] = v
        return list(waits.items())

    def op(self, eng, fn, reads=(), writes=()):
        waits = self._deps(eng, reads, writes)
        self.cnt[eng] += 1
        key = "e:" + eng
        val = self.cnt[eng]
        for r in reads:
            r.res.r[key] = val
        for w in writes:
            w.res.w[key] = val
        self.ops[eng].append((waits, fn, (key, 1)))

    def dma(self, eng, out, in_, reads=(), writes=(), sem=None):
        waits = self._deps(eng, reads, writes)
        r = sem.res
        if r.dsem is None:
            r.dsem = self.free_d.pop() if eng != "pool" else self.free_d.pop(0)
            self.used_d.append(r.dsem)
        key = r.dsem
        self.dval[key] += 16
        val = self.dval[key]
        for x in reads:
            x.res.r[key] = val
        for x in writes:
            x.res.w[key] = val
        self.ops[eng].append((waits, lambda e: e.dma_start(out=out, in_=in_), (key, 16)))

    def end(self):
        for e in ENGS:
            waits = []
            for x in ENGS:
                k = "e:" + x
                if self.cnt[x] > self.known[e].get(k, 0):
                    waits.append((k, self.cnt[x]))
                    self.known[e][k] = self.cnt[x]
            if e == "sp":
                for k in self.used_d:
                    if self.dval[k] > self.known[e].get(k, 0):
                        waits.append((k, self.dval[k]))
            if waits:
                self.ops[e].append((waits, None, None))
        for e in ENGS:
            for k in self.used_d:
                self.known[e][k] = self.dval[k]
        ops = self.ops
        semobj = self.semobj

        def mk(name):
            def body(e):
                for waits, fn, inc in ops[name]:
                    for k, v in waits:
                        e.wait_ge(semobj[k], v)
                    if fn is not None:
                        ins = fn(e)
                        if inc is not None:
                            ins.then_inc(semobj[inc[0]], inc[1])
            return body

        for name in ENGS:
            self.n_inst += len(ops[name])
        with self.nc.Block() as block:
            block.tensor(mk("pe"))
            block.scalar(mk("act"))
            block.vector(mk("dve"))
            block.gpsimd(mk("pool"))
            block.sync(mk("sp"))
        self.stack.close()
        self.ops = None

    def evac_eng(self):
        self.rr += 1
        return "act" if (self.rr & 1) else "dve"


def copy_op(P, eng, out_ap, in_ap, reads, writes):
    if eng == "act":
        P.op("act", lambda e: e.activation(out=out_ap, in_=in_ap, func=AF.Copy), reads, writes)
    else:
        P.op("dve", lambda e: e.tensor_copy(out=out_ap, in_=in_ap), reads, writes)


def norm_phase(P, src, dst, gcol_dram, N, out_dtype, nfeat=D, T=512):
    T = min(T, N)
    KC = nfeat // 128
    P.begin()
    xt = [P.sb([128, KC, T], F32) for _ in range(2)]
    sq = [P.sb([128, KC, T], BF16) for _ in range(2)]
    st = [P.sb([128, KC, T], out_dtype) for _ in range(2)]
    ln = [P.sb([128, T], F32) for _ in range(2)]
    rs = [P.sb([128, T], F32) for _ in range(2)]
    ones = P.sb([128, 128], BF16)
    g = P.sb([128, KC], F32)
    epsc = P.sb([128, 1], F32)
    ps = [P.ps() for _ in range(2)]
    P.op("dve", lambda e: e.memset(ones.t[:], 1.0), (), (ones,))
    P.op("dve", lambda e: e.memset(epsc.t[:], EPS), (), (epsc,))
    P.dma("sp", g.t[:], gcol_dram, (), (g,), sem=g)
    srcv = src.rearrange("(kc p) n -> p kc n", p=128)
    dstv = dst.rearrange("(kc p) n -> p kc n", p=128)
    nt = N // T
    H = KC // 2 if KC >= 2 else KC

    def load(i):
        s = xt[i % 2]
        P.dma("sp", s.t[:, 0:H, :], srcv[:, 0:H, i * T:(i + 1) * T], (), (s,), sem=s)
        if H < KC:
            P.dma("sp", s.t[:, H:KC, :], srcv[:, H:KC, i * T:(i + 1) * T], (), (s,), sem=s)

    def square(i):
        x, q = xt[i % 2], sq[i % 2]
        P.op("act", lambda e: e.activation(out=q.t[:], in_=x.t[:], func=AF.Square), (x,), (q,))

    load(0)
    square(0)
    for i in range(nt):
        if i + 1 < nt:
            load(i + 1)
        x, q, o, l, r, p = xt[i % 2], sq[i % 2], st[i % 2], ln[i % 2], rs[i % 2], ps[i % 2]
        for kc in range(KC):
            P.op("pe", lambda e, p=p, q=q, kc=kc: e.matmul(p.t[:, 0:T], lhsT=ones.t[:], rhs=q.t[:, kc, :],
                                                          start=(kc == 0), stop=(kc == KC - 1)),
                 (ones, q), (p,))
        if i + 1 < nt:
            square(i + 1)
        P.op("act", lambda e, p=p, l=l: e.activation(out=l.t[:], in_=p.t[:, 0:T], func=AF.Ln,
                                                     scale=1.0 / nfeat, bias=epsc.t[:]), (p, epsc), (l,))
        P.op("act", lambda e, l=l, r=r: e.activation(out=r.t[:], in_=l.t[:], func=AF.Exp, scale=-0.5), (l,), (r,))
        for kc in range(KC):
            P.op("dve", lambda e, o=o, x=x, r=r, kc=kc: e.scalar_tensor_tensor(
                out=o.t[:, kc, :], in0=x.t[:, kc, :], scalar=g.t[:, kc:kc + 1], in1=r.t[:],
                op0=ALU.mult, op1=ALU.mult), (x, r, g), (o,))
        P.dma("sp", dstv[:, :, i * T:(i + 1) * T], o.t[:], (o,), (), sem=o)
    P.end()


def gemm_phase(P, act_srcs, KC, N, blocks, T=512, CB=512, setup=None, npsum=6, nwbuf=2):
    SEG = 512
    P.begin()
    nseg = (CB + SEG - 1) // SEG
    wt = [[P.sb([128, KC, min(SEG, CB - s * SEG)], BF16) for s in range(nseg)] for _ in range(nwbuf)]
    at = [P.sb([128, KC, T], BF16) for _ in range(2)]
    pss = [P.ps() for _ in range(npsum)]
    ctx = setup(P) if setup is not None else None
    nt = N // T
    seq = [(b, t) for b in range(len(blocks)) for t in range(nt)]
    psi = [0]

    def load_w(b):
        w = wt[b % nwbuf]
        for (wap, kc0, nkc, off, ncols) in blocks[b]["wsegs"]:
            if isinstance(wap, int):
                sg, so, ss = off // SEG, off % SEG, wap % SEG
                assert wap // SEG == sg
                ws = w[sg]
                P.op("dve", lambda e, ws=ws, kc0=kc0, nkc=nkc, so=so, ss=ss, ncols=ncols: e.tensor_copy(
                    out=ws.t[:, kc0:kc0 + nkc, so:so + ncols], in_=ws.t[:, kc0:kc0 + nkc, ss:ss + ncols]), (ws,), (ws,))
                continue
            c = 0
            while c < ncols:
                o = off + c
                sg, so = o // SEG, o % SEG
                n = min(ncols - c, SEG - so)
                ws = w[sg]
                P.dma("pool", ws.t[:, kc0:kc0 + nkc, so:so + n],
                      wap[:, c:c + n].rearrange("(kc p) n -> p kc n", p=128), (), (ws,), sem=ws)
                c += n

    def load_a(j):
        a = at[j % 2]
        t = seq[j][1]
        for (aap, kc0, nkc) in act_srcs:
            P.dma("sp", a.t[:, kc0:kc0 + nkc, :],
                  aap.rearrange("(kc p) n -> p kc n", p=128)[:, :, t * T:(t + 1) * T], (), (a,), sem=a)

    load_w(0)
    load_a(0)
    for j, (b, t) in enumerate(seq):
        if t == 0:
            if nwbuf == 2 and b + 1 < len(blocks):
                load_w(b + 1)
            if nwbuf == 1 and b > 0:
                load_w(b)
        if j + 1 < len(seq):
            load_a(j + 1)
        w = wt[b % nwbuf]
        a = at[j % 2]
        for grp in blocks[b]["groups"]:
            accs = grp["accs"]
            if accs[0][0] == "fm":
                pts = []
                for (kind, off, ncols, klo, khi) in accs:
                    p = pss[psi[0] % npsum]
                    psi[0] += 1
                    ws, so = w[off // SEG], off % SEG
                    for kc in range(klo, khi):
                        P.op("pe", lambda e, p=p, ws=ws, a=a, kc=kc, so=so, ncols=ncols, klo=klo, khi=khi:
                             e.matmul(p.t[0:ncols, 0:T], lhsT=ws.t[:, kc, so:so + ncols], rhs=a.t[:, kc, :],
                                      start=(kc == klo), stop=(kc == khi - 1)), (ws, a), (p,))
                    pts.append(p)
                grp["epi"](P, ctx, t * T, T, pts, grp)
            else:
                (kind, off, ncols, klo, khi) = accs[0]
                ws, so = w[off // SEG], off % SEG
                for ts in range(T // 128):
                    p = pss[psi[0] % npsum]
                    psi[0] += 1
                    for kc in range(klo, khi):
                        P.op("pe", lambda e, p=p, ws=ws, a=a, kc=kc, so=so, ncols=ncols, klo=klo, khi=khi, ts=ts:
                             e.matmul(p.t[:, 0:ncols], lhsT=a.t[:, kc, ts * 128:(ts + 1) * 128],
                                      rhs=ws.t[:, kc, so:so + ncols],
                                      start=(kc == klo), stop=(kc == khi - 1)), (ws, a), (p,))
                    grp["epi"](P, ctx, t * T + ts * 128, 128, [p], grp)
    P.end()


class Stager:
    def __init__(self, P, shape, dtype, n=4):
        self.tl = [P.sb(shape, dtype) for _ in range(n)]
        self.i = 0

    def get(self):
        t = self.tl[self.i % len(self.tl)]
        self.i += 1
        return t


def epi_store_fm(dst, row_of):
    def epi(P, ctx, tok0, ntok, pts, grp):
        p = pts[0]
        ncols = grp["accs"][0][2]
        s = ctx["stg"].get()
        copy_op(P, P.evac_eng(), s.t[0:ncols, 0:ntok], p.t[0:ncols, 0:ntok], (p,), (s,))
        r0 = grp["row0"]
        P.dma("sp", dst[r0:r0 + ncols, tok0:tok0 + ntok], s.t[0:ncols, 0:ntok], (s,), (), sem=s)
    return epi


def epi_store_fm32(dst):
    def epi(P, ctx, tok0, ntok, pts, grp):
        p = pts[0]
        ncols = grp["accs"][0][2]
        s = ctx["stg32"].get()
        copy_op(P, P.evac_eng(), s.t[0:ncols, 0:ntok], p.t[0:ncols, 0:ntok], (p,), (s,))
        r0 = grp["row0"]
        P.dma("sp", dst[r0:r0 + ncols, tok0:tok0 + ntok], s.t[0:ncols, 0:ntok], (s,), (), sem=s)
    return epi


def epi_store_tm(dst):
    def epi(P, ctx, tok0, ntok, pts, grp):
        p = pts[0]
        ncols = grp["accs"][0][2]
        s = ctx["stg"].get()
        copy_op(P, P.evac_eng(), s.t[:, 0:ncols], p.t[:, 0:ncols], (p,), (s,))
        c0 = grp["col0"]
        P.dma("sp", dst[tok0:tok0 + 128, c0:c0 + ncols], s.t[:, 0:ncols], (s,), (), sem=s)
    return epi


def setup_stg(dtype=BF16, n=4):
    def setup(P):
        return {"stg": Stager(P, [128, 512], dtype, n)}
    return setup


def epi_gate(dst):
    def epi(P, ctx, tok0, ntok, pts, grp):
        p = pts[0]
        s = ctx["stg"].get()
        bcol = grp["bcol"]
        bg = ctx["bg"]
        P.op("act", lambda e: e.activation(out=s.t[:, 0:ntok], in_=p.t[:, 0:ntok], func=AF.Sigmoid,
                                           bias=bg.t[:, bcol:bcol + 1]), (p, bg), (s,))
        r0 = grp["row0"]
        P.dma("sp", dst[r0:r0 + 128, tok0:tok0 + ntok], s.t[:, 0:ntok], (s,), (), sem=s)
    return epi


def epi_rope(dst, cos_d, sin_d):
    def epi(P, ctx, tok0, ntok, pts, grp):
        pa, pb = pts
        k = ctx["cs_i"]
        ctx["cs_i"] += 1
        ct, sn = ctx["cos"][k % 2], ctx["sin"][k % 2]
        P.dma("sp", ct.t[0:64, 0:ntok], cos_d[:, tok0:tok0 + ntok], (), (ct,), sem=ct)
        P.dma("sp", sn.t[0:64, 0:ntok], sin_d[:, tok0:tok0 + ntok], (), (sn,), sem=sn)
        t1, t2 = ctx["rt1"][k % 2], ctx["rt2"][k % 2]
        P.op("dve", lambda e: e.tensor_tensor(out=t1.t[0:64, 0:ntok], in0=pa.t[0:64, 0:ntok],
                                              in1=ct.t[0:64, 0:ntok], op=ALU.mult), (pa, ct), (t1,))
        P.op("dve", lambda e: e.tensor_tensor(out=t2.t[0:64, 0:ntok], in0=pb.t[0:64, 0:ntok],
                                              in1=sn.t[0:64, 0:ntok], op=ALU.mult), (pb, sn), (t2,))
        s = ctx["stg"].get()
        P.op("dve", lambda e: e.tensor_tensor(out=s.t[0:64, 0:ntok], in0=t1.t[0:64, 0:ntok],
                                               in1=t2.t[0:64, 0:ntok], op=ALU.add), (t1, t2), (s,))
        r0 = grp["row0"]
        P.dma("sp", dst[r0:r0 + 64, tok0:tok0 + ntok], s.t[0:64, 0:ntok], (s,), (), sem=s)
    return epi


def setup_rope(extra=None):
    def setup(P):
        ctx = {"stg": Stager(P, [128, 512], BF16, 4), "stg32": Stager(P, [128, 512], F32, 4), "cs_i": 0,
               "cos": [P.sb([128, 512], F32) for _ in range(2)],
               "sin": [P.sb([128, 512], F32) for _ in range(2)],
               "rt1": [P.sb([128, 512], F32) for _ in range(2)],
               "rt2": [P.sb([128, 512], F32) for _ in range(2)]}
        if extra is not None:
            extra(P, ctx)
        return ctx
    return setup


def epi_resid(dst, resid):
    def epi(P, ctx, tok0, ntok, pts, grp):
        p = pts[0]
        r0 = grp["row0"]
        k = ctx["r_i"]
        ctx["r_i"] += 1
        rt = ctx["rt"][k % 3]
        P.dma("sp", rt.t[:, 0:ntok], resid[r0:r0 + 128, tok0:tok0 + ntok], (), (rt,), sem=rt)
        s = ctx["stg"].get()
        P.op("dve", lambda e: e.tensor_tensor(out=s.t[:, 0:ntok], in0=p.t[:, 0:ntok], in1=rt.t[:, 0:ntok],
                                              op=ALU.add), (p, rt), (s,))
        P.dma("sp", dst[r0:r0 + 128, tok0:tok0 + ntok], s.t[:, 0:ntok], (s,), (), sem=s)
    return epi


def setup_resid(P):
    return {"stg": Stager(P, [128, 512], F32, 3), "r_i": 0, "rt": [P.sb([128, 512], F32) for _ in range(3)]}


def epi_merge(dst, gT):
    def epi(P, ctx, tok0, ntok, pts, grp):
        pa, pb = pts
        r0 = grp["row0"]
        k = ctx["r_i"]
        ctx["r_i"] += 1
        g0, g1 = ctx["g0"][k % 2], ctx["g1"][k % 2]
        t1, t2 = ctx["t1"][k % 2], ctx["t2"][k % 2]
        P.dma("sp", g0.t[:, 0:ntok], gT[r0:r0 + 128, tok0:tok0 + ntok], (), (g0,), sem=g0)
        P.dma("sp", g1.t[:, 0:ntok], gT[D + r0:D + r0 + 128, tok0:tok0 + ntok], (), (g1,), sem=g1)
        P.op("dve", lambda e: e.tensor_tensor(out=t1.t[:, 0:ntok], in0=pa.t[:, 0:ntok], in1=g0.t[:, 0:ntok],
                                              op=ALU.mult), (pa, g0), (t1,))
        P.op("dve", lambda e: e.tensor_tensor(out=t2.t[:, 0:ntok], in0=pb.t[:, 0:ntok], in1=g1.t[:, 0:ntok],
                                              op=ALU.mult), (pb, g1), (t2,))
        s = ctx["stg"].get()
        P.op("dve", lambda e: e.tensor_tensor(out=s.t[:, 0:ntok], in0=t1.t[:, 0:ntok], in1=t2.t[:, 0:ntok],
                                               op=ALU.add), (t1, t2), (s,))
        P.dma("sp", dst[r0:r0 + 128, tok0:tok0 + ntok], s.t[:, 0:ntok], (s,), (), sem=s)
    return epi


def setup_merge(P):
    return {"stg": Stager(P, [128, 512], BF16, 3), "r_i": 0,
            "g0": [P.sb([128, 512], F32) for _ in range(2)], "g1": [P.sb([128, 512], F32) for _ in range(2)],
            "t1": [P.sb([128, 512], F32) for _ in range(2)], "t2": [P.sb([128, 512], F32) for _ in range(2)]}


def epi_swiglu(dst):
    def epi(P, ctx, tok0, ntok, pts, grp):
        pg, pu = pts
        r0 = grp["row0"]
        k = ctx["r_i"]
        ctx["r_i"] += 1
        t1 = ctx["t1"][k % 3]
        P.op("act", lambda e: e.activation(out=t1.t[:, 0:ntok], in_=pg.t[:, 0:ntok], func=AF.Silu), (pg,), (t1,))
        s = ctx["stg"].get()
        P.op("dve", lambda e: e.tensor_tensor(out=s.t[:, 0:ntok], in0=pu.t[:, 0:ntok], in1=t1.t[:, 0:ntok],
                                              op=ALU.mult), (pu, t1), (s,))
        P.dma("sp", dst[r0:r0 + 128, tok0:tok0 + ntok], s.t[:, 0:ntok], (s,), (), sem=s)
    return epi


def setup_swiglu(P):
    return {"stg": Stager(P, [128, 512], BF16, 3), "r_i": 0, "t1": [P.sb([128, 512], F32) for _ in range(3)]}


def latent_blocks(w_in, off, nlat, rawdst, rope=None):
    ncl = nlat * 128
    wsegs = [(w_in[:, off:off + ncl], 0, 16, 0, ncl)]
    groups = [dict(accs=[("fm", m * 128, 128, 0, 16)], row0=m * 128, epi=epi_store_fm32(rawdst)) for m in range(nlat)]
    CB = ncl
    if rope is not None:
        (rdst, cos_d, sin_d, roff) = rope
        wsegs.append((w_in[:, roff:roff + 64], 0, 16, ncl, 64))
        wsegs.append((ncl + 32, 0, 16, ncl + 64, 32))
        wsegs.append((ncl, 0, 16, ncl + 96, 32))
        groups.append(dict(accs=[("fm", ncl, 64, 0, 16), ("fm", ncl + 64, 64, 0, 16)], row0=0,
                           epi=epi_rope(rdst, cos_d, sin_d)))
        CB = ncl + 128
    return [dict(wsegs=wsegs, groups=groups)], setup_rope(), CB


def attn_phase(P, S, So, units, consts, kind):
    nq = So // 256
    nkt_all = S // 128
    dvv = 128 if kind == "mla" else 256
    P.begin()
    if kind == "mla":
        kn = [P.sb([128, S], BF16) for _ in range(2)]
        qn = [P.sb([128, So], BF16) for _ in range(2)]
    else:
        kn = [P.sb([128, 2, S], BF16) for _ in range(2)]
        qn = [P.sb([128, 2, So], BF16) for _ in range(2)]
    vt = [P.sb([128, nkt_all, 257], BF16) for _ in range(2)]
    ident = P.sb([128, 128], BF16)
    P.dma("pool", ident.t[:], consts["ident"], (), (ident,), sem=ident)
    for v in vt:
        if kind == "mla":
            P.op("pool", lambda e, v=v: e.memset(v.t[:, :, 128:257], 0.0), (), (v,))
        P.op("pool", lambda e, v=v: e.memset(v.t[:, :, dvv:dvv + 1], 1.0), (), (v,))
    nS = 5 if kind == "mla" else 3
    LA = nS - 2
    NPT = LA + 3
    pt = [P.sb([128, 512], BF16) for _ in range(NPT)]
    sbank = [P.ps() for _ in range(nS)]
    tpb = P.ps([128, 1024], BF16)
    tpr = [Res(), Res()]
    rsm = [P.sb([128, 1], F32) for _ in range(4)]
    if kind == "mla":
        kr = P.sb([128, S], BF16)
        qr = [P.sb([128, So], BF16) for _ in range(2)]
        P.op("pool", lambda e: e.memset(kr.t[64:128, :], 0.0), (), (kr,))
        for q_ in qr:
            P.op("pool", lambda e, q_=q_: e.memset(q_.t[64:128, :], 0.0), (), (q_,))
        P.dma("sp", kr.t[0:64, :], consts["krT"], (), (kr,), sem=kr)
        msk = P.sb([128, 2048], F32)
        P.dma("sp", msk.t[:], consts["mask_mla"].rearrange("p a b -> p (a b)"), (), (msk,), sem=msk)
        obank = [P.ps() for _ in range(2)]
        ybf = [P.sb([128, 128], BF16) for _ in range(4)]
        ystg = [P.sb([128, 256], BF16) for _ in range(2)]
    else:
        mskd = [P.sb([128, 2048], F32) for _ in range(2)]
        abias = P.sb([128, NH_D * 80], F32)
        P.dma("sp", abias.t[:], consts["abias"], (), (abias,), sem=abias)
        obank = [P.ps() for _ in range(4)]
        o1n = [P.sb([128, 256], F32) for _ in range(2)]
        ofull = [P.sb([128, 256], F32) for _ in range(2)]
        junk = P.sb([128, 256], F32)
        ybf = [P.sb([128, 256], BF16) for _ in range(2)]
        ystg = [P.sb([128, 2, 256], BF16) for _ in range(2)]
        r2l = [P.sb([128, 1], F32) for _ in range(2)]
        ssq = [P.sb([128, 1], F32) for _ in range(2)]
        lnv = [P.sb([128, 1], F32) for _ in range(2)]
        rstd = [P.sb([128, 1], F32) for _ in range(2)]
        epsc = P.sb([128, 1], F32)
        P.op("dve", lambda e: e.memset(epsc.t[:], EPS), (), (epsc,))
        gsub = P.sb([128, 256], F32)
        P.dma("sp", gsub.t[:], consts["gsub"], (), (gsub,), sem=gsub)
        P.op("act", lambda e: e.activation(out=gsub.t[:], in_=gsub.t[:], func=AF.Copy, scale=1.0 - LAMBDA_INIT),
             (gsub,), (gsub,))
        lt = [P.sb([128, 128], F32) for _ in range(4)]
        for i, nm in enumerate(["lq1", "lk1", "lq2", "lk2"]):
            P.dma("sp", lt[i].t[:], consts[nm], (), (lt[i],), sem=lt[i])
        pr = [P.sb([128, 128], F32) for _ in range(2)]
        sm = [P.sb([128, 1], F32) for _ in range(2)]
        ex = [P.sb([128, 1], F32) for _ in range(2)]
        neglam = P.sb([128, 1], F32)
        for i in range(2):
            P.op("dve", lambda e, i=i: e.tensor_tensor(out=pr[i].t[:], in0=lt[2 * i].t[:], in1=lt[2 * i + 1].t[:],
                                                       op=ALU.mult), (lt[2 * i], lt[2 * i + 1]), (pr[i],))
            P.op("act", lambda e, i=i: e.activation(out=junk.t[:, 0:128], in_=pr[i].t[:], func=AF.Copy,
                                                    accum_out=sm[i].t[:]), (pr[i],), (junk, sm[i]))
            P.op("act", lambda e, i=i: e.activation(out=ex[i].t[:], in_=sm[i].t[:], func=AF.Exp), (sm[i],), (ex[i],))
        P.op("dve", lambda e: e.tensor_tensor(out=neglam.t[:], in0=ex[1].t[:], in1=ex[0].t[:], op=ALU.subtract),
             (ex[0], ex[1]), (neglam,))
        P.op("dve", lambda e: e.tensor_scalar(out=neglam.t[:], in0=neglam.t[:], scalar1=-LAMBDA_INIT, scalar2=None,
                                              op0=ALU.add), (neglam,), (neglam,))

    vloaded = {}

    def load_unit(u):
        un = units[u]
        b = u % 2
        if kind == "mla":
            P.dma("sp", kn[b].t[:], un["kT"], (), (kn[b],), sem=kn[b])
            P.dma("sp", qn[b].t[:], un["qT"], (), (qn[b],), sem=qn[b])
            P.dma("sp", qr[b].t[0:64, :], un["qrT"], (), (qr[b],), sem=qr[b])
        else:
            for m in range(2):
                P.dma("sp", kn[b].t[:, m, :], un["kT"][m], (), (kn[b],), sem=kn[b])
                P.dma("sp", qn[b].t[:, m, :], un["qT"][m], (), (qn[b],), sem=qn[b])
        vk = un["vkey"]
        if vk not in vloaded:
            vb = len(vloaded) % 2
            vloaded[vk] = vb
            vv = un["v"].rearrange("(t p) c -> p t c", p=128)
            step = 16
            for t0 in range(0, nkt_all, step):
                t1 = min(nkt_all, t0 + step)
                P.dma("sp", vt[vb].t[:, t0:t1, 0:dvv], vv[:, t0:t1, :], (), (vt[vb],), sem=vt[vb])
            if kind == "diff":
                P.dma("sp", mskd[vb].t[:], consts["mask_diff"][un["h"]].rearrange("p a b -> p (a b)"), (),
                      (mskd[vb],), sem=mskd[vb])

    pairs = []
    for u in range(len(units)):
        for i in range(nq):
            if kind == "mla":
                ks = list(range(4 * (i + 1)))
            else:
                slope = 2.0 ** (-(units[u]["h"] + 1))
                ks = [kt for kt in range(8 * (i + 1)) if slope * ((8 * i - kt) * 128 - 127) < 140.0]
            for n_, k_ in enumerate(ks):
                pairs.append((u, i, k_, n_ == 0, n_ == len(ks) - 1))
    cnt = {"s": 0, "p": 0, "e": 0}
    pend = {}

    def do_qk(n):
        (u, i, kx, first, last) = pairs[n]
        un = units[u]
        b = u % 2
        sp_ = sbank[cnt["s"] % nS]
        cnt["s"] += 1
        q0 = i * 256
        p = pt[cnt["p"] % NPT]
        cnt["p"] += 1
        sc = un["scale"]
        if kind == "mla":
            nk2 = 4 * (i + 1)
            for j in range(2):
                k0 = (2 * kx + j) * 128
                P.op("pe", lambda e, j=j, k0=k0: e.matmul(
                    sp_.t[:, j * 256:(j + 1) * 256], lhsT=kn[b].t[:, k0:k0 + 128], rhs=qn[b].t[:, q0:q0 + 256],
                    start=True, stop=False), (kn[b], qn[b]), (sp_,))
                P.op("pe", lambda e, j=j, k0=k0: e.matmul(
                    sp_.t[:, j * 256:(j + 1) * 256], lhsT=kr.t[:, k0:k0 + 128], rhs=qr[b].t[:, q0:q0 + 256],
                    start=False, stop=True), (kr, qr[b]), (sp_,))
            P.op("act", lambda e: e.activation(out=p.t[:], in_=sp_.t[:, 0:512], func=AF.Exp, scale=sc), (sp_,), (p,))
            if kx >= nk2 - 4:
                kk = 2 * (kx - (nk2 - 4))
                P.op("dve", lambda e: e.tensor_tensor(out=p.t[:], in0=p.t[:], in1=msk.t[:, kk * 256:(kk + 2) * 256],
                                                      op=ALU.mult), (p, msk), (p,))
        else:
            h = un["h"]
            k0 = kx * 128
            for m in range(2):
                P.op("pe", lambda e, m=m: e.matmul(
                    sp_.t[:, m * 256:(m + 1) * 256], lhsT=kn[b].t[:, m, k0:k0 + 128], rhs=qn[b].t[:, m, q0:q0 + 256],
                    start=True, stop=True), (kn[b], qn[b]), (sp_,))
            dst = kx - 8 * i
            if h == 0:
                for jq in range(2):
                    bi = h * 80 + dst - 2 * jq + 66
                    c0 = jq * 128
                    P.op("act", lambda e, bi=bi, c0=c0: e.activation(
                        out=p.t[:, 0:512].rearrange("p (m q) -> p m q", m=2)[:, :, c0:c0 + 128],
                        in_=sp_.t[:, 0:512].rearrange("p (m q) -> p m q", m=2)[:, :, c0:c0 + 128], func=AF.Exp,
                        scale=sc, bias=abias.t[:, bi:bi + 1]), (sp_, abias), (p,))
            else:
                bi = h * 80 + dst + 66
                P.op("act", lambda e, bi=bi: e.activation(out=p.t[:], in_=sp_.t[:, 0:512], func=AF.Exp, scale=sc,
                                                          bias=abias.t[:, bi:bi + 1]), (sp_, abias), (p,))
            if kx >= 8 * i:
                kk = kx - 8 * i
                mk_ = mskd[vloaded[un["vkey"]]]
                for m in range(2):
                    P.op("dve", lambda e, m=m: e.tensor_tensor(
                        out=p.t[:, m * 256:(m + 1) * 256], in0=p.t[:, m * 256:(m + 1) * 256],
                        in1=mk_.t[:, kk * 256:(kk + 1) * 256], op=ALU.mult), (p, mk_), (p,))
        pend[n] = p

    def do_pv(n):
        (u, i, kx, first, last) = pairs[n]
        un = units[u]
        p = pend.pop(n)
        if i == 0 and first and u + 1 < len(units):
            load_unit(u + 1)
        v = vt[vloaded[un["vkey"]]]
        ei = u * nq + i
        h = un["h"]
        if kind == "mla":
            for j in range(2):
                for qs in range(2):
                    P.op("pe", lambda e, qs=qs, j=j: e.matmul(
                        obank[qs].t[:, 0:129], lhsT=p.t[:, j * 256 + qs * 128:j * 256 + (qs + 1) * 128],
                        rhs=v.t[:, 2 * kx + j, 0:129], start=(first and j == 0), stop=(last and j == 1)),
                        (p, v), (obank[qs],))
            if last:
                ys = ystg[ei % 2]
                tr = tpr[ei % 2]
                for qs in range(2):
                    ob = obank[qs]
                    r = rsm[cnt["e"] % 4]
                    yb = ybf[cnt["e"] % 4]
                    cnt["e"] += 1
                    P.op("dve", lambda e, r=r, ob=ob: e.reciprocal(out=r.t[:], in_=ob.t[:, 128:129]), (ob,), (r,))
                    P.op("dve", lambda e, r=r, yb=yb, ob=ob: e.tensor_scalar(
                        out=yb.t[:], in0=ob.t[:, 0:128], scalar1=r.t[:], scalar2=None, op0=ALU.mult), (ob, r), (yb,))
                    c0 = (ei % 2) * 256 + qs * 128
                    P.op("pe", lambda e, yb=yb, c0=c0: e.transpose(tpb.t[:, c0:c0 + 128], yb.t[:], ident.t[:]),
                         (yb, ident), (_R(tr),))
                c0 = (ei % 2) * 256
                copy_op(P, "dve", ys.t[:], tpb.t[:, c0:c0 + 256], (_R(tr),), (ys,))
                P.dma("sp", consts["yT"][h * 128:(h + 1) * 128, i * 256:(i + 1) * 256], ys.t[:], (ys,), (), sem=ys)
        else:
            for m in range(2):
                for qs in range(2):
                    ob = obank[m * 2 + qs]
                    P.op("pe", lambda e, qs=qs, m=m, ob=ob: e.matmul(
                        ob.t[:, 0:257], lhsT=p.t[:, m * 256 + qs * 128:m * 256 + (qs + 1) * 128],
                        rhs=v.t[:, kx, :], start=first, stop=last), (p, v), (ob,))
            if last:
                ys = ystg[ei % 2]
                tr = tpr[ei % 2]
                for qs in range(2):
                    o1b, o2b = obank[qs], obank[2 + qs]
                    r1 = rsm[cnt["e"] % 4]
                    r2 = rsm[(cnt["e"] + 1) % 4]
                    cnt["e"] += 2
                    k2 = qs
                    t1, rl, of, sq_, ln_, rs_, yb = o1n[k2], r2l[k2], ofull[k2], ssq[k2], lnv[k2], rstd[k2], ybf[k2]
                    P.op("dve", lambda e, r1=r1, o1b=o1b: e.reciprocal(out=r1.t[:], in_=o1b.t[:, 256:257]), (o1b,), (r1,))
                    P.op("dve", lambda e, r1=r1, o1b=o1b, t1=t1: e.tensor_scalar(
                        out=t1.t[:], in0=o1b.t[:, 0:256], scalar1=r1.t[:], scalar2=None, op0=ALU.mult),
                        (o1b, r1), (t1,))
                    P.op("dve", lambda e, r2=r2, o2b=o2b: e.reciprocal(out=r2.t[:], in_=o2b.t[:, 256:257]), (o2b,), (r2,))
                    P.op("dve", lambda e, r2=r2, rl=rl: e.tensor_tensor(out=rl.t[:], in0=r2.t[:], in1=neglam.t[:],
                                                                        op=ALU.mult), (r2, neglam), (rl,))
                    P.op("dve", lambda e, rl=rl, of=of, o2b=o2b, t1=t1: e.scalar_tensor_tensor(
                        out=of.t[:], in0=o2b.t[:, 0:256], scalar=rl.t[:], in1=t1.t[:],
                        op0=ALU.mult, op1=ALU.add), (o2b, rl, t1), (of,))
                    P.op("act", lambda e, of=of, sq_=sq_: e.activation(out=junk.t[:], in_=of.t[:], func=AF.Square,
                                                                       accum_out=sq_.t[:]), (of,), (junk, sq_))
                    P.op("act", lambda e, sq_=sq_, ln_=ln_: e.activation(out=ln_.t[:], in_=sq_.t[:], func=AF.Ln,
                                                                         scale=1.0 / 256, bias=epsc.t[:]),
                         (sq_, epsc), (ln_,))
                    P.op("act", lambda e, ln_=ln_, rs_=rs_: e.activation(out=rs_.t[:], in_=ln_.t[:], func=AF.Exp,
                                                                         scale=-0.5), (ln_,), (rs_,))
                    P.op("dve", lambda e, yb=yb, of=of, rs_=rs_: e.scalar_tensor_tensor(
                        out=yb.t[:], in0=of.t[:], scalar=rs_.t[:], in1=gsub.t[:], op0=ALU.mult, op1=ALU.mult),
                        (of, rs_, gsub), (yb,))
                    for c in range(2):
                        c0 = (ei % 2) * 512 + c * 256 + qs * 128
                        P.op("pe", lambda e, yb=yb, c=c, c0=c0: e.transpose(
                            tpb.t[:, c0:c0 + 128], yb.t[:, c * 128:(c + 1) * 128], ident.t[:]),
                            (yb, ident), (_R(tr),))
                c0 = (ei % 2) * 512
                copy_op(P, "dve", ys.t[:, :, :], tpb.t[:, c0:c0 + 512].rearrange("p (c q) -> p c q", c=2),
                        (_R(tr),), (ys,))
                P.dma("sp", consts["yT"][h * 256:(h + 1) * 256, i * 256:(i + 1) * 256].rearrange(
                    "(c p) q -> p c q", p=128), ys.t[:, :, :], (ys,), (), sem=ys)

    load_unit(0)
    for n in range(len(pairs) + LA):
        if n < len(pairs):
            do_qk(n)
        if n >= LA:
            do_pv(n - LA)
    P.end()


class _R:
    __slots__ = ("res",)

    def __init__(self, res):
        self.res = res


def build_program(S, debug=False, upto=99):
    So = S // 4
    To = min(512, So)
    nc = bass.Bass("TRN2", target_bir_lowering=False)
    P = Prog(nc)

    def din(name, shape, dt=F32):
        return nc.dram_tensor(name, list(shape), dt, kind="ExternalInput").ap()

    def scr(name, shape, dt):
        kind = "ExternalOutput" if debug else "Internal"
        return nc.dram_tensor(name, list(shape), dt, kind=kind).ap()

    xall = din("xall", [D, S])
    xown = din("xown", [D, So])
    w_in = din("w_in", [D, D_IN])
    w_uq = din("w_uq", [QR, NH_M * 192])
    w_ukv = din("w_ukv", [KVR, NH_M * 256])
    w_mla = din("w_mla_proj", [D, D])
    w_dif = din("w_diff_proj", [D, D])
    w_out = din("w_out", [D, D])
    w_fg = din("w_ffn_gate", [D, DFF])
    w_fu = din("w_ffn_up", [D, DFF])
    w_fd = din("w_ffn_down", [DFF, D])
    g_attn = din("g_attn", [128, 16])
    g_q = din("g_q", [128, 6])
    g_kv = din("g_kv", [128, 4])
    g_ffn = din("g_ffn", [128, 16])
    g_fin = din("g_fin", [128, 16])
    b_gate = din("b_gate", [128, 32])
    gsub = din("gsub", [128, 256])
    lq1 = din("lq1", [128, 128])
    lk1 = din("lk1", [128, 128])
    lq2 = din("lq2", [128, 128])
    lk2 = din("lk2", [128, 128])
    cos_all = din("cos_all", [64, S])
    sin_all = din("sin_all", [64, S])
    cos_own = din("cos_own", [64, So])
    sin_own = din("sin_own", [64, So])
    abias = din("abias", [128, NH_D * 80])
    mask_mla = din("mask_mla", [128, 8, 256])
    mask_diff = din("mask_diff", [NH_D, 128, 8, 256])
    ident = din("ident", [128, 128])

    hT_all = scr("hT_all", [D, S], BF16)
    hT_own = scr("hT_own", [D, So], BF16)
    dkT = scr("dkT", [D, S], BF16)
    dv = scr("dv", [S, D], BF16)
    ckvnT = scr("ckvnT", [KVR, S], BF16)
    ckvraw = scr("ckvraw", [KVR, S], F32)
    cqraw = scr("cqraw", [QR, So], F32)
    krT = scr("krT", [64, S], BF16)
    knT = scr("knT", [D, S], BF16)
    vm = scr("vm", [S, D], BF16)
    dqT = scr("dqT", [D, So], BF16)
    gT = scr("gT", [2 * D, So], F32)
    cqnT = scr("cqnT", [QR, So], BF16)
    qnT = scr("qnT", [D, So], BF16)
    qrT = scr("qrT", [NH_M * 64, So], BF16)
    ymT = scr("ymT", [D, So], BF16)
    ydT = scr("ydT", [D, So], BF16)
    mT = scr("mT", [D, So], BF16)
    x1T = scr("x1T", [D, So], F32)
    h2T = scr("h2T", [D, So], BF16)
    aT = scr("aT", [DFF, So], BF16)
    x2T = scr("x2T", [D, So], F32)
    outT = nc.dram_tensor("outT", [D, So], F32, kind="ExternalOutput").ap()

    def done():
        P.top.close()
        return nc, P

    norm_phase(P, xall, hT_all, g_attn, S, BF16)
    norm_phase(P, xown, hT_own, g_attn, So, BF16)
    if upto <= 0:
        return done()

    blocks = [dict(wsegs=[(w_in[:, OFF_DK:OFF_DK + 2048], 0, 16, 0, 2048)],
                   groups=[dict(accs=[("fm", m * 128, 128, 0, 16)], row0=m * 128, epi=epi_store_fm(dkT, None))
                           for m in range(16)]),
              dict(wsegs=[(w_in[:, OFF_DV:OFF_DV + 2048], 0, 16, 0, 2048)],
                   groups=[dict(accs=[("tm", cb * 512, 512, 0, 16)], col0=cb * 512, epi=epi_store_tm(dv))
                           for cb in range(4)])]
    gemm_phase(P, [(hT_all, 0, 16)], 16, S, blocks, CB=2048, setup=setup_stg())
    if upto <= 0.3:
        return done()
    blocks, setup, CB = latent_blocks(w_in, OFF_CKV, 4, ckvraw, rope=(krT, cos_all, sin_all, OFF_KPE))
    gemm_phase(P, [(hT_all, 0, 16)], 16, S, blocks, CB=CB, setup=setup)
    norm_phase(P, ckvraw, ckvnT, g_kv, S, BF16, nfeat=KVR)

    if upto <= 0.6:
        return done()
    groups = []
    for h in range(NH_M):
        groups.append(dict(accs=[("fm", h * 256, 128, 0, 4)], row0=h * 128, epi=epi_store_fm(knT, None)))
        groups.append(dict(accs=[("tm", h * 256 + 128, 128, 0, 4)], col0=h * 128, epi=epi_store_tm(vm)))
    blocks = [dict(wsegs=[(w_ukv[:, :], 0, 4, 0, 4096)], groups=groups)]
    gemm_phase(P, [(ckvnT, 0, 4)], 4, S, blocks, CB=4096, nwbuf=1, setup=setup_stg())
    if upto <= 1:
        return done()

    blocks = [dict(wsegs=[(w_in[:, OFF_DQ:OFF_DQ + 2048], 0, 16, 0, 2048)],
                   groups=[dict(accs=[("fm", m * 128, 128, 0, 16)], row0=m * 128, epi=epi_store_fm(dqT, None))
                           for m in range(16)])]
    gemm_phase(P, [(hT_own, 0, 16)], 16, So, blocks, T=To, CB=2048, nwbuf=1, setup=setup_stg())

    blocks = []
    for cb in range(2):
        c0 = OFF_G + cb * 2048
        blocks.append(dict(
            wsegs=[(w_in[:, c0:c0 + 2048], 0, 16, 0, 2048)],
            groups=[dict(accs=[("fm", m * 128, 128, 0, 16)], row0=cb * 2048 + m * 128, bcol=cb * 16 + m,
                         epi=epi_gate(gT)) for m in range(16)]))

    def setup_gate(P):
        ctx = {"stg": Stager(P, [128, 512], F32, 4), "bg": P.sb([128, 32], F32)}
        P.dma("sp", ctx["bg"].t[:], b_gate, (), (ctx["bg"],), sem=ctx["bg"])
        return ctx
    gemm_phase(P, [(hT_own, 0, 16)], 16, So, blocks, T=To, CB=2048, setup=setup_gate)

    blocks, setup, CB = latent_blocks(w_in, OFF_CQ, 6, cqraw)
    gemm_phase(P, [(hT_own, 0, 16)], 16, So, blocks, T=To, CB=CB, setup=setup)
    norm_phase(P, cqraw, cqnT, g_q, So, BF16, nfeat=QR)

    wsegs, groups = [], []
    for h in range(NH_M):
        o = h * 256
        wsegs.append((w_uq[:, h * 192:h * 192 + 192], 0, 6, o, 192))
        wsegs.append((o + 160, 0, 6, o + 192, 32))
        wsegs.append((o + 128, 0, 6, o + 224, 32))
        groups.append(dict(accs=[("fm", o, 128, 0, 6)], row0=h * 128, epi=epi_store_fm(qnT, None)))
        groups.append(dict(accs=[("fm", o + 128, 64, 0, 6), ("fm", o + 192, 64, 0, 6)], row0=h * 64,
                           epi=epi_rope(qrT, cos_own, sin_own)))
    blocks = [dict(wsegs=wsegs, groups=groups)]
    gemm_phase(P, [(cqnT, 0, 6)], 6, So, blocks, T=To, CB=4096, nwbuf=1, setup=setup_rope())
    if upto <= 2:
        return done()

    consts = dict(ident=ident, krT=krT, mask_mla=mask_mla, yT=ymT)
    units = [dict(kT=knT[h * 128:(h + 1) * 128, :], qT=qnT[h * 128:(h + 1) * 128, :],
                  qrT=qrT[h * 64:(h + 1) * 64, :], v=vm[:, h * 128:(h + 1) * 128], vkey=h, h=h, m=0,
                  scale=1.0 / math.sqrt(192.0)) for h in range(NH_M)]
    attn_phase(P, S, So, units, consts, "mla")
    if upto <= 3:
        return done()

    consts = dict(ident=ident, abias=abias, mask_diff=mask_diff, yT=ydT, gsub=gsub, lq1=lq1, lk1=lk1, lq2=lq2, lk2=lk2)
    units = []
    for h in range(NH_D):
        rows = [(h * 2 + m) * 128 for m in range(2)]
        units.append(dict(kT=[dkT[r:r + 128, :] for r in rows], qT=[dqT[r:r + 128, :] for r in rows],
                          v=dv[:, h * 256:(h + 1) * 256], vkey=h, h=h, scale=1.0 / math.sqrt(128.0)))
    attn_phase(P, S, So, units, consts, "diff")
    if upto <= 4:
        return done()

    blocks = []
    for cb in range(2):
        c0 = cb * 1024
        blocks.append(dict(
            wsegs=[(w_mla[:, c0:c0 + 1024], 0, 16, 0, 1024), (w_dif[:, c0:c0 + 1024], 16, 16, 0, 1024)],
            groups=[dict(accs=[("fm", m * 128, 128, 0, 16), ("fm", m * 128, 128, 16, 32)], row0=c0 + m * 128,
                         epi=epi_merge(mT, gT)) for m in range(8)]))
    gemm_phase(P, [(ymT, 0, 16), (ydT, 16, 16)], 32, So, blocks, T=To, CB=1024, nwbuf=1, setup=setup_merge)

    blocks = [dict(wsegs=[(w_out[:, :], 0, 16, 0, 2048)],
                   groups=[dict(accs=[("fm", m * 128, 128, 0, 16)], row0=m * 128, epi=epi_resid(x1T, xown))
                           for m in range(16)])]
    gemm_phase(P, [(mT, 0, 16)], 16, So, blocks, T=To, CB=2048, nwbuf=1, setup=setup_resid)
    if upto <= 5:
        return done()

    norm_phase(P, x1T, h2T, g_ffn, So, BF16)
    blocks = []
    for c0 in range(0, DFF, 1024):
        n = min(1024, DFF - c0)
        blocks.append(dict(
            wsegs=[(w_fg[:, c0:c0 + n], 0, 16, 0, n), (w_fu[:, c0:c0 + n], 0, 16, 1024, n)],
            groups=[dict(accs=[("fm", m * 128, 128, 0, 16), ("fm", 1024 + m * 128, 128, 0, 16)], row0=c0 + m * 128,
                         epi=epi_swiglu(aT)) for m in range(n // 128)]))
    gemm_phase(P, [(h2T, 0, 16)], 16, So, blocks, T=To, CB=2048, setup=setup_swiglu)
    blocks = []
    for cb in range(2):
        c0 = cb * 1024
        blocks.append(dict(
            wsegs=[(w_fd[:, c0:c0 + 1024], 0, 44, 0, 1024)],
            groups=[dict(accs=[("fm", m * 128, 128, 0, 44)], row0=c0 + m * 128, epi=epi_resid(x2T, x1T))
                    for m in range(8)]))
    gemm_phase(P, [(aT, 0, 44)], 44, So, blocks, T=To, CB=1024, nwbuf=1, setup=setup_resid)

    norm_phase(P, x2T, outT, g_fin, So, F32)
    return done()


def _col(v, n):
    return np.ascontiguousarray(np.asarray(v, np.float32).reshape(n, 128).T)


def _rep(v):
    v = np.asarray(v, np.float32).reshape(1, -1)
    return np.ascontiguousarray(np.broadcast_to(v, (128, v.shape[1])))


def position_tables(S):
    inv = (10000.0 ** (-np.arange(0, 64, 2, dtype=np.float32) / np.float32(64))).astype(np.float32)
    ang = np.arange(S, dtype=np.float32)[:, None] * inv[None, :]
    c, s = np.cos(ang).astype(np.float32).T, np.sin(ang).astype(np.float32).T
    cos2 = np.ascontiguousarray(np.concatenate([c, c], 0))
    sin2 = np.ascontiguousarray(np.concatenate([-s, s], 0))
    slopes = np.exp2(-8.0 * np.arange(1, NH_D + 1, dtype=np.float32) / NH_D).astype(np.float64)
    per_core = []
    p = np.arange(128)[:, None, None]
    for c_ in range(4):
        ab = np.zeros((128, NH_D, 80), np.float32)
        for h in range(NH_D):
            clampv = slopes[h] * (127 if h == 0 else 255)
            di = np.arange(80)[None, :]
            val = slopes[h] * ((di - 66 - 2 * c_) * 128 + np.arange(128)[:, None])
            ab[:, h, :] = np.minimum(val, clampv)
        kk = np.arange(8)[None, :, None]
        col = np.arange(256)[None, None, :]
        kr = kk * 128 + p
        qr = c_ * 256 + col
        allowed = (kr // 64) <= (qr // 64)
        mm = allowed.astype(np.float32)
        md = np.zeros((NH_D, 128, 8, 256), np.float32)
        for h in range(NH_D):
            corr = np.where(kr > qr, np.exp(-2.0 * slopes[h] * np.maximum(kr - qr, 0)), 1.0)
            md[h] = (allowed * corr).astype(np.float32)
        per_core.append(dict(abias=np.ascontiguousarray(ab.reshape(128, NH_D * 80)), mask_mla=np.ascontiguousarray(mm),
                             mask_diff=md))
    return cos2, sin2, per_core


_CACHE = {}


def kernel(x, attn_norm_g, w_in, b_gate, q_norm_g, w_uq, kv_norm_g, w_ukv,
           lambda_q1, lambda_k1, lambda_q2, lambda_k2, diff_norm_g,
           w_mla_proj, w_diff_proj, w_out, ffn_norm_g, w_ffn_gate, w_ffn_up,
           w_ffn_down, final_norm_g, _debug=False, _upto=99):
    x = np.asarray(x, np.float32)
    B, S, _ = x.shape
    So = S // 4
    nq = So // 256
    key = (S, _debug, _upto)
    if key not in _CACHE:
        _CACHE[key] = build_program(S, debug=_debug, upto=_upto)
    nc, P = _CACHE[key]
    cos2, sin2, per_core = position_tables(S)
    f = lambda a: np.ascontiguousarray(np.asarray(a, np.float32))
    shared = dict(
        w_in=f(w_in[0]), w_uq=f(w_uq[0]), w_ukv=f(w_ukv[0]), w_mla_proj=f(w_mla_proj[0]),
        w_diff_proj=f(w_diff_proj[0]), w_out=f(w_out[0]), w_ffn_gate=f(w_ffn_gate[0]), w_ffn_up=f(w_ffn_up[0]),
        w_ffn_down=f(w_ffn_down[0]),
        g_attn=_col(attn_norm_g[0], 16), g_q=_col(q_norm_g[0], 6), g_kv=_col(kv_norm_g[0], 4),
        g_ffn=_col(ffn_norm_g[0], 16), g_fin=_col(final_norm_g, 16), b_gate=_col(b_gate[0], 32),
        gsub=_rep(diff_norm_g[0]), lq1=_rep(lambda_q1[0]), lk1=_rep(lambda_k1[0]), lq2=_rep(lambda_q2[0]),
        lk2=_rep(lambda_k2[0]), cos_all=cos2, sin_all=sin2, ident=np.eye(128, dtype=np.float32))
    in_maps = []
    own_idx = []
    for core in range(8):
        b, c = core // 4, core % 4
        idx = np.concatenate([np.arange((4 * i + c) * 256, (4 * i + c + 1) * 256) for i in range(nq)])
        own_idx.append(idx)
        xT = np.ascontiguousarray(x[b].T)
        m = dict(shared)
        m.update(xall=xT, xown=np.ascontiguousarray(xT[:, idx]),
                 cos_own=np.ascontiguousarray(cos2[:, idx]), sin_own=np.ascontiguousarray(sin2[:, idx]),
                 abias=per_core[c]["abias"], mask_mla=per_core[c]["mask_mla"], mask_diff=per_core[c]["mask_diff"])
        in_maps.append(m)
    res = run_bass_kernel_spmd(nc, in_maps, core_ids=list(range(8)))
    out = np.empty((B, S, D), np.float32)
    for core in range(8):
        b = core // 4
        out[b, own_idx[core], :] = np.asarray(res.results[core]["outT"], np.float32).T
    if _debug:
        return out, res.results
    return out
```

```python
import math
from contextlib import ExitStack

import numpy as np
import concourse.bass as bass
import concourse.mybir as mybir
from concourse.bass_utils import run_bass_kernel_spmd

F32 = mybir.dt.float32
BF16 = mybir.dt.bfloat16
AF = mybir.ActivationFunctionType
ALU = mybir.AluOpType

D = 2048
NH_M = 16
NH_D = 8
QR = 768
KVR = 512
DFF = 5632
EPS = 1e-6
OFF_CQ, OFF_CKV, OFF_KPE, OFF_DQ, OFF_DK, OFF_DV, OFF_G = 0, 768, 1280, 1344, 3392, 5440, 7488
D_IN = 11584
LAMBDA_INIT = 0.8 - 0.6 * math.exp(-0.3 * 0)
ENGS = ["pe", "act", "dve", "pool", "sp"]
NDELTA = 66


class Res:
    __slots__ = ("w", "r", "dsem")

    def __init__(self):
        self.w = {}
        self.r = {}
        self.dsem = None


class Tl:
    __slots__ = ("t", "res")

    def __init__(self, t):
        self.t = t
        self.res = Res()


class Prog:
    def __init__(self, nc, n_dma_sems=80):
        self.nc = nc
        self.top = ExitStack()
        self.semobj = {}
        for e in ENGS:
            self.semobj["e:" + e] = self.top.enter_context(nc.semaphore("sem_" + e))
        self.cnt = {e: 0 for e in ENGS}
        self.dkeys = []
        self.dval = {}
        for i in range(n_dma_sems):
            k = "d:%d" % i
            self.semobj[k] = self.top.enter_context(nc.semaphore("semd_%d" % i))
            self.dkeys.append(k)
            self.dval[k] = 0
        self.known = {e: {} for e in ENGS}
        self.n_inst = 0
        self.uid = 0

    def begin(self):
        self.stack = ExitStack()
        self.ops = {e: [] for e in ENGS}
        self.free_d = list(self.dkeys)
        self.used_d = []
        self.rr = 0

    def sb(self, shape, dtype):
        self.uid += 1
        return Tl(self.stack.enter_context(self.nc.sbuf_tensor("sb%d" % self.uid, list(shape), dtype)))

    def ps(self, shape=(128, 512), dtype=F32):
        self.uid += 1
        return Tl(self.stack.enter_context(self.nc.psum_tensor("ps%d" % self.uid, list(shape), dtype)))

    def _need(self, eng, toks, waits):
        kn = self.known[eng]
        for k, v in toks.items():
            if eng == "pe" and k == "e:pe":
                continue
            if kn.get(k, 0) >= v:
                continue
            if waits.get(k, 0) < v:
                waits[k] = v

    def _deps(self, eng, reads, writes):
        waits = {}
        for r in reads:
            self._need(eng, r.res.w, waits)
        for w in writes:
            self._need(eng, w.res.w, waits)
            self._need(eng, w.res.r, waits)
        kn = self.known[eng]
        for k, v in waits.items():
            kn[k] = v
        return list(waits.items())

    def op(self, eng, fn, reads=(), writes=()):
        waits = self._deps(eng, reads, writes)
        self.cnt[eng] += 1
        key = "e:" + eng
        val = self.cnt[eng]
        for r in reads:
            r.res.r[key] = val
        for w in writes:
            w.res.w[key] = val
        self.ops[eng].append((waits, fn, (key, 1)))

    def dma(self, eng, out, in_, reads=(), writes=(), sem=None):
        waits = self._deps(eng, reads, writes)
        r = sem.res
        if r.dsem is None:
            r.dsem = self.free_d.pop() if eng != "pool" else self.free_d.pop(0)
            self.used_d.append(r.dsem)
        key = r.dsem
        self.dval[key] += 16
        val = self.dval[key]
        for x in reads:
            x.res.r[key] = val
        for x in writes:
            x.res.w[key] = val
        self.ops[eng].append((waits, lambda e: e.dma_start(out=out, in_=in_), (key, 16)))

    def end(self):
        for e in ENGS:
            waits = []
            for x in ENGS:
                k = "e:" + x
                if self.cnt[x] > self.known[e].get(k, 0):
                    waits.append((k, self.cnt[x]))
                    self.known[e][k] = self.cnt[x]
            if e == "sp":
                for k in self.used_d:
                    if self.dval[k] > self.known[e].get(k, 0):
                        waits.append((k, self.dval[k]))
            if waits:
                self.ops[e].append((waits, None, None))
        for e in ENGS:
            for k in self.used_d:
                self.known[e][k] = self.dval[k]
        ops = self.ops
        semobj = self.semobj

        def mk(name):
            def body(e):
                for waits, fn, inc in ops[name]:
                    for k, v in waits:
                        e.wait_ge(semobj[k], v)
                    if fn is not None:
                        ins = fn(e)
                        if inc is not None:
                            ins.then_inc(semobj[inc[0]], inc[1])
            return body

        for name in ENGS:
            self.n_inst += len(ops[name])
        with self.nc.Block() as block:
            block.tensor(mk("pe"))
            block.scalar(mk("act"))
            block.vector(mk("dve"))
            block.gpsimd(mk("pool"))
            block.sync(mk("sp"))
        self.stack.close()
        self.ops = None

    def evac_eng(self):
        self.rr += 1
        return "act" if (self.rr & 1) else "dve"


def copy_op(P, eng, out_ap, in_ap, reads, writes):
    if eng == "act":
        P.op("act", lambda e: e.activation(out=out_ap, in_=in_ap, func=AF.Copy), reads, writes)
    else:
        P.op("dve", lambda e: e.tensor_copy(out=out_ap, in_=in_ap), reads, writes)


def norm_phase(P, src, dst, gcol_dram, N, out_dtype, nfeat=D, T=256):
    KC = nfeat // 128
    P.begin()
    xt = [P.sb([128, KC, T], F32) for _ in range(2)]
    sq = [P.sb([128, KC, T], BF16) for _ in range(2)]
    st = [P.sb([128, KC, T], out_dtype) for _ in range(2)]
    ln = [P.sb([128, T], F32) for _ in range(2)]
    rs = [P.sb([128, T], F32) for _ in range(2)]
    ones = P.sb([128, 128], BF16)
    g = P.sb([128, KC], F32)
    epsc = P.sb([128, 1], F32)
    ps = [P.ps() for _ in range(2)]
    P.op("dve", lambda e: e.memset(ones.t[:], 1.0), (), (ones,))
    P.op("dve", lambda e: e.memset(epsc.t[:], EPS), (), (epsc,))
    P.dma("sp", g.t[:], gcol_dram, (), (g,), sem=g)
    srcv = src.rearrange("(kc p) n -> p kc n", p=128)
    dstv = dst.rearrange("(kc p) n -> p kc n", p=128)
    nt = N // T

    def load(i):
        s = xt[i % 2]
        P.dma("sp", s.t[:], srcv[:, :, i * T:(i + 1) * T], (), (s,), sem=s)

    load(0)
    for i in range(nt):
        if i + 1 < nt:
            load(i + 1)
        x, q, o, l, r, p = xt[i % 2], sq[i % 2], st[i % 2], ln[i % 2], rs[i % 2], ps[i % 2]
        P.op("act", lambda e, x=x, q=q: e.activation(out=q.t[:], in_=x.t[:], func=AF.Square), (x,), (q,))
        for kc in range(KC):
            P.op("pe", lambda e, p=p, q=q, kc=kc: e.matmul(p.t[:, 0:T], lhsT=ones.t[:], rhs=q.t[:, kc, :],
                                                          start=(kc == 0), stop=(kc == KC - 1)),
                 (ones, q), (p,))
        P.op("act", lambda e, p=p, l=l: e.activation(out=l.t[:], in_=p.t[:, 0:T], func=AF.Ln,
                                                     scale=1.0 / nfeat, bias=epsc.t[:]), (p, epsc), (l,))
        P.op("act", lambda e, l=l, r=r: e.activation(out=r.t[:], in_=l.t[:], func=AF.Exp, scale=-0.5), (l,), (r,))
        for kc in range(KC):
            P.op("dve", lambda e, o=o, x=x, r=r, kc=kc: e.scalar_tensor_tensor(
                out=o.t[:, kc, :], in0=x.t[:, kc, :], scalar=g.t[:, kc:kc + 1], in1=r.t[:],
                op0=ALU.mult, op1=ALU.mult), (x, r, g), (o,))
        P.dma("sp", dstv[:, :, i * T:(i + 1) * T], o.t[:], (o,), (), sem=o)
    P.end()


def gemm_phase(P, act_srcs, KC, N, blocks, T=512, CB=512, setup=None, npsum=6, nwbuf=2):
    SEG = 512
    P.begin()
    nseg = (CB + SEG - 1) // SEG
    wt = [[P.sb([128, KC, min(SEG, CB - s * SEG)], BF16) for s in range(nseg)] for _ in range(nwbuf)]
    at = [P.sb([128, KC, T], BF16) for _ in range(2)]
    pss = [P.ps() for _ in range(npsum)]
    ctx = setup(P) if setup is not None else None
    nt = N // T
    seq = [(b, t) for b in range(len(blocks)) for t in range(nt)]
    psi = [0]

    def load_w(b):
        w = wt[b % nwbuf]
        for (wap, kc0, nkc, off, ncols) in blocks[b]["wsegs"]:
            if isinstance(wap, int):
                sg, so, ss = off // SEG, off % SEG, wap % SEG
                assert wap // SEG == sg
                ws = w[sg]
                P.op("dve", lambda e, ws=ws, kc0=kc0, nkc=nkc, so=so, ss=ss, ncols=ncols: e.tensor_copy(
                    out=ws.t[:, kc0:kc0 + nkc, so:so + ncols], in_=ws.t[:, kc0:kc0 + nkc, ss:ss + ncols]), (ws,), (ws,))
                continue
            c = 0
            while c < ncols:
                o = off + c
                sg, so = o // SEG, o % SEG
                n = min(ncols - c, SEG - so)
                ws = w[sg]
                P.dma("pool", ws.t[:, kc0:kc0 + nkc, so:so + n],
                      wap[:, c:c + n].rearrange("(kc p) n -> p kc n", p=128), (), (ws,), sem=ws)
                c += n

    def load_a(j):
        a = at[j % 2]
        t = seq[j][1]
        for (aap, kc0, nkc) in act_srcs:
            P.dma("sp", a.t[:, kc0:kc0 + nkc, :],
                  aap.rearrange("(kc p) n -> p kc n", p=128)[:, :, t * T:(t + 1) * T], (), (a,), sem=a)

    load_w(0)
    load_a(0)
    for j, (b, t) in enumerate(seq):
        if t == 0:
            if nwbuf == 2 and b + 1 < len(blocks):
                load_w(b + 1)
            if nwbuf == 1 and b > 0:
                load_w(b)
        if j + 1 < len(seq):
            load_a(j + 1)
        w = wt[b % nwbuf]
        a = at[j % 2]
        for grp in blocks[b]["groups"]:
            accs = grp["accs"]
            if accs[0][0] == "fm":
                pts = []
                for (kind, off, ncols, klo, khi) in accs:
                    p = pss[psi[0] % npsum]
                    psi[0] += 1
                    ws, so = w[off // SEG], off % SEG
                    for kc in range(klo, khi):
                        P.op("pe", lambda e, p=p, ws=ws, a=a, kc=kc, so=so, ncols=ncols, klo=klo, khi=khi:
                             e.matmul(p.t[0:ncols, 0:T], lhsT=ws.t[:, kc, so:so + ncols], rhs=a.t[:, kc, :],
                                      start=(kc == klo), stop=(kc == khi - 1)), (ws, a), (p,))
                    pts.append(p)
                grp["epi"](P, ctx, t * T, T, pts, grp)
            else:
                (kind, off, ncols, klo, khi) = accs[0]
                ws, so = w[off // SEG], off % SEG
                for ts in range(T // 128):
                    p = pss[psi[0] % npsum]
                    psi[0] += 1
                    for kc in range(klo, khi):
                        P.op("pe", lambda e, p=p, ws=ws, a=a, kc=kc, so=so, ncols=ncols, klo=klo, khi=khi, ts=ts:
                             e.matmul(p.t[:, 0:ncols], lhsT=a.t[:, kc, ts * 128:(ts + 1) * 128],
                                      rhs=ws.t[:, kc, so:so + ncols],
                                      start=(kc == klo), stop=(kc == khi - 1)), (ws, a), (p,))
                    grp["epi"](P, ctx, t * T + ts * 128, 128, [p], grp)
    P.end()


class Stager:
    def __init__(self, P, shape, dtype, n=4):
        self.tl = [P.sb(shape, dtype) for _ in range(n)]
        self.i = 0

    def get(self):
        t = self.tl[self.i % len(self.tl)]
        self.i += 1
        return t


def epi_store_fm(dst, row_of):
    def epi(P, ctx, tok0, ntok, pts, grp):
        p = pts[0]
        ncols = grp["accs"][0][2]
        s = ctx["stg"].get()
        copy_op(P, P.evac_eng(), s.t[0:ncols, 0:ntok], p.t[0:ncols, 0:ntok], (p,), (s,))
        r0 = grp["row0"]
        P.dma("sp", dst[r0:r0 + ncols, tok0:tok0 + ntok], s.t[0:ncols, 0:ntok], (s,), (), sem=s)
    return epi


def epi_store_fm32(dst):
    def epi(P, ctx, tok0, ntok, pts, grp):
        p = pts[0]
        ncols = grp["accs"][0][2]
        s = ctx["stg32"].get()
        copy_op(P, P.evac_eng(), s.t[0:ncols, 0:ntok], p.t[0:ncols, 0:ntok], (p,), (s,))
        r0 = grp["row0"]
        P.dma("sp", dst[r0:r0 + ncols, tok0:tok0 + ntok], s.t[0:ncols, 0:ntok], (s,), (), sem=s)
    return epi


def epi_store_tm(dst):
    def epi(P, ctx, tok0, ntok, pts, grp):
        p = pts[0]
        ncols = grp["accs"][0][2]
        s = ctx["stg"].get()
        copy_op(P, P.evac_eng(), s.t[:, 0:ncols], p.t[:, 0:ncols], (p,), (s,))
        c0 = grp["col0"]
        P.dma("sp", dst[tok0:tok0 + 128, c0:c0 + ncols], s.t[:, 0:ncols], (s,), (), sem=s)
    return epi


def setup_stg(dtype=BF16, n=4):
    def setup(P):
        return {"stg": Stager(P, [128, 512], dtype, n)}
    return setup


def epi_gate(dst):
    def epi(P, ctx, tok0, ntok, pts, grp):
        p = pts[0]
        s = ctx["stg"].get()
        bcol = grp["bcol"]
        bg = ctx["bg"]
        P.op("act", lambda e: e.activation(out=s.t[:, 0:ntok], in_=p.t[:, 0:ntok], func=AF.Sigmoid,
                                           bias=bg.t[:, bcol:bcol + 1]), (p, bg), (s,))
        r0 = grp["row0"]
        P.dma("sp", dst[r0:r0 + 128, tok0:tok0 + ntok], s.t[:, 0:ntok], (s,), (), sem=s)
    return epi


def epi_rope(dst, cos_d, sin_d):
    def epi(P, ctx, tok0, ntok, pts, grp):
        pa, pb = pts
        k = ctx["cs_i"]
        ctx["cs_i"] += 1
        ct, sn = ctx["cos"][k % 2], ctx["sin"][k % 2]
        P.dma("sp", ct.t[0:64, 0:ntok], cos_d[:, tok0:tok0 + ntok], (), (ct,), sem=ct)
        P.dma("sp", sn.t[0:64, 0:ntok], sin_d[:, tok0:tok0 + ntok], (), (sn,), sem=sn)
        t1, t2 = ctx["rt1"][k % 2], ctx["rt2"][k % 2]
        P.op("dve", lambda e: e.tensor_tensor(out=t1.t[0:64, 0:ntok], in0=pa.t[0:64, 0:ntok],
                                              in1=ct.t[0:64, 0:ntok], op=ALU.mult), (pa, ct), (t1,))
        P.op("dve", lambda e: e.tensor_tensor(out=t2.t[0:64, 0:ntok], in0=pb.t[0:64, 0:ntok],
                                              in1=sn.t[0:64, 0:ntok], op=ALU.mult), (pb, sn), (t2,))
        s = ctx["stg"].get()
        P.op("dve", lambda e: e.tensor_tensor(out=s.t[0:64, 0:ntok], in0=t1.t[0:64, 0:ntok],
                                               in1=t2.t[0:64, 0:ntok], op=ALU.add), (t1, t2), (s,))
        r0 = grp["row0"]
        P.dma("sp", dst[r0:r0 + 64, tok0:tok0 + ntok], s.t[0:64, 0:ntok], (s,), (), sem=s)
    return epi


def setup_rope(extra=None):
    def setup(P):
        ctx = {"stg": Stager(P, [128, 512], BF16, 4), "stg32": Stager(P, [128, 512], F32, 4), "cs_i": 0,
               "cos": [P.sb([128, 512], F32) for _ in range(2)],
               "sin": [P.sb([128, 512], F32) for _ in range(2)],
               "rt1": [P.sb([128, 512], F32) for _ in range(2)],
               "rt2": [P.sb([128, 512], F32) for _ in range(2)]}
        if extra is not None:
            extra(P, ctx)
        return ctx
    return setup


def epi_resid(dst, resid):
    def epi(P, ctx, tok0, ntok, pts, grp):
        p = pts[0]
        r0 = grp["row0"]
        k = ctx["r_i"]
        ctx["r_i"] += 1
        rt = ctx["rt"][k % 3]
        P.dma("sp", rt.t[:, 0:ntok], resid[r0:r0 + 128, tok0:tok0 + ntok], (), (rt,), sem=rt)
        s = ctx["stg"].get()
        P.op("dve", lambda e: e.tensor_tensor(out=s.t[:, 0:ntok], in0=p.t[:, 0:ntok], in1=rt.t[:, 0:ntok],
                                              op=ALU.add), (p, rt), (s,))
        P.dma("sp", dst[r0:r0 + 128, tok0:tok0 + ntok], s.t[:, 0:ntok], (s,), (), sem=s)
    return epi


def setup_resid(P):
    return {"stg": Stager(P, [128, 512], F32, 3), "r_i": 0, "rt": [P.sb([128, 512], F32) for _ in range(3)]}


def epi_merge(dst, gT):
    def epi(P, ctx, tok0, ntok, pts, grp):
        pa, pb = pts
        r0 = grp["row0"]
        k = ctx["r_i"]
        ctx["r_i"] += 1
        g0, g1 = ctx["g0"][k % 2], ctx["g1"][k % 2]
        t1, t2 = ctx["t1"][k % 2], ctx["t2"][k % 2]
        P.dma("sp", g0.t[:, 0:ntok], gT[r0:r0 + 128, tok0:tok0 + ntok], (), (g0,), sem=g0)
        P.dma("sp", g1.t[:, 0:ntok], gT[D + r0:D + r0 + 128, tok0:tok0 + ntok], (), (g1,), sem=g1)
        P.op("dve", lambda e: e.tensor_tensor(out=t1.t[:, 0:ntok], in0=pa.t[:, 0:ntok], in1=g0.t[:, 0:ntok],
                                              op=ALU.mult), (pa, g0), (t1,))
        P.op("dve", lambda e: e.tensor_tensor(out=t2.t[:, 0:ntok], in0=pb.t[:, 0:ntok], in1=g1.t[:, 0:ntok],
                                              op=ALU.mult), (pb, g1), (t2,))
        s = ctx["stg"].get()
        P.op("dve", lambda e: e.tensor_tensor(out=s.t[:, 0:ntok], in0=t1.t[:, 0:ntok], in1=t2.t[:, 0:ntok],
                                               op=ALU.add), (t1, t2), (s,))
        P.dma("sp", dst[r0:r0 + 128, tok0:tok0 + ntok], s.t[:, 0:ntok], (s,), (), sem=s)
    return epi


def setup_merge(P):
    return {"stg": Stager(P, [128, 512], BF16, 3), "r_i": 0,
            "g0": [P.sb([128, 512], F32) for _ in range(2)], "g1": [P.sb([128, 512], F32) for _ in range(2)],
            "t1": [P.sb([128, 512], F32) for _ in range(2)], "t2": [P.sb([128, 512], F32) for _ in range(2)]}


def epi_swiglu(dst):
    def epi(P, ctx, tok0, ntok, pts, grp):
        pg, pu = pts
        r0 = grp["row0"]
        k = ctx["r_i"]
        ctx["r_i"] += 1
        t1 = ctx["t1"][k % 3]
        P.op("act", lambda e: e.activation(out=t1.t[:, 0:ntok], in_=pg.t[:, 0:ntok], func=AF.Silu), (pg,), (t1,))
        s = ctx["stg"].get()
        P.op("dve", lambda e: e.tensor_tensor(out=s.t[:, 0:ntok], in0=pu.t[:, 0:ntok], in1=t1.t[:, 0:ntok],
                                              op=ALU.mult), (pu, t1), (s,))
        P.dma("sp", dst[r0:r0 + 128, tok0:tok0 + ntok], s.t[:, 0:ntok], (s,), (), sem=s)
    return epi


def setup_swiglu(P):
    return {"stg": Stager(P, [128, 512], BF16, 3), "r_i": 0, "t1": [P.sb([128, 512], F32) for _ in range(3)]}


def latent_blocks(w_in, off, nlat, rawdst, rope=None):
    ncl = nlat * 128
    wsegs = [(w_in[:, off:off + ncl], 0, 16, 0, ncl)]
    groups = [dict(accs=[("fm", m * 128, 128, 0, 16)], row0=m * 128, epi=epi_store_fm32(rawdst)) for m in range(nlat)]
    CB = ncl
    if rope is not None:
        (rdst, cos_d, sin_d, roff) = rope
        wsegs.append((w_in[:, roff:roff + 64], 0, 16, ncl, 64))
        wsegs.append((ncl + 32, 0, 16, ncl + 64, 32))
        wsegs.append((ncl, 0, 16, ncl + 96, 32))
        groups.append(dict(accs=[("fm", ncl, 64, 0, 16), ("fm", ncl + 64, 64, 0, 16)], row0=0,
                           epi=epi_rope(rdst, cos_d, sin_d)))
        CB = ncl + 128
    return [dict(wsegs=wsegs, groups=groups)], setup_rope(), CB


def attn_phase(P, S, So, units, consts, kind):
    nq = So // 256
    nkt_all = S // 128
    dvv = 128 if kind == "mla" else 256
    P.begin()
    if kind == "mla":
        kn = [P.sb([128, S], BF16) for _ in range(2)]
        qn = [P.sb([128, So], BF16) for _ in range(2)]
    else:
        kn = [P.sb([128, 2, S], BF16) for _ in range(2)]
        qn = [P.sb([128, 2, So], BF16) for _ in range(2)]
    vt = [P.sb([128, nkt_all, 257], BF16) for _ in range(2)]
    ident = P.sb([128, 128], BF16)
    P.dma("pool", ident.t[:], consts["ident"], (), (ident,), sem=ident)
    for v in vt:
        if kind == "mla":
            P.op("pool", lambda e, v=v: e.memset(v.t[:, :, 128:257], 0.0), (), (v,))
        P.op("pool", lambda e, v=v: e.memset(v.t[:, :, dvv:dvv + 1], 1.0), (), (v,))
    nS = 5 if kind == "mla" else 4
    LA = nS - 2
    NPT = LA + 3
    pt = [P.sb([128, 512], BF16) for _ in range(NPT)]
    sbank = [P.ps() for _ in range(nS)]
    if kind == "mla":
        tpb = P.ps([128, 1024], BF16)
        tpr = [Res(), Res()]
    rsm = [P.sb([128, 1], F32) for _ in range(4)]
    if kind == "mla":
        kr = P.sb([128, S], BF16)
        qr = [P.sb([128, So], BF16) for _ in range(2)]
        P.op("pool", lambda e: e.memset(kr.t[64:128, :], 0.0), (), (kr,))
        for q_ in qr:
            P.op("pool", lambda e, q_=q_: e.memset(q_.t[64:128, :], 0.0), (), (q_,))
        P.dma("sp", kr.t[0:64, :], consts["krT"], (), (kr,), sem=kr)
        msk = P.sb([128, 2048], F32)
        P.dma("sp", msk.t[:], consts["mask_mla"].rearrange("p a b -> p (a b)"), (), (msk,), sem=msk)
        obank = [P.ps() for _ in range(2)]
        ybf = [P.sb([128, 128], BF16) for _ in range(4)]
        ystg = [P.sb([128, 256], BF16) for _ in range(2)]
    else:
        mskd = [P.sb([128, 2048], F32) for _ in range(2)]
        abias = P.sb([128, NH_D * 80], F32)
        P.dma("sp", abias.t[:], consts["abias"], (), (abias,), sem=abias)
        obank = [P.ps() for _ in range(4)]
        o1n = [P.sb([128, 256], F32) for _ in range(2)]
        ofull = [P.sb([128, 256], F32) for _ in range(2)]
        junk = P.sb([128, 256], F32)
        ybf = [P.sb([128, 256], BF16) for _ in range(2)]
        ystg = [P.sb([128, 2, 256], BF16) for _ in range(2)]
        r2l = [P.sb([128, 1], F32) for _ in range(2)]
        ssq = [P.sb([128, 1], F32) for _ in range(2)]
        lnv = [P.sb([128, 1], F32) for _ in range(2)]
        rstd = [P.sb([128, 1], F32) for _ in range(2)]
        epsc = P.sb([128, 1], F32)
        P.op("dve", lambda e: e.memset(epsc.t[:], EPS), (), (epsc,))
        gsub = P.sb([128, 256], F32)
        P.dma("sp", gsub.t[:], consts["gsub"], (), (gsub,), sem=gsub)
        P.op("act", lambda e: e.activation(out=gsub.t[:], in_=gsub.t[:], func=AF.Copy, scale=1.0 - LAMBDA_INIT),
             (gsub,), (gsub,))
        lt = [P.sb([128, 128], F32) for _ in range(4)]
        for i, nm in enumerate(["lq1", "lk1", "lq2", "lk2"]):
            P.dma("sp", lt[i].t[:], consts[nm], (), (lt[i],), sem=lt[i])
        pr = [P.sb([128, 128], F32) for _ in range(2)]
        sm = [P.sb([128, 1], F32) for _ in range(2)]
        ex = [P.sb([128, 1], F32) for _ in range(2)]
        neglam = P.sb([128, 1], F32)
        for i in range(2):
            P.op("dve", lambda e, i=i: e.tensor_tensor(out=pr[i].t[:], in0=lt[2 * i].t[:], in1=lt[2 * i + 1].t[:],
                                                       op=ALU.mult), (lt[2 * i], lt[2 * i + 1]), (pr[i],))
            P.op("act", lambda e, i=i: e.activation(out=junk.t[:, 0:128], in_=pr[i].t[:], func=AF.Copy,
                                                    accum_out=sm[i].t[:]), (pr[i],), (junk, sm[i]))
            P.op("act", lambda e, i=i: e.activation(out=ex[i].t[:], in_=sm[i].t[:], func=AF.Exp), (sm[i],), (ex[i],))
        P.op("dve", lambda e: e.tensor_tensor(out=neglam.t[:], in0=ex[1].t[:], in1=ex[0].t[:], op=ALU.subtract),
             (ex[0], ex[1]), (neglam,))
        P.op("dve", lambda e: e.tensor_scalar(out=neglam.t[:], in0=neglam.t[:], scalar1=-LAMBDA_INIT, scalar2=None,
                                              op0=ALU.add), (neglam,), (neglam,))

    vloaded = {}

    def load_unit(u):
        un = units[u]
        b = u % 2
        if kind == "mla":
            P.dma("sp", kn[b].t[:], un["kT"], (), (kn[b],), sem=kn[b])
            P.dma("sp", qn[b].t[:], un["qT"], (), (qn[b],), sem=qn[b])
            P.dma("sp", qr[b].t[0:64, :], un["qrT"], (), (qr[b],), sem=qr[b])
        else:
            for m in range(2):
                P.dma("sp", kn[b].t[:, m, :], un["kT"][m], (), (kn[b],), sem=kn[b])
                P.dma("sp", qn[b].t[:, m, :], un["qT"][m], (), (qn[b],), sem=qn[b])
        vk = un["vkey"]
        if vk not in vloaded:
            vb = len(vloaded) % 2
            vloaded[vk] = vb
            vv = un["v"].rearrange("(t p) c -> p t c", p=128)
            step = 16
            for t0 in range(0, nkt_all, step):
                t1 = min(nkt_all, t0 + step)
                P.dma("sp", vt[vb].t[:, t0:t1, 0:dvv], vv[:, t0:t1, :], (), (vt[vb],), sem=vt[vb])
            if kind == "diff":
                P.dma("sp", mskd[vb].t[:], consts["mask_diff"][un["h"]].rearrange("p a b -> p (a b)"), (),
                      (mskd[vb],), sem=mskd[vb])

    pairs = []
    for u in range(len(units)):
        for i in range(nq):
            if kind == "mla":
                ks = list(range(4 * (i + 1)))
            else:
                slope = 2.0 ** (-(units[u]["h"] + 1))
                ks = [kt for kt in range(8 * (i + 1)) if slope * ((8 * i - kt) * 128 - 127) < 140.0]
            for n_, k_ in enumerate(ks):
                pairs.append((u, i, k_, n_ == 0, n_ == len(ks) - 1))
    cnt = {"s": 0, "p": 0, "e": 0}
    pend = {}

    def do_qk(n):
        (u, i, kx, first, last) = pairs[n]
        un = units[u]
        b = u % 2
        sp_ = sbank[cnt["s"] % nS]
        cnt["s"] += 1
        q0 = i * 256
        p = pt[cnt["p"] % NPT]
        cnt["p"] += 1
        sc = un["scale"]
        if kind == "mla":
            nk2 = 4 * (i + 1)
            for j in range(2):
                k0 = (2 * kx + j) * 128
                P.op("pe", lambda e, j=j, k0=k0: e.matmul(
                    sp_.t[:, j * 256:(j + 1) * 256], lhsT=kn[b].t[:, k0:k0 + 128], rhs=qn[b].t[:, q0:q0 + 256],
                    start=True, stop=False), (kn[b], qn[b]), (sp_,))
                P.op("pe", lambda e, j=j, k0=k0: e.matmul(
                    sp_.t[:, j * 256:(j + 1) * 256], lhsT=kr.t[:, k0:k0 + 128], rhs=qr[b].t[:, q0:q0 + 256],
                    start=False, stop=True), (kr, qr[b]), (sp_,))
            P.op("act", lambda e: e.activation(out=p.t[:], in_=sp_.t[:, 0:512], func=AF.Exp, scale=sc), (sp_,), (p,))
            if kx >= nk2 - 4:
                kk = 2 * (kx - (nk2 - 4))
                P.op("dve", lambda e: e.tensor_tensor(out=p.t[:], in0=p.t[:], in1=msk.t[:, kk * 256:(kk + 2) * 256],
                                                      op=ALU.mult), (p, msk), (p,))
        else:
            h = un["h"]
            k0 = kx * 128
            for m in range(2):
                P.op("pe", lambda e, m=m: e.matmul(
                    sp_.t[:, m * 256:(m + 1) * 256], lhsT=kn[b].t[:, m, k0:k0 + 128], rhs=qn[b].t[:, m, q0:q0 + 256],
                    start=True, stop=True), (kn[b], qn[b]), (sp_,))
            dst = kx - 8 * i
            if h == 0:
                for jq in range(2):
                    bi = h * 80 + dst - 2 * jq + 66
                    c0 = jq * 128
                    P.op("act", lambda e, bi=bi, c0=c0: e.activation(
                        out=p.t[:, 0:512].rearrange("p (m q) -> p m q", m=2)[:, :, c0:c0 + 128],
                        in_=sp_.t[:, 0:512].rearrange("p (m q) -> p m q", m=2)[:, :, c0:c0 + 128], func=AF.Exp,
                        scale=sc, bias=abias.t[:, bi:bi + 1]), (sp_, abias), (p,))
            else:
                bi = h * 80 + dst + 66
                P.op("act", lambda e, bi=bi: e.activation(out=p.t[:], in_=sp_.t[:, 0:512], func=AF.Exp, scale=sc,
                                                          bias=abias.t[:, bi:bi + 1]), (sp_, abias), (p,))
            if kx >= 8 * i:
                kk = kx - 8 * i
                mk_ = mskd[vloaded[un["vkey"]]]
                for m in range(2):
                    P.op("dve", lambda e, m=m: e.tensor_tensor(
                        out=p.t[:, m * 256:(m + 1) * 256], in0=p.t[:, m * 256:(m + 1) * 256],
                        in1=mk_.t[:, kk * 256:(kk + 1) * 256], op=ALU.mult), (p, mk_), (p,))
        pend[n] = p

    def do_pv(n):
        (u, i, kx, first, last) = pairs[n]
        un = units[u]
        p = pend.pop(n)
        if i == 0 and first and u + 1 < len(units):
            load_unit(u + 1)
        v = vt[vloaded[un["vkey"]]]
        ei = u * nq + i
        h = un["h"]
        if kind == "mla":
            for j in range(2):
                for qs in range(2):
                    P.op("pe", lambda e, qs=qs, j=j: e.matmul(
                        obank[qs].t[:, 0:129], lhsT=p.t[:, j * 256 + qs * 128:j * 256 + (qs + 1) * 128],
                        rhs=v.t[:, 2 * kx + j, 0:129], start=(first and j == 0), stop=(last and j == 1)),
                        (p, v), (obank[qs],))
            if last:
                ys = ystg[ei % 2]
                tb = _R(tpr[ei % 2])
                tpv = tpb.t[:, (ei % 2) * 256:(ei % 2) * 256 + 256]
                for qs in range(2):
                    ob = obank[qs]
                    r = rsm[cnt["e"] % 4]
                    yb = ybf[cnt["e"] % 4]
                    cnt["e"] += 1
                    P.op("dve", lambda e, r=r, ob=ob: e.reciprocal(out=r.t[:], in_=ob.t[:, 128:129]), (ob,), (r,))
                    P.op("dve", lambda e, r=r, yb=yb, ob=ob: e.tensor_scalar(
                        out=yb.t[:], in0=ob.t[:, 0:128], scalar1=r.t[:], scalar2=None, op0=ALU.mult), (ob, r), (yb,))
                    c0 = qs * 128
                    P.op("pe", lambda e, yb=yb, c0=c0: e.transpose(tpv[:, c0:c0 + 128], yb.t[:], ident.t[:]),
                         (yb, ident), (tb,))
                copy_op(P, "dve", ys.t[:], tpv[:, 0:256], (tb,), (ys,))
                P.dma("sp", consts["yT"][h * 128:(h + 1) * 128, i * 256:(i + 1) * 256], ys.t[:], (ys,), (), sem=ys)
        else:
            for m in range(2):
                for qs in range(2):
                    ob = obank[m * 2 + qs]
                    P.op("pe", lambda e, qs=qs, m=m, ob=ob: e.matmul(
                        ob.t[:, 0:257], lhsT=p.t[:, m * 256 + qs * 128:m * 256 + (qs + 1) * 128],
                        rhs=v.t[:, kx, :], start=first, stop=last), (p, v), (ob,))
            if last:
                ys = ystg[ei % 2]
                tb = sbank[cnt["s"] % nS]
                cnt["s"] += 1
                tpv = tb.t[:, 0:256].bitcast(BF16)
                for qs in range(2):
                    o1b, o2b = obank[qs], obank[2 + qs]
                    r1 = rsm[cnt["e"] % 4]
                    r2 = rsm[(cnt["e"] + 1) % 4]
                    cnt["e"] += 2
                    k2 = qs
                    t1, rl, of, sq_, ln_, rs_, yb = o1n[k2], r2l[k2], ofull[k2], ssq[k2], lnv[k2], rstd[k2], ybf[k2]
                    P.op("dve", lambda e, r1=r1, o1b=o1b: e.reciprocal(out=r1.t[:], in_=o1b.t[:, 256:257]), (o1b,), (r1,))
                    P.op("dve", lambda e, r1=r1, o1b=o1b, t1=t1: e.tensor_scalar(
                        out=t1.t[:], in0=o1b.t[:, 0:256], scalar1=r1.t[:], scalar2=None, op0=ALU.mult),
                        (o1b, r1), (t1,))
                    P.op("dve", lambda e, r2=r2, o2b=o2b: e.reciprocal(out=r2.t[:], in_=o2b.t[:, 256:257]), (o2b,), (r2,))
                    P.op("dve", lambda e, r2=r2, rl=rl: e.tensor_tensor(out=rl.t[:], in0=r2.t[:], in1=neglam.t[:],
                                                                        op=ALU.mult), (r2, neglam), (rl,))
                    P.op("dve", lambda e, rl=rl, of=of, o2b=o2b, t1=t1: e.scalar_tensor_tensor(
                        out=of.t[:], in0=o2b.t[:, 0:256], scalar=rl.t[:], in1=t1.t[:],
                        op0=ALU.mult, op1=ALU.add), (o2b, rl, t1), (of,))
                    P.op("act", lambda e, of=of, sq_=sq_: e.activation(out=junk.t[:], in_=of.t[:], func=AF.Square,
                                                                       accum_out=sq_.t[:]), (of,), (junk, sq_))
                    P.op("act", lambda e, sq_=sq_, ln_=ln_: e.activation(out=ln_.t[:], in_=sq_.t[:], func=AF.Ln,
                                                                         scale=1.0 / 256, bias=epsc.t[:]),
                         (sq_, epsc), (ln_,))
                    P.op("act", lambda e, ln_=ln_, rs_=rs_: e.activation(out=rs_.t[:], in_=ln_.t[:], func=AF.Exp,
                                                                         scale=-0.5), (ln_,), (rs_,))
                    P.op("dve", lambda e, yb=yb, of=of, rs_=rs_: e.scalar_tensor_tensor(
                        out=yb.t[:], in0=of.t[:], scalar=rs_.t[:], in1=gsub.t[:], op0=ALU.mult, op1=ALU.mult),
                        (of, rs_, gsub), (yb,))
                    for c in range(2):
                        c0 = c * 256 + qs * 128
                        P.op("pe", lambda e, yb=yb, c=c, c0=c0: e.transpose(
                            tpv[:, c0:c0 + 128], yb.t[:, c * 128:(c + 1) * 128], ident.t[:]),
                            (yb, ident), (tb,))
                copy_op(P, "dve", ys.t[:, :, :], tpv[:, 0:512].rearrange("p (c q) -> p c q", c=2), (tb,), (ys,))
                P.dma("sp", consts["yT"][h * 256:(h + 1) * 256, i * 256:(i + 1) * 256].rearrange(
                    "(c p) q -> p c q", p=128), ys.t[:, :, :], (ys,), (), sem=ys)

    load_unit(0)
    for n in range(len(pairs) + LA):
        if n < len(pairs):
            do_qk(n)
        if n >= LA:
            do_pv(n - LA)
    P.end()


class _R:
    __slots__ = ("res",)

    def __init__(self, res):
        self.res = res


def build_program(S, debug=False, upto=99):
    So = S // 4
    To = min(512, So)
    nc = bass.Bass("TRN2", target_bir_lowering=False)
    P = Prog(nc)

    def din(name, shape, dt=F32):
        return nc.dram_tensor(name, list(shape), dt, kind="ExternalInput").ap()

    def scr(name, shape, dt):
        kind = "ExternalOutput" if debug else "Internal"
        return nc.dram_tensor(name, list(shape), dt, kind=kind).ap()

    xall = din("xall", [D, S])
    xown = din("xown", [D, So])
    w_in = din("w_in", [D, D_IN])
    w_uq = din("w_uq", [QR, NH_M * 192])
    w_ukv = din("w_ukv", [KVR, NH_M * 256])
    w_mla = din("w_mla_proj", [D, D])
    w_dif = din("w_diff_proj", [D, D])
    w_out = din("w_out", [D, D])
    w_fg = din("w_ffn_gate", [D, DFF])
    w_fu = din("w_ffn_up", [D, DFF])
    w_fd = din("w_ffn_down", [DFF, D])
    g_attn = din("g_attn", [128, 16])
    g_q = din("g_q", [128, 6])
    g_kv = din("g_kv", [128, 4])
    g_ffn = din("g_ffn", [128, 16])
    g_fin = din("g_fin", [128, 16])
    b_gate = din("b_gate", [128, 32])
    gsub = din("gsub", [128, 256])
    lq1 = din("lq1", [128, 128])
    lk1 = din("lk1", [128, 128])
    lq2 = din("lq2", [128, 128])
    lk2 = din("lk2", [128, 128])
    cos_all = din("cos_all", [64, S])
    sin_all = din("sin_all", [64, S])
    cos_own = din("cos_own", [64, So])
    sin_own = din("sin_own", [64, So])
    abias = din("abias", [128, NH_D * 80])
    mask_mla = din("mask_mla", [128, 8, 256])
    mask_diff = din("mask_diff", [NH_D, 128, 8, 256])
    ident = din("ident", [128, 128])

    hT_all = scr("hT_all", [D, S], BF16)
    hT_own = scr("hT_own", [D, So], BF16)
    dkT = scr("dkT", [D, S], BF16)
    dv = scr("dv", [S, D], BF16)
    ckvnT = scr("ckvnT", [KVR, S], BF16)
    ckvraw = scr("ckvraw", [KVR, S], F32)
    cqraw = scr("cqraw", [QR, So], F32)
    krT = scr("krT", [64, S], BF16)
    knT = scr("knT", [D, S], BF16)
    vm = scr("vm", [S, D], BF16)
    dqT = scr("dqT", [D, So], BF16)
    gT = scr("gT", [2 * D, So], F32)
    cqnT = scr("cqnT", [QR, So], BF16)
    qnT = scr("qnT", [D, So], BF16)
    qrT = scr("qrT", [NH_M * 64, So], BF16)
    ymT = scr("ymT", [D, So], BF16)
    ydT = scr("ydT", [D, So], BF16)
    mT = scr("mT", [D, So], BF16)
    x1T = scr("x1T", [D, So], F32)
    h2T = scr("h2T", [D, So], BF16)
    aT = scr("aT", [DFF, So], BF16)
    x2T = scr("x2T", [D, So], F32)
    outT = nc.dram_tensor("outT", [D, So], F32, kind="ExternalOutput").ap()

    def done():
        P.top.close()
        return nc, P

    norm_phase(P, xall, hT_all, g_attn, S, BF16)
    norm_phase(P, xown, hT_own, g_attn, So, BF16)
    if upto <= 0:
        return done()

    blocks = [dict(wsegs=[(w_in[:, OFF_DK:OFF_DK + 2048], 0, 16, 0, 2048)],
                   groups=[dict(accs=[("fm", m * 128, 128, 0, 16)], row0=m * 128, epi=epi_store_fm(dkT, None))
                           for m in range(16)]),
              dict(wsegs=[(w_in[:, OFF_DV:OFF_DV + 2048], 0, 16, 0, 2048)],
                   groups=[dict(accs=[("tm", cb * 512, 512, 0, 16)], col0=cb * 512, epi=epi_store_tm(dv))
                           for cb in range(4)])]
    gemm_phase(P, [(hT_all, 0, 16)], 16, S, blocks, CB=2048, setup=setup_stg())
    if upto <= 0.3:
        return done()
    blocks, setup, CB = latent_blocks(w_in, OFF_CKV, 4, ckvraw, rope=(krT, cos_all, sin_all, OFF_KPE))
    gemm_phase(P, [(hT_all, 0, 16)], 16, S, blocks, CB=CB, setup=setup)
    norm_phase(P, ckvraw, ckvnT, g_kv, S, BF16, nfeat=KVR)

    if upto <= 0.6:
        return done()
    groups = []
    for h in range(NH_M):
        groups.append(dict(accs=[("fm", h * 256, 128, 0, 4)], row0=h * 128, epi=epi_store_fm(knT, None)))
        groups.append(dict(accs=[("tm", h * 256 + 128, 128, 0, 4)], col0=h * 128, epi=epi_store_tm(vm)))
    blocks = [dict(wsegs=[(w_ukv[:, :], 0, 4, 0, 4096)], groups=groups)]
    gemm_phase(P, [(ckvnT, 0, 4)], 4, S, blocks, CB=4096, nwbuf=1, setup=setup_stg())
    if upto <= 1:
        return done()

    blocks = [dict(wsegs=[(w_in[:, OFF_DQ:OFF_DQ + 2048], 0, 16, 0, 2048)],
                   groups=[dict(accs=[("fm", m * 128, 128, 0, 16)], row0=m * 128, epi=epi_store_fm(dqT, None))
                           for m in range(16)])]
    gemm_phase(P, [(hT_own, 0, 16)], 16, So, blocks, T=To, CB=2048, nwbuf=1, setup=setup_stg())

    blocks = []
    for cb in range(2):
        c0 = OFF_G + cb * 2048
        blocks.append(dict(
            wsegs=[(w_in[:, c0:c0 + 2048], 0, 16, 0, 2048)],
            groups=[dict(accs=[("fm", m * 128, 128, 0, 16)], row0=cb * 2048 + m * 128, bcol=cb * 16 + m,
                         epi=epi_gate(gT)) for m in range(16)]))

    def setup_gate(P):
        ctx = {"stg": Stager(P, [128, 512], F32, 4), "bg": P.sb([128, 32], F32)}
        P.dma("sp", ctx["bg"].t[:], b_gate, (), (ctx["bg"],), sem=ctx["bg"])
        return ctx
    gemm_phase(P, [(hT_own, 0, 16)], 16, So, blocks, T=To, CB=2048, setup=setup_gate)

    blocks, setup, CB = latent_blocks(w_in, OFF_CQ, 6, cqraw)
    gemm_phase(P, [(hT_own, 0, 16)], 16, So, blocks, T=To, CB=CB, setup=setup)
    norm_phase(P, cqraw, cqnT, g_q, So, BF16, nfeat=QR)

    wsegs, groups = [], []
    for h in range(NH_M):
        o = h * 256
        wsegs.append((w_uq[:, h * 192:h * 192 + 192], 0, 6, o, 192))
        wsegs.append((o + 160, 0, 6, o + 192, 32))
        wsegs.append((o + 128, 0, 6, o + 224, 32))
        groups.append(dict(accs=[("fm", o, 128, 0, 6)], row0=h * 128, epi=epi_store_fm(qnT, None)))
        groups.append(dict(accs=[("fm", o + 128, 64, 0, 6), ("fm", o + 192, 64, 0, 6)], row0=h * 64,
                           epi=epi_rope(qrT, cos_own, sin_own)))
    blocks = [dict(wsegs=wsegs, groups=groups)]
    gemm_phase(P, [(cqnT, 0, 6)], 6, So, blocks, T=To, CB=4096, nwbuf=1, setup=setup_rope())
    if upto <= 2:
        return done()

    consts = dict(ident=ident, krT=krT, mask_mla=mask_mla, yT=ymT)
    units = [dict(kT=knT[h * 128:(h + 1) * 128, :], qT=qnT[h * 128:(h + 1) * 128, :],
                  qrT=qrT[h * 64:(h + 1) * 64, :], v=vm[:, h * 128:(h + 1) * 128], vkey=h, h=h, m=0,
                  scale=1.0 / math.sqrt(192.0)) for h in range(NH_M)]
    attn_phase(P, S, So, units, consts, "mla")
    if upto <= 3:
        return done()

    consts = dict(ident=ident, abias=abias, mask_diff=mask_diff, yT=ydT, gsub=gsub, lq1=lq1, lk1=lk1, lq2=lq2, lk2=lk2)
    units = []
    for h in range(NH_D):
        rows = [(h * 2 + m) * 128 for m in range(2)]
        units.append(dict(kT=[dkT[r:r + 128, :] for r in rows], qT=[dqT[r:r + 128, :] for r in rows],
                          v=dv[:, h * 256:(h + 1) * 256], vkey=h, h=h, scale=1.0 / math.sqrt(128.0)))
    attn_phase(P, S, So, units, consts, "diff")
    if upto <= 4:
        return done()

    blocks = []
    for cb in range(2):
        c0 = cb * 1024
        blocks.append(dict(
            wsegs=[(w_mla[:, c0:c0 + 1024], 0, 16, 0, 1024), (w_dif[:, c0:c0 + 1024], 16, 16, 0, 1024)],
            groups=[dict(accs=[("fm", m * 128, 128, 0, 16), ("fm", m * 128, 128, 16, 32)], row0=c0 + m * 128,
                         epi=epi_merge(mT, gT)) for m in range(8)]))
    gemm_phase(P, [(ymT, 0, 16), (ydT, 16, 16)], 32, So, blocks, T=To, CB=1024, nwbuf=1, setup=setup_merge)

    blocks = [dict(wsegs=[(w_out[:, :], 0, 16, 0, 2048)],
                   groups=[dict(accs=[("fm", m * 128, 128, 0, 16)], row0=m * 128, epi=epi_resid(x1T, xown))
                           for m in range(16)])]
    gemm_phase(P, [(mT, 0, 16)], 16, So, blocks, T=To, CB=2048, nwbuf=1, setup=setup_resid)
    if upto <= 5:
        return done()

    norm_phase(P, x1T, h2T, g_ffn, So, BF16)
    blocks = []
    for c0 in range(0, DFF, 1024):
        n = min(1024, DFF - c0)
        blocks.append(dict(
            wsegs=[(w_fg[:, c0:c0 + n], 0, 16, 0, n), (w_fu[:, c0:c0 + n], 0, 16, 1024, n)],
            groups=[dict(accs=[("fm", m * 128, 128, 0, 16), ("fm", 1024 + m * 128, 128, 0, 16)], row0=c0 + m * 128,
                         epi=epi_swiglu(aT)) for m in range(n // 128)]))
    gemm_phase(P, [(h2T, 0, 16)], 16, So, blocks, T=To, CB=2048, setup=setup_swiglu)
    blocks = []
    for cb in range(2):
        c0 = cb * 1024
        blocks.append(dict(
            wsegs=[(w_fd[:, c0:c0 + 1024], 0, 44, 0, 1024)],
            groups=[dict(accs=[("fm", m * 128, 128, 0, 44)], row0=c0 + m * 128, epi=epi_resid(x2T, x1T))
                    for m in range(8)]))
    gemm_phase(P, [(aT, 0, 44)], 44, So, blocks, T=To, CB=1024, nwbuf=1, setup=setup_resid)

    norm_phase(P, x2T, outT, g_fin, So, F32)
    return done()


def _col(v, n):
    return np.ascontiguousarray(np.asarray(v, np.float32).reshape(n, 128).T)


def _rep(v):
    v = np.asarray(v, np.float32).reshape(1, -1)
    return np.ascontiguousarray(np.broadcast_to(v, (128, v.shape[1])))


def position_tables(S):
    inv = (10000.0 ** (-np.arange(0, 64, 2, dtype=np.float32) / np.float32(64))).astype(np.float32)
    ang = np.arange(S, dtype=np.float32)[:, None] * inv[None, :]
    c, s = np.cos(ang).astype(np.float32).T, np.sin(ang).astype(np.float32).T
    cos2 = np.ascontiguousarray(np.concatenate([c, c], 0))
    sin2 = np.ascontiguousarray(np.concatenate([-s, s], 0))
    slopes = np.exp2(-8.0 * np.arange(1, NH_D + 1, dtype=np.float32) / NH_D).astype(np.float64)
    per_core = []
    p = np.arange(128)[:, None, None]
    for c_ in range(4):
        ab = np.zeros((128, NH_D, 80), np.float32)
        for h in range(NH_D):
            clampv = slopes[h] * (127 if h == 0 else 255)
            di = np.arange(80)[None, :]
            val = slopes[h] * ((di - 66 - 2 * c_) * 128 + np.arange(128)[:, None])
            ab[:, h, :] = np.minimum(val, clampv)
        kk = np.arange(8)[None, :, None]
        col = np.arange(256)[None, None, :]
        kr = kk * 128 + p
        qr = c_ * 256 + col
        allowed = (kr // 64) <= (qr // 64)
        mm = allowed.astype(np.float32)
        md = np.zeros((NH_D, 128, 8, 256), np.float32)
        for h in range(NH_D):
            corr = np.where(kr > qr, np.exp(-2.0 * slopes[h] * np.maximum(kr - qr, 0)), 1.0)
            md[h] = (allowed * corr).astype(np.float32)
        per_core.append(dict(abias=np.ascontiguousarray(ab.reshape(128, NH_D * 80)), mask_mla=np.ascontiguousarray(mm),
                             mask_diff=md))
    return cos2, sin2, per_core


_CACHE = {}


def kernel(x, attn_norm_g, w_in, b_gate, q_norm_g, w_uq, kv_norm_g, w_ukv,
           lambda_q1, lambda_k1, lambda_q2, lambda_k2, diff_norm_g,
           w_mla_proj, w_diff_proj, w_out, ffn_norm_g, w_ffn_gate, w_ffn_up,
           w_ffn_down, final_norm_g, _debug=False, _upto=99):
    x = np.asarray(x, np.float32)
    B, S, _ = x.shape
    So = S // 4
    nq = So // 256
    key = (S, _debug, _upto)
    if key not in _CACHE:
        _CACHE[key] = build_program(S, debug=_debug, upto=_upto)
    nc, P = _CACHE[key]
    cos2, sin2, per_core = position_tables(S)
    f = lambda a: np.ascontiguousarray(np.asarray(a, np.float32))
    shared = dict(
        w_in=f(w_in[0]), w_uq=f(w_uq[0]), w_ukv=f(w_ukv[0]), w_mla_proj=f(w_mla_proj[0]),
        w_diff_proj=f(w_diff_proj[0]), w_out=f(w_out[0]), w_ffn_gate=f(w_ffn_gate[0]), w_ffn_up=f(w_ffn_up[0]),
        w_ffn_down=f(w_ffn_down[0]),
        g_attn=_col(attn_norm_g[0], 16), g_q=_col(q_norm_g[0], 6), g_kv=_col(kv_norm_g[0], 4),
        g_ffn=_col(ffn_norm_g[0], 16), g_fin=_col(final_norm_g, 16), b_gate=_col(b_gate[0], 32),
        gsub=_rep(diff_norm_g[0]), lq1=_rep(lambda_q1[0]), lk1=_rep(lambda_k1[0]), lq2=_rep(lambda_q2[0]),
        lk2=_rep(lambda_k2[0]), cos_all=cos2, sin_all=sin2, ident=np.eye(128, dtype=np.float32))
    in_maps = []
    own_idx = []
    for core in range(8):
        b, c = core // 4, core % 4
        idx = np.concatenate([np.arange((4 * i + c) * 256, (4 * i + c + 1) * 256) for i in range(nq)])
        own_idx.append(idx)
        xT = np.ascontiguousarray(x[b].T)
        m = dict(shared)
        m.update(xall=xT, xown=np.ascontiguousarray(xT[:, idx]),
                 cos_own=np.ascontiguousarray(cos2[:, idx]), sin_own=np.ascontiguousarray(sin2[:, idx]),
                 abias=per_core[c]["abias"], mask_mla=per_core[c]["mask_mla"], mask_diff=per_core[c]["mask_diff"])
        in_maps.append(m)
    res = run_bass_kernel_spmd(nc, in_maps, core_ids=list(range(8)))
    out = np.empty((B, S, D), np.float32)
    for core in range(8):
        b = core // 4
        out[b, own_idx[core], :] = np.asarray(res.results[core]["outT"], np.float32).T
    if _debug:
        return out, res.results
    return out
```

```python
import math
from contextlib import ExitStack

import numpy as np
import concourse.bass as bass
import concourse.mybir as mybir
from concourse.bass_utils import run_bass_kernel_spmd

F32 = mybir.dt.float32
BF16 = mybir.dt.bfloat16
AF = mybir.ActivationFunctionType
ALU = mybir.AluOpType

D = 2048
NH_M = 16
NH_D = 8
QR = 768
KVR = 512
DFF = 5632
EPS = 1e-6
OFF_CQ, OFF_CKV, OFF_KPE, OFF_DQ, OFF_DK, OFF_DV, OFF_G = 0, 768, 1280, 1344, 3392, 5440, 7488
D_IN = 11584
LAMBDA_INIT = 0.8 - 0.6 * math.exp(-0.3 * 0)
ENGS = ["pe", "act", "dve", "pool", "sp"]
NDELTA = 66


class Res:
    __slots__ = ("w", "r", "dsem")

    def __init__(self):
        self.w = {}
        self.r = {}
        self.dsem = None


class Tl:
    __slots__ = ("t", "res")

    def __init__(self, t):
        self.t = t
        self.res = Res()


class Prog:
    def __init__(self, nc, n_dma_sems=80):
        self.nc = nc
        self.top = ExitStack()
        self.semobj = {}
        for e in ENGS:
            self.semobj["e:" + e] = self.top.enter_context(nc.semaphore("sem_" + e))
        self.cnt = {e: 0 for e in ENGS}
        self.dkeys = []
        self.dval = {}
        for i in range(n_dma_sems):
            k = "d:%d" % i
            self.semobj[k] = self.top.enter_context(nc.semaphore("semd_%d" % i))
            self.dkeys.append(k)
            self.dval[k] = 0
        self.known = {e: {} for e in ENGS}
        self.n_inst = 0
        self.uid = 0

    def begin(self):
        self.stack = ExitStack()
        self.ops = {e: [] for e in ENGS}
        self.free_d = list(self.dkeys)
        self.used_d = []
        self.rr = 0

    def sb(self, shape, dtype):
        self.uid += 1
        return Tl(self.stack.enter_context(self.nc.sbuf_tensor("sb%d" % self.uid, list(shape), dtype)))

    def ps(self, shape=(128, 512), dtype=F32):
        self.uid += 1
        return Tl(self.stack.enter_context(self.nc.psum_tensor("ps%d" % self.uid, list(shape), dtype)))

    def _need(self, eng, toks, waits):
        kn = self.known[eng]
        for k, v in toks.items():
            if eng == "pe" and k == "e:pe":
                continue
            if kn.get(k, 0) >= v:
                continue
            if waits.get(k, 0) < v:
                waits[k] = v

    def _deps(self, eng, reads, writes):
        waits = {}
        for r in reads:
            self._need(eng, r.res.w, waits)
        for w in writes:
            self._need(eng, w.res.w, waits)
            self._need(eng, w.res.r, waits)
        kn = self.known[eng]
        for k, v in waits.items():
            kn[k] = v
        return list(waits.items())

    def op(self, eng, fn, reads=(), writes=()):
        waits = self._deps(eng, reads, writes)
        self.cnt[eng] += 1
        key = "e:" + eng
        val = self.cnt[eng]
        for r in reads:
            r.res.r[key] = val
        for w in writes:
            w.res.w[key] = val
        self.ops[eng].append((waits, fn, (key, 1)))

    def dma(self, eng, out, in_, reads=(), writes=(), sem=None):
        waits = self._deps(eng, reads, writes)
        r = sem.res
        if r.dsem is None:
            r.dsem = self.free_d.pop() if eng != "pool" else self.free_d.pop(0)
            self.used_d.append(r.dsem)
        key = r.dsem
        self.dval[key] += 16
        val = self.dval[key]
        for x in reads:
            x.res.r[key] = val
        for x in writes:
            x.res.w[key] = val
        self.ops[eng].append((waits, lambda e: e.dma_start(out=out, in_=in_), (key, 16)))

    def end(self):
        for e in ENGS:
            waits = []
            for x in ENGS:
                k = "e:" + x
                if self.cnt[x] > self.known[e].get(k, 0):
                    waits.append((k, self.cnt[x]))
                    self.known[e][k] = self.cnt[x]
            if e == "sp":
                for k in self.used_d:
                    if self.dval[k] > self.known[e].get(k, 0):
                        waits.append((k, self.dval[k]))
            if waits:
                self.ops[e].append((waits, None, None))
        for e in ENGS:
            for k in self.used_d:
                self.known[e][k] = self.dval[k]
        ops = self.ops
        semobj = self.semobj

        def mk(name):
            def body(e):
                for waits, fn, inc in ops[name]:
                    for k, v in waits:
                        e.wait_ge(semobj[k], v)
                    if fn is not None:
                        ins = fn(e)
                        if inc is not None:
                            ins.then_inc(semobj[inc[0]], inc[1])
            return body

        for name in ENGS:
            self.n_inst += len(ops[name])
        with self.nc.Block() as block:
            block.tensor(mk("pe"))
            block.scalar(mk("act"))
            block.vector(mk("dve"))
            block.gpsimd(mk("pool"))
            block.sync(mk("sp"))
        self.stack.close()
        self.ops = None

    def evac_eng(self):
        self.rr += 1
        return "act" if (self.rr & 1) else "dve"


def copy_op(P, eng, out_ap, in_ap, reads, writes):
    if eng == "act":
        P.op("act", lambda e: e.activation(out=out_ap, in_=in_ap, func=AF.Copy), reads, writes)
    else:
        P.op("dve", lambda e: e.tensor_copy(out=out_ap, in_=in_ap), reads, writes)


def norm_phase(P, src, dst, gcol_dram, N, out_dtype, nfeat=D, T=256):
    KC = nfeat // 128
    P.begin()
    xt = [P.sb([128, KC, T], F32) for _ in range(3)]
    sq = [P.sb([128, KC, T], BF16) for _ in range(2)]
    st = [P.sb([128, KC, T], out_dtype) for _ in range(2)]
    ln = [P.sb([128, T], F32) for _ in range(2)]
    rs = [P.sb([128, T], F32) for _ in range(2)]
    ones = P.sb([128, 128], BF16)
    g = P.sb([128, KC], F32)
    epsc = P.sb([128, 1], F32)
    ps = [P.ps() for _ in range(2)]
    P.op("dve", lambda e: e.memset(ones.t[:], 1.0), (), (ones,))
    P.op("dve", lambda e: e.memset(epsc.t[:], EPS), (), (epsc,))
    P.dma("sp", g.t[:], gcol_dram, (), (g,), sem=g)
    srcv = src.rearrange("(kc p) n -> p kc n", p=128)
    dstv = dst.rearrange("(kc p) n -> p kc n", p=128)
    nt = N // T

    def load(i):
        s = xt[i % 3]
        P.dma("sp", s.t[:], srcv[:, :, i * T:(i + 1) * T], (), (s,), sem=s)

    def square(i):
        x, q = xt[i % 3], sq[i % 2]
        P.op("act", lambda e: e.activation(out=q.t[:], in_=x.t[:], func=AF.Square), (x,), (q,))

    load(0)
    if nt > 1:
        load(1)
    square(0)
    for i in range(nt):
        if i + 2 < nt:
            load(i + 2)
        x, q, o, l, r, p = xt[i % 3], sq[i % 2], st[i % 2], ln[i % 2], rs[i % 2], ps[i % 2]
        for kc in range(KC):
            P.op("pe", lambda e, p=p, q=q, kc=kc: e.matmul(p.t[:, 0:T], lhsT=ones.t[:], rhs=q.t[:, kc, :],
                                                          start=(kc == 0), stop=(kc == KC - 1)),
                 (ones, q), (p,))
        if i + 1 < nt:
            square(i + 1)
        P.op("act", lambda e, p=p, l=l: e.activation(out=l.t[:], in_=p.t[:, 0:T], func=AF.Ln,
                                                     scale=1.0 / nfeat, bias=epsc.t[:]), (p, epsc), (l,))
        P.op("act", lambda e, l=l, r=r: e.activation(out=r.t[:], in_=l.t[:], func=AF.Exp, scale=-0.5), (l,), (r,))
        for kc in range(KC):
            P.op("dve", lambda e, o=o, x=x, r=r, kc=kc: e.scalar_tensor_tensor(
                out=o.t[:, kc, :], in0=x.t[:, kc, :], scalar=g.t[:, kc:kc + 1], in1=r.t[:],
                op0=ALU.mult, op1=ALU.mult), (x, r, g), (o,))
        P.dma("sp", dstv[:, :, i * T:(i + 1) * T], o.t[:], (o,), (), sem=o)
    P.end()


def gemm_phase(P, act_srcs, KC, N, blocks, T=512, CB=512, setup=None, npsum=6, nwbuf=2):
    SEG = 512
    P.begin()
    nseg = (CB + SEG - 1) // SEG
    wt = [[P.sb([128, KC, min(SEG, CB - s * SEG)], BF16) for s in range(nseg)] for _ in range(nwbuf)]
    at = [P.sb([128, KC, T], BF16) for _ in range(2)]
    pss = [P.ps() for _ in range(npsum)]
    ctx = setup(P) if setup is not None else None
    nt = N // T
    seq = [(b, t) for b in range(len(blocks)) for t in range(nt)]
    psi = [0]

    def load_w(b):
        w = wt[b % nwbuf]
        for (wap, kc0, nkc, off, ncols) in blocks[b]["wsegs"]:
            if isinstance(wap, int):
                sg, so, ss = off // SEG, off % SEG, wap % SEG
                assert wap // SEG == sg
                ws = w[sg]
                P.op("dve", lambda e, ws=ws, kc0=kc0, nkc=nkc, so=so, ss=ss, ncols=ncols: e.tensor_copy(
                    out=ws.t[:, kc0:kc0 + nkc, so:so + ncols], in_=ws.t[:, kc0:kc0 + nkc, ss:ss + ncols]), (ws,), (ws,))
                continue
            c = 0
            while c < ncols:
                o = off + c
                sg, so = o // SEG, o % SEG
                n = min(ncols - c, SEG - so)
                ws = w[sg]
                P.dma("pool", ws.t[:, kc0:kc0 + nkc, so:so + n],
                      wap[:, c:c + n].rearrange("(kc p) n -> p kc n", p=128), (), (ws,), sem=ws)
                c += n

    def load_a(j):
        a = at[j % 2]
        t = seq[j][1]
        for (aap, kc0, nkc) in act_srcs:
            P.dma("sp", a.t[:, kc0:kc0 + nkc, :],
                  aap.rearrange("(kc p) n -> p kc n", p=128)[:, :, t * T:(t + 1) * T], (), (a,), sem=a)

    load_w(0)
    load_a(0)
    for j, (b, t) in enumerate(seq):
        if t == 0:
            if nwbuf == 2 and b + 1 < len(blocks):
                load_w(b + 1)
            if nwbuf == 1 and b > 0:
                load_w(b)
        if j + 1 < len(seq):
            load_a(j + 1)
        w = wt[b % nwbuf]
        a = at[j % 2]
        for grp in blocks[b]["groups"]:
            accs = grp["accs"]
            if accs[0][0] == "fm":
                pts = []
                for (kind, off, ncols, klo, khi) in accs:
                    p = pss[psi[0] % npsum]
                    psi[0] += 1
                    ws, so = w[off // SEG], off % SEG
                    for kc in range(klo, khi):
                        P.op("pe", lambda e, p=p, ws=ws, a=a, kc=kc, so=so, ncols=ncols, klo=klo, khi=khi:
                             e.matmul(p.t[0:ncols, 0:T], lhsT=ws.t[:, kc, so:so + ncols], rhs=a.t[:, kc, :],
                                      start=(kc == klo), stop=(kc == khi - 1)), (ws, a), (p,))
                    pts.append(p)
                grp["epi"](P, ctx, t * T, T, pts, grp)
            else:
                (kind, off, ncols, klo, khi) = accs[0]
                ws, so = w[off // SEG], off % SEG
                for ts in range(T // 128):
                    p = pss[psi[0] % npsum]
                    psi[0] += 1
                    for kc in range(klo, khi):
                        P.op("pe", lambda e, p=p, ws=ws, a=a, kc=kc, so=so, ncols=ncols, klo=klo, khi=khi, ts=ts:
                             e.matmul(p.t[:, 0:ncols], lhsT=a.t[:, kc, ts * 128:(ts + 1) * 128],
                                      rhs=ws.t[:, kc, so:so + ncols],
                                      start=(kc == klo), stop=(kc == khi - 1)), (ws, a), (p,))
                    grp["epi"](P, ctx, t * T + ts * 128, 128, [p], grp)
    P.end()


class Stager:
    def __init__(self, P, shape, dtype, n=4):
        self.tl = [P.sb(shape, dtype) for _ in range(n)]
        self.i = 0

    def get(self):
        t = self.tl[self.i % len(self.tl)]
        self.i += 1
        return t


def epi_store_fm(dst, row_of):
    def epi(P, ctx, tok0, ntok, pts, grp):
        p = pts[0]
        ncols = grp["accs"][0][2]
        s = ctx["stg"].get()
        copy_op(P, P.evac_eng(), s.t[0:ncols, 0:ntok], p.t[0:ncols, 0:ntok], (p,), (s,))
        r0 = grp["row0"]
        P.dma("sp", dst[r0:r0 + ncols, tok0:tok0 + ntok], s.t[0:ncols, 0:ntok], (s,), (), sem=s)
    return epi


def epi_store_fm32(dst):
    def epi(P, ctx, tok0, ntok, pts, grp):
        p = pts[0]
        ncols = grp["accs"][0][2]
        s = ctx["stg32"].get()
        copy_op(P, P.evac_eng(), s.t[0:ncols, 0:ntok], p.t[0:ncols, 0:ntok], (p,), (s,))
        r0 = grp["row0"]
        P.dma("sp", dst[r0:r0 + ncols, tok0:tok0 + ntok], s.t[0:ncols, 0:ntok], (s,), (), sem=s)
    return epi


def epi_store_tm(dst):
    def epi(P, ctx, tok0, ntok, pts, grp):
        p = pts[0]
        ncols = grp["accs"][0][2]
        s = ctx["stg"].get()
        copy_op(P, P.evac_eng(), s.t[:, 0:ncols], p.t[:, 0:ncols], (p,), (s,))
        c0 = grp["col0"]
        P.dma("sp", dst[tok0:tok0 + 128, c0:c0 + ncols], s.t[:, 0:ncols], (s,), (), sem=s)
    return epi


def setup_stg(dtype=BF16, n=4):
    def setup(P):
        return {"stg": Stager(P, [128, 512], dtype, n)}
    return setup


def epi_gate(dst):
    def epi(P, ctx, tok0, ntok, pts, grp):
        p = pts[0]
        s = ctx["stg"].get()
        bcol = grp["bcol"]
        bg = ctx["bg"]
        P.op("act", lambda e: e.activation(out=s.t[:, 0:ntok], in_=p.t[:, 0:ntok], func=AF.Sigmoid,
                                           bias=bg.t[:, bcol:bcol + 1]), (p, bg), (s,))
        r0 = grp["row0"]
        P.dma("sp", dst[r0:r0 + 128, tok0:tok0 + ntok], s.t[:, 0:ntok], (s,), (), sem=s)
    return epi


def epi_rope(dst, cos_d, sin_d):
    def epi(P, ctx, tok0, ntok, pts, grp):
        pa, pb = pts
        k = ctx["cs_i"]
        ctx["cs_i"] += 1
        ct, sn = ctx["cos"][k % 2], ctx["sin"][k % 2]
        P.dma("sp", ct.t[0:64, 0:ntok], cos_d[:, tok0:tok0 + ntok], (), (ct,), sem=ct)
        P.dma("sp", sn.t[0:64, 0:ntok], sin_d[:, tok0:tok0 + ntok], (), (sn,), sem=sn)
        t1, t2 = ctx["rt1"][k % 2], ctx["rt2"][k % 2]
        P.op("dve", lambda e: e.tensor_tensor(out=t1.t[0:64, 0:ntok], in0=pa.t[0:64, 0:ntok],
                                              in1=ct.t[0:64, 0:ntok], op=ALU.mult), (pa, ct), (t1,))
        P.op("dve", lambda e: e.tensor_tensor(out=t2.t[0:64, 0:ntok], in0=pb.t[0:64, 0:ntok],
                                              in1=sn.t[0:64, 0:ntok], op=ALU.mult), (pb, sn), (t2,))
        s = ctx["stg"].get()
        P.op("dve", lambda e: e.tensor_tensor(out=s.t[0:64, 0:ntok], in0=t1.t[0:64, 0:ntok],
                                               in1=t2.t[0:64, 0:ntok], op=ALU.add), (t1, t2), (s,))
        r0 = grp["row0"]
        P.dma("sp", dst[r0:r0 + 64, tok0:tok0 + ntok], s.t[0:64, 0:ntok], (s,), (), sem=s)
    return epi


def setup_rope(extra=None):
    def setup(P):
        ctx = {"stg": Stager(P, [128, 512], BF16, 4), "stg32": Stager(P, [128, 512], F32, 4), "cs_i": 0,
               "cos": [P.sb([128, 512], F32) for _ in range(2)],
               "sin": [P.sb([128, 512], F32) for _ in range(2)],
               "rt1": [P.sb([128, 512], F32) for _ in range(2)],
               "rt2": [P.sb([128, 512], F32) for _ in range(2)]}
        if extra is not None:
            extra(P, ctx)
        return ctx
    return setup


def epi_resid(dst, resid):
    def epi(P, ctx, tok0, ntok, pts, grp):
        p = pts[0]
        r0 = grp["row0"]
        k = ctx["r_i"]
        ctx["r_i"] += 1
        rt = ctx["rt"][k % 3]
        P.dma("sp", rt.t[:, 0:ntok], resid[r0:r0 + 128, tok0:tok0 + ntok], (), (rt,), sem=rt)
        s = ctx["stg"].get()
        P.op("dve", lambda e: e.tensor_tensor(out=s.t[:, 0:ntok], in0=p.t[:, 0:ntok], in1=rt.t[:, 0:ntok],
                                              op=ALU.add), (p, rt), (s,))
        P.dma("sp", dst[r0:r0 + 128, tok0:tok0 + ntok], s.t[:, 0:ntok], (s,), (), sem=s)
    return epi


def setup_resid(P):
    return {"stg": Stager(P, [128, 512], F32, 3), "r_i": 0, "rt": [P.sb([128, 512], F32) for _ in range(3)]}


def epi_merge(dst, gT):
    def epi(P, ctx, tok0, ntok, pts, grp):
        pa, pb = pts
        r0 = grp["row0"]
        k = ctx["r_i"]
        ctx["r_i"] += 1
        g0, g1 = ctx["g0"][k % 2], ctx["g1"][k % 2]
        t1, t2 = ctx["t1"][k % 2], ctx["t2"][k % 2]
        P.dma("sp", g0.t[:, 0:ntok], gT[r0:r0 + 128, tok0:tok0 + ntok], (), (g0,), sem=g0)
        P.dma("sp", g1.t[:, 0:ntok], gT[D + r0:D + r0 + 128, tok0:tok0 + ntok], (), (g1,), sem=g1)
        P.op("dve", lambda e: e.tensor_tensor(out=t1.t[:, 0:ntok], in0=pa.t[:, 0:ntok], in1=g0.t[:, 0:ntok],
                                              op=ALU.mult), (pa, g0), (t1,))
        P.op("dve", lambda e: e.tensor_tensor(out=t2.t[:, 0:ntok], in0=pb.t[:, 0:ntok], in1=g1.t[:, 0:ntok],
                                              op=ALU.mult), (pb, g1), (t2,))
        s = ctx["stg"].get()
        P.op("dve", lambda e: e.tensor_tensor(out=s.t[:, 0:ntok], in0=t1.t[:, 0:ntok], in1=t2.t[:, 0:ntok],
                                               op=ALU.add), (t1, t2), (s,))
        P.dma("sp", dst[r0:r0 + 128, tok0:tok0 + ntok], s.t[:, 0:ntok], (s,), (), sem=s)
    return epi


def setup_merge(P):
    return {"stg": Stager(P, [128, 512], BF16, 3), "r_i": 0,
            "g0": [P.sb([128, 512], F32) for _ in range(2)], "g1": [P.sb([128, 512], F32) for _ in range(2)],
            "t1": [P.sb([128, 512], F32) for _ in range(2)], "t2": [P.sb([128, 512], F32) for _ in range(2)]}


def epi_swiglu(dst):
    def epi(P, ctx, tok0, ntok, pts, grp):
        pg, pu = pts
        r0 = grp["row0"]
        k = ctx["r_i"]
        ctx["r_i"] += 1
        t1 = ctx["t1"][k % 3]
        P.op("act", lambda e: e.activation(out=t1.t[:, 0:ntok], in_=pg.t[:, 0:ntok], func=AF.Silu), (pg,), (t1,))
        s = ctx["stg"].get()
        P.op("dve", lambda e: e.tensor_tensor(out=s.t[:, 0:ntok], in0=pu.t[:, 0:ntok], in1=t1.t[:, 0:ntok],
                                              op=ALU.mult), (pu, t1), (s,))
        P.dma("sp", dst[r0:r0 + 128, tok0:tok0 + ntok], s.t[:, 0:ntok], (s,), (), sem=s)
    return epi


def setup_swiglu(P):
    return {"stg": Stager(P, [128, 512], BF16, 3), "r_i": 0, "t1": [P.sb([128, 512], F32) for _ in range(3)]}


def latent_blocks(w_in, off, nlat, rawdst, rope=None):
    ncl = nlat * 128
    wsegs = [(w_in[:, off:off + ncl], 0, 16, 0, ncl)]
    groups = [dict(accs=[("fm", m * 128, 128, 0, 16)], row0=m * 128, epi=epi_store_fm32(rawdst)) for m in range(nlat)]
    CB = ncl
    if rope is not None:
        (rdst, cos_d, sin_d, roff) = rope
        wsegs.append((w_in[:, roff:roff + 64], 0, 16, ncl, 64))
        wsegs.append((ncl + 32, 0, 16, ncl + 64, 32))
        wsegs.append((ncl, 0, 16, ncl + 96, 32))
        groups.append(dict(accs=[("fm", ncl, 64, 0, 16), ("fm", ncl + 64, 64, 0, 16)], row0=0,
                           epi=epi_rope(rdst, cos_d, sin_d)))
        CB = ncl + 128
    return [dict(wsegs=wsegs, groups=groups)], setup_rope(), CB


def attn_phase(P, S, So, units, consts, kind):
    nq = So // 256
    nkt_all = S // 128
    dvv = 128 if kind == "mla" else 256
    P.begin()
    if kind == "mla":
        kn = [P.sb([128, S], BF16) for _ in range(2)]
        qn = [P.sb([128, So], BF16) for _ in range(2)]
    else:
        kn = [P.sb([128, 2, S], BF16) for _ in range(2)]
        qn = [P.sb([128, 2, So], BF16) for _ in range(2)]
    vt = [P.sb([128, nkt_all, 257], BF16) for _ in range(2)]
    ident = P.sb([128, 128], BF16)
    P.dma("pool", ident.t[:], consts["ident"], (), (ident,), sem=ident)
    for v in vt:
        if kind == "mla":
            P.op("pool", lambda e, v=v: e.memset(v.t[:, :, 128:257], 0.0), (), (v,))
        P.op("pool", lambda e, v=v: e.memset(v.t[:, :, dvv:dvv + 1], 1.0), (), (v,))
    nS = 5 if kind == "mla" else 4
    LA = nS - 2
    NPT = LA + 3
    pt = [P.sb([128, 512], BF16) for _ in range(NPT)]
    sbank = [P.ps() for _ in range(nS)]
    if kind == "mla":
        tpb = P.ps([128, 1024], BF16)
        tpr = [Res(), Res()]
    rsm = [P.sb([128, 1], F32) for _ in range(4)]
    if kind == "mla":
        kr = P.sb([128, S], BF16)
        qr = [P.sb([128, So], BF16) for _ in range(2)]
        P.op("pool", lambda e: e.memset(kr.t[64:128, :], 0.0), (), (kr,))
        for q_ in qr:
            P.op("pool", lambda e, q_=q_: e.memset(q_.t[64:128, :], 0.0), (), (q_,))
        P.dma("sp", kr.t[0:64, :], consts["krT"], (), (kr,), sem=kr)
        msk = P.sb([128, 2048], F32)
        P.dma("sp", msk.t[:], consts["mask_mla"].rearrange("p a b -> p (a b)"), (), (msk,), sem=msk)
        obank = [P.ps() for _ in range(2)]
        ybf = [P.sb([128, 128], BF16) for _ in range(4)]
        ystg = [P.sb([128, 256], BF16) for _ in range(2)]
    else:
        mskd = [P.sb([128, 2048], F32) for _ in range(2)]
        abias = P.sb([128, NH_D * 80], F32)
        P.dma("sp", abias.t[:], consts["abias"], (), (abias,), sem=abias)
        obank = [P.ps() for _ in range(4)]
        o1n = [P.sb([128, 256], F32) for _ in range(2)]
        ofull = [P.sb([128, 256], F32) for _ in range(2)]
        junk = P.sb([128, 256], F32)
        ybf = [P.sb([128, 256], BF16) for _ in range(2)]
        ystg = [P.sb([128, 2, 256], BF16) for _ in range(2)]
        r2l = [P.sb([128, 1], F32) for _ in range(2)]
        ssq = [P.sb([128, 1], F32) for _ in range(2)]
        lnv = [P.sb([128, 1], F32) for _ in range(2)]
        rstd = [P.sb([128, 1], F32) for _ in range(2)]
        epsc = P.sb([128, 1], F32)
        P.op("dve", lambda e: e.memset(epsc.t[:], EPS), (), (epsc,))
        gsub = P.sb([128, 256], F32)
        P.dma("sp", gsub.t[:], consts["gsub"], (), (gsub,), sem=gsub)
        P.op("act", lambda e: e.activation(out=gsub.t[:], in_=gsub.t[:], func=AF.Copy, scale=1.0 - LAMBDA_INIT),
             (gsub,), (gsub,))
        lt = [P.sb([128, 128], F32) for _ in range(4)]
        for i, nm in enumerate(["lq1", "lk1", "lq2", "lk2"]):
            P.dma("sp", lt[i].t[:], consts[nm], (), (lt[i],), sem=lt[i])
        pr = [P.sb([128, 128], F32) for _ in range(2)]
        sm = [P.sb([128, 1], F32) for _ in range(2)]
        ex = [P.sb([128, 1], F32) for _ in range(2)]
        neglam = P.sb([128, 1], F32)
        for i in range(2):
            P.op("dve", lambda e, i=i: e.tensor_tensor(out=pr[i].t[:], in0=lt[2 * i].t[:], in1=lt[2 * i + 1].t[:],
                                                       op=ALU.mult), (lt[2 * i], lt[2 * i + 1]), (pr[i],))
            P.op("act", lambda e, i=i: e.activation(out=junk.t[:, 0:128], in_=pr[i].t[:], func=AF.Copy,
                                                    accum_out=sm[i].t[:]), (pr[i],), (junk, sm[i]))
            P.op("act", lambda e, i=i: e.activation(out=ex[i].t[:], in_=sm[i].t[:], func=AF.Exp), (sm[i],), (ex[i],))
        P.op("dve", lambda e: e.tensor_tensor(out=neglam.t[:], in0=ex[1].t[:], in1=ex[0].t[:], op=ALU.subtract),
             (ex[0], ex[1]), (neglam,))
        P.op("dve", lambda e: e.tensor_scalar(out=neglam.t[:], in0=neglam.t[:], scalar1=-LAMBDA_INIT, scalar2=None,
                                              op0=ALU.add), (neglam,), (neglam,))

    vloaded = {}

    def load_unit(u):
        un = units[u]
        b = u % 2
        if kind == "mla":
            P.dma("sp", kn[b].t[:], un["kT"], (), (kn[b],), sem=kn[b])
            P.dma("sp", qn[b].t[:], un["qT"], (), (qn[b],), sem=qn[b])
            P.dma("sp", qr[b].t[0:64, :], un["qrT"], (), (qr[b],), sem=qr[b])
        else:
            for m in range(2):
                P.dma("sp", kn[b].t[:, m, :], un["kT"][m], (), (kn[b],), sem=kn[b])
                P.dma("sp", qn[b].t[:, m, :], un["qT"][m], (), (qn[b],), sem=qn[b])
        vk = un["vkey"]
        if vk not in vloaded:
            vb = len(vloaded) % 2
            vloaded[vk] = vb
            vv = un["v"].rearrange("(t p) c -> p t c", p=128)
            step = 16
            for t0 in range(0, nkt_all, step):
                t1 = min(nkt_all, t0 + step)
                P.dma("sp", vt[vb].t[:, t0:t1, 0:dvv], vv[:, t0:t1, :], (), (vt[vb],), sem=vt[vb])
            if kind == "diff":
                P.dma("sp", mskd[vb].t[:], consts["mask_diff"][un["h"]].rearrange("p a b -> p (a b)"), (),
                      (mskd[vb],), sem=mskd[vb])

    pairs = []
    for u in range(len(units)):
        for i in range(nq):
            if kind == "mla":
                ks = list(range(4 * (i + 1)))
            else:
                slope = 2.0 ** (-(units[u]["h"] + 1))
                ks = [kt for kt in range(8 * (i + 1)) if slope * ((8 * i - kt) * 128 - 127) < 140.0]
            for n_, k_ in enumerate(ks):
                pairs.append((u, i, k_, n_ == 0, n_ == len(ks) - 1))
    cnt = {"s": 0, "p": 0, "e": 0}
    pend = {}

    def do_qk(n):
        (u, i, kx, first, last) = pairs[n]
        un = units[u]
        b = u % 2
        sp_ = sbank[cnt["s"] % nS]
        cnt["s"] += 1
        q0 = i * 256
        p = pt[cnt["p"] % NPT]
        cnt["p"] += 1
        sc = un["scale"]
        if kind == "mla":
            nk2 = 4 * (i + 1)
            for j in range(2):
                k0 = (2 * kx + j) * 128
                P.op("pe", lambda e, j=j, k0=k0: e.matmul(
                    sp_.t[:, j * 256:(j + 1) * 256], lhsT=kn[b].t[:, k0:k0 + 128], rhs=qn[b].t[:, q0:q0 + 256],
                    start=True, stop=False), (kn[b], qn[b]), (sp_,))
                P.op("pe", lambda e, j=j, k0=k0: e.matmul(
                    sp_.t[:, j * 256:(j + 1) * 256], lhsT=kr.t[:, k0:k0 + 128], rhs=qr[b].t[:, q0:q0 + 256],
                    start=False, stop=True), (kr, qr[b]), (sp_,))
            P.op("act", lambda e: e.activation(out=p.t[:], in_=sp_.t[:, 0:512], func=AF.Exp, scale=sc), (sp_,), (p,))
            if kx >= nk2 - 4:
                kk = 2 * (kx - (nk2 - 4))
                P.op("dve", lambda e: e.tensor_tensor(out=p.t[:], in0=p.t[:], in1=msk.t[:, kk * 256:(kk + 2) * 256],
                                                      op=ALU.mult), (p, msk), (p,))
        else:
            h = un["h"]
            k0 = kx * 128
            for m in range(2):
                P.op("pe", lambda e, m=m: e.matmul(
                    sp_.t[:, m * 256:(m + 1) * 256], lhsT=kn[b].t[:, m, k0:k0 + 128], rhs=qn[b].t[:, m, q0:q0 + 256],
                    start=True, stop=True), (kn[b], qn[b]), (sp_,))
            dst = kx - 8 * i
            if h == 0:
                for jq in range(2):
                    bi = h * 80 + dst - 2 * jq + 66
                    c0 = jq * 128
                    P.op("act", lambda e, bi=bi, c0=c0: e.activation(
                        out=p.t[:, 0:512].rearrange("p (m q) -> p m q", m=2)[:, :, c0:c0 + 128],
                        in_=sp_.t[:, 0:512].rearrange("p (m q) -> p m q", m=2)[:, :, c0:c0 + 128], func=AF.Exp,
                        scale=sc, bias=abias.t[:, bi:bi + 1]), (sp_, abias), (p,))
            else:
                bi = h * 80 + dst + 66
                P.op("act", lambda e, bi=bi: e.activation(out=p.t[:], in_=sp_.t[:, 0:512], func=AF.Exp, scale=sc,
                                                          bias=abias.t[:, bi:bi + 1]), (sp_, abias), (p,))
            if kx >= 8 * i:
                kk = kx - 8 * i
                mk_ = mskd[vloaded[un["vkey"]]]
                for m in range(2):
                    P.op("dve", lambda e, m=m: e.tensor_tensor(
                        out=p.t[:, m * 256:(m + 1) * 256], in0=p.t[:, m * 256:(m + 1) * 256],
                        in1=mk_.t[:, kk * 256:(kk + 1) * 256], op=ALU.mult), (p, mk_), (p,))
        pend[n] = p

    def do_pv(n):
        (u, i, kx, first, last) = pairs[n]
        un = units[u]
        p = pend.pop(n)
        if i == 0 and first and u + 1 < len(units):
            load_unit(u + 1)
        v = vt[vloaded[un["vkey"]]]
        ei = u * nq + i
        h = un["h"]
        if kind == "mla":
            for j in range(2):
                for qs in range(2):
                    P.op("pe", lambda e, qs=qs, j=j: e.matmul(
                        obank[qs].t[:, 0:129], lhsT=p.t[:, j * 256 + qs * 128:j * 256 + (qs + 1) * 128],
                        rhs=v.t[:, 2 * kx + j, 0:129], start=(first and j == 0), stop=(last and j == 1)),
                        (p, v), (obank[qs],))
            if last:
                ys = ystg[ei % 2]
                tb = _R(tpr[ei % 2])
                tpv = tpb.t[:, (ei % 2) * 256:(ei % 2) * 256 + 256]
                for qs in range(2):
                    ob = obank[qs]
                    r = rsm[cnt["e"] % 4]
                    yb = ybf[cnt["e"] % 4]
                    cnt["e"] += 1
                    P.op("dve", lambda e, r=r, ob=ob: e.reciprocal(out=r.t[:], in_=ob.t[:, 128:129]), (ob,), (r,))
                    P.op("dve", lambda e, r=r, yb=yb, ob=ob: e.tensor_scalar(
                        out=yb.t[:], in0=ob.t[:, 0:128], scalar1=r.t[:], scalar2=None, op0=ALU.mult), (ob, r), (yb,))
                    c0 = qs * 128
                    P.op("pe", lambda e, yb=yb, c0=c0: e.transpose(tpv[:, c0:c0 + 128], yb.t[:], ident.t[:]),
                         (yb, ident), (tb,))
                copy_op(P, "dve", ys.t[:], tpv[:, 0:256], (tb,), (ys,))
                P.dma("sp", consts["yT"][h * 128:(h + 1) * 128, i * 256:(i + 1) * 256], ys.t[:], (ys,), (), sem=ys)
        else:
            for m in range(2):
                for qs in range(2):
                    ob = obank[m * 2 + qs]
                    P.op("pe", lambda e, qs=qs, m=m, ob=ob: e.matmul(
                        ob.t[:, 0:257], lhsT=p.t[:, m * 256 + qs * 128:m * 256 + (qs + 1) * 128],
                        rhs=v.t[:, kx, :], start=first, stop=last), (p, v), (ob,))
            if last:
                ys = ystg[ei % 2]
                tb = sbank[cnt["s"] % nS]
                cnt["s"] += 1
                tpv = tb.t[:, 0:256].bitcast(BF16)
                for qs in range(2):
                    o1b, o2b = obank[qs], obank[2 + qs]
                    r1 = rsm[cnt["e"] % 4]
                    r2 = rsm[(cnt["e"] + 1) % 4]
                    cnt["e"] += 2
                    k2 = qs
                    t1, rl, of, sq_, ln_, rs_, yb = o1n[k2], r2l[k2], ofull[k2], ssq[k2], lnv[k2], rstd[k2], ybf[k2]
                    P.op("dve", lambda e, r1=r1, o1b=o1b: e.reciprocal(out=r1.t[:], in_=o1b.t[:, 256:257]), (o1b,), (r1,))
                    P.op("dve", lambda e, r1=r1, o1b=o1b, t1=t1: e.tensor_scalar(
                        out=t1.t[:], in0=o1b.t[:, 0:256], scalar1=r1.t[:], scalar2=None, op0=ALU.mult),
                        (o1b, r1), (t1,))
                    P.op("dve", lambda e, r2=r2, o2b=o2b: e.reciprocal(out=r2.t[:], in_=o2b.t[:, 256:257]), (o2b,), (r2,))
                    P.op("dve", lambda e, r2=r2, rl=rl: e.tensor_tensor(out=rl.t[:], in0=r2.t[:], in1=neglam.t[:],
                                                                        op=ALU.mult), (r2, neglam), (rl,))
                    P.op("dve", lambda e, rl=rl, of=of, o2b=o2b, t1=t1: e.scalar_tensor_tensor(
                        out=of.t[:], in0=o2b.t[:, 0:256], scalar=rl.t[:], in1=t1.t[:],
                        op0=ALU.mult, op1=ALU.add), (o2b, rl, t1), (of,))
                    P.op("act", lambda e, of=of, sq_=sq_: e.activation(out=junk.t[:], in_=of.t[:], func=AF.Square,
                                                                       accum_out=sq_.t[:]), (of,), (junk, sq_))
                    P.op("act", lambda e, sq_=sq_, ln_=ln_: e.activation(out=ln_.t[:], in_=sq_.t[:], func=AF.Ln,
                                                                         scale=1.0 / 256, bias=epsc.t[:]),
                         (sq_, epsc), (ln_,))
                    P.op("act", lambda e, ln_=ln_, rs_=rs_: e.activation(out=rs_.t[:], in_=ln_.t[:], func=AF.Exp,
                                                                         scale=-0.5), (ln_,), (rs_,))
                    P.op("dve", lambda e, yb=yb, of=of, rs_=rs_: e.scalar_tensor_tensor(
                        out=yb.t[:], in0=of.t[:], scalar=rs_.t[:], in1=gsub.t[:], op0=ALU.mult, op1=ALU.mult),
                        (of, rs_, gsub), (yb,))
                    for c in range(2):
                        c0 = c * 256 + qs * 128
                        P.op("pe", lambda e, yb=yb, c=c, c0=c0: e.transpose(
                            tpv[:, c0:c0 + 128], yb.t[:, c * 128:(c + 1) * 128], ident.t[:]),
                            (yb, ident), (tb,))
                copy_op(P, "dve", ys.t[:, :, :], tpv[:, 0:512].rearrange("p (c q) -> p c q", c=2), (tb,), (ys,))
                P.dma("sp", consts["yT"][h * 256:(h + 1) * 256, i * 256:(i + 1) * 256].rearrange(
                    "(c p) q -> p c q", p=128), ys.t[:, :, :], (ys,), (), sem=ys)

    load_unit(0)
    for n in range(len(pairs) + LA):
        if n < len(pairs):
            do_qk(n)
        if n >= LA:
            do_pv(n - LA)
    P.end()


class _R:
    __slots__ = ("res",)

    def __init__(self, res):
        self.res = res


def build_program(S, debug=False, upto=99):
    So = S // 4
    To = min(512, So)
    nc = bass.Bass("TRN2", target_bir_lowering=False)
    P = Prog(nc)

    def din(name, shape, dt=F32):
        return nc.dram_tensor(name, list(shape), dt, kind="ExternalInput").ap()

    def scr(name, shape, dt):
        kind = "ExternalOutput" if debug else "Internal"
        return nc.dram_tensor(name, list(shape), dt, kind=kind).ap()

    xall = din("xall", [D, S])
    xown = din("xown", [D, So])
    w_in = din("w_in", [D, D_IN])
    w_uq = din("w_uq", [QR, NH_M * 192])
    w_ukv = din("w_ukv", [KVR, NH_M * 256])
    w_mla = din("w_mla_proj", [D, D])
    w_dif = din("w_diff_proj", [D, D])
    w_out = din("w_out", [D, D])
    w_fg = din("w_ffn_gate", [D, DFF])
    w_fu = din("w_ffn_up", [D, DFF])
    w_fd = din("w_ffn_down", [DFF, D])
    g_attn = din("g_attn", [128, 16])
    g_q = din("g_q", [128, 6])
    g_kv = din("g_kv", [128, 4])
    g_ffn = din("g_ffn", [128, 16])
    g_fin = din("g_fin", [128, 16])
    b_gate = din("b_gate", [128, 32])
    gsub = din("gsub", [128, 256])
    lq1 = din("lq1", [128, 128])
    lk1 = din("lk1", [128, 128])
    lq2 = din("lq2", [128, 128])
    lk2 = din("lk2", [128, 128])
    cos_all = din("cos_all", [64, S])
    sin_all = din("sin_all", [64, S])
    cos_own = din("cos_own", [64, So])
    sin_own = din("sin_own", [64, So])
    abias = din("abias", [128, NH_D * 80])
    mask_mla = din("mask_mla", [128, 8, 256])
    mask_diff = din("mask_diff", [NH_D, 128, 8, 256])
    ident = din("ident", [128, 128])

    hT_all = scr("hT_all", [D, S], BF16)
    hT_own = scr("hT_own", [D, So], BF16)
    dkT = scr("dkT", [D, S], BF16)
    dv = scr("dv", [S, D], BF16)
    ckvnT = scr("ckvnT", [KVR, S], BF16)
    ckvraw = scr("ckvraw", [KVR, S], F32)
    cqraw = scr("cqraw", [QR, So], F32)
    krT = scr("krT", [64, S], BF16)
    knT = scr("knT", [D, S], BF16)
    vm = scr("vm", [S, D], BF16)
    dqT = scr("dqT", [D, So], BF16)
    gT = scr("gT", [2 * D, So], F32)
    cqnT = scr("cqnT", [QR, So], BF16)
    qnT = scr("qnT", [D, So], BF16)
    qrT = scr("qrT", [NH_M * 64, So], BF16)
    ymT = scr("ymT", [D, So], BF16)
    ydT = scr("ydT", [D, So], BF16)
    mT = scr("mT", [D, So], BF16)
    x1T = scr("x1T", [D, So], F32)
    h2T = scr("h2T", [D, So], BF16)
    aT = scr("aT", [DFF, So], BF16)
    x2T = scr("x2T", [D, So], F32)
    outT = nc.dram_tensor("outT", [D, So], F32, kind="ExternalOutput").ap()

    def done():
        P.top.close()
        return nc, P

    norm_phase(P, xall, hT_all, g_attn, S, BF16)
    norm_phase(P, xown, hT_own, g_attn, So, BF16)
    if upto <= 0:
        return done()

    blocks = [dict(wsegs=[(w_in[:, OFF_DK:OFF_DK + 2048], 0, 16, 0, 2048)],
                   groups=[dict(accs=[("fm", m * 128, 128, 0, 16)], row0=m * 128, epi=epi_store_fm(dkT, None))
                           for m in range(16)]),
              dict(wsegs=[(w_in[:, OFF_DV:OFF_DV + 2048], 0, 16, 0, 2048)],
                   groups=[dict(accs=[("tm", cb * 512, 512, 0, 16)], col0=cb * 512, epi=epi_store_tm(dv))
                           for cb in range(4)])]
    gemm_phase(P, [(hT_all, 0, 16)], 16, S, blocks, CB=2048, setup=setup_stg())
    if upto <= 0.3:
        return done()
    blocks, setup, CB = latent_blocks(w_in, OFF_CKV, 4, ckvraw, rope=(krT, cos_all, sin_all, OFF_KPE))
    gemm_phase(P, [(hT_all, 0, 16)], 16, S, blocks, CB=CB, setup=setup)
    norm_phase(P, ckvraw, ckvnT, g_kv, S, BF16, nfeat=KVR)

    if upto <= 0.6:
        return done()
    groups = []
    for h in range(NH_M):
        groups.append(dict(accs=[("fm", h * 256, 128, 0, 4)], row0=h * 128, epi=epi_store_fm(knT, None)))
        groups.append(dict(accs=[("tm", h * 256 + 128, 128, 0, 4)], col0=h * 128, epi=epi_store_tm(vm)))
    blocks = [dict(wsegs=[(w_ukv[:, :], 0, 4, 0, 4096)], groups=groups)]
    gemm_phase(P, [(ckvnT, 0, 4)], 4, S, blocks, CB=4096, nwbuf=1, setup=setup_stg())
    if upto <= 1:
        return done()

    blocks = [dict(wsegs=[(w_in[:, OFF_DQ:OFF_DQ + 2048], 0, 16, 0, 2048)],
                   groups=[dict(accs=[("fm", m * 128, 128, 0, 16)], row0=m * 128, epi=epi_store_fm(dqT, None))
                           for m in range(16)])]
    gemm_phase(P, [(hT_own, 0, 16)], 16, So, blocks, T=To, CB=2048, nwbuf=1, setup=setup_stg())

    blocks = []
    for cb in range(2):
        c0 = OFF_G + cb * 2048
        blocks.append(dict(
            wsegs=[(w_in[:, c0:c0 + 2048], 0, 16, 0, 2048)],
            groups=[dict(accs=[("fm", m * 128, 128, 0, 16)], row0=cb * 2048 + m * 128, bcol=cb * 16 + m,
                         epi=epi_gate(gT)) for m in range(16)]))

    def setup_gate(P):
        ctx = {"stg": Stager(P, [128, 512], F32, 4), "bg": P.sb([128, 32], F32)}
        P.dma("sp", ctx["bg"].t[:], b_gate, (), (ctx["bg"],), sem=ctx["bg"])
        return ctx
    gemm_phase(P, [(hT_own, 0, 16)], 16, So, blocks, T=To, CB=2048, setup=setup_gate)

    blocks, setup, CB = latent_blocks(w_in, OFF_CQ, 6, cqraw)
    gemm_phase(P, [(hT_own, 0, 16)], 16, So, blocks, T=To, CB=CB, setup=setup)
    norm_phase(P, cqraw, cqnT, g_q, So, BF16, nfeat=QR)

    wsegs, groups = [], []
    for h in range(NH_M):
        o = h * 256
        wsegs.append((w_uq[:, h * 192:h * 192 + 192], 0, 6, o, 192))
        wsegs.append((o + 160, 0, 6, o + 192, 32))
        wsegs.append((o + 128, 0, 6, o + 224, 32))
        groups.append(dict(accs=[("fm", o, 128, 0, 6)], row0=h * 128, epi=epi_store_fm(qnT, None)))
        groups.append(dict(accs=[("fm", o + 128, 64, 0, 6), ("fm", o + 192, 64, 0, 6)], row0=h * 64,
                           epi=epi_rope(qrT, cos_own, sin_own)))
    blocks = [dict(wsegs=wsegs, groups=groups)]
    gemm_phase(P, [(cqnT, 0, 6)], 6, So, blocks, T=To, CB=4096, nwbuf=1, setup=setup_rope())
    if upto <= 2:
        return done()

    consts = dict(ident=ident, krT=krT, mask_mla=mask_mla, yT=ymT)
    units = [dict(kT=knT[h * 128:(h + 1) * 128, :], qT=qnT[h * 128:(h + 1) * 128, :],
                  qrT=qrT[h * 64:(h + 1) * 64, :], v=vm[:, h * 128:(h + 1) * 128], vkey=h, h=h, m=0,
                  scale=1.0 / math.sqrt(192.0)) for h in range(NH_M)]
    attn_phase(P, S, So, units, consts, "mla")
    if upto <= 3:
        return done()

    consts = dict(ident=ident, abias=abias, mask_diff=mask_diff, yT=ydT, gsub=gsub, lq1=lq1, lk1=lk1, lq2=lq2, lk2=lk2)
    units = []
    for h in range(NH_D):
        rows = [(h * 2 + m) * 128 for m in range(2)]
        units.append(dict(kT=[dkT[r:r + 128, :] for r in rows], qT=[dqT[r:r + 128, :] for r in rows],
                          v=dv[:, h * 256:(h + 1) * 256], vkey=h, h=h, scale=1.0 / math.sqrt(128.0)))
    attn_phase(P, S, So, units, consts, "diff")
    if upto <= 4:
        return done()

    blocks = []
    for cb in range(2):
        c0 = cb * 1024
        blocks.append(dict(
            wsegs=[(w_mla[:, c0:c0 + 1024], 0, 16, 0, 1024), (w_dif[:, c0:c0 + 1024], 16, 16, 0, 1024)],
            groups=[dict(accs=[("fm", m * 128, 128, 0, 16), ("fm", m * 128, 128, 16, 32)], row0=c0 + m * 128,
                         epi=epi_merge(mT, gT)) for m in range(8)]))
    gemm_phase(P, [(ymT, 0, 16), (ydT, 16, 16)], 32, So, blocks, T=To, CB=1024, nwbuf=1, setup=setup_merge)

    blocks = [dict(wsegs=[(w_out[:, :], 0, 16, 0, 2048)],
                   groups=[dict(accs=[("fm", m * 128, 128, 0, 16)], row0=m * 128, epi=epi_resid(x1T, xown))
                           for m in range(16)])]
    gemm_phase(P, [(mT, 0, 16)], 16, So, blocks, T=To, CB=2048, nwbuf=1, setup=setup_resid)
    if upto <= 5:
        return done()

    norm_phase(P, x1T, h2T, g_ffn, So, BF16)
    blocks = []
    for c0 in range(0, DFF, 1024):
        n = min(1024, DFF - c0)
        blocks.append(dict(
            wsegs=[(w_fg[:, c0:c0 + n], 0, 16, 0, n), (w_fu[:, c0:c0 + n], 0, 16, 1024, n)],
            groups=[dict(accs=[("fm", m * 128, 128, 0, 16), ("fm", 1024 + m * 128, 128, 0, 16)], row0=c0 + m * 128,
                         epi=epi_swiglu(aT)) for m in range(n // 128)]))
    gemm_phase(P, [(h2T, 0, 16)], 16, So, blocks, T=To, CB=2048, setup=setup_swiglu)
    blocks = []
    for cb in range(2):
        c0 = cb * 1024
        blocks.append(dict(
            wsegs=[(w_fd[:, c0:c0 + 1024], 0, 44, 0, 1024)],
            groups=[dict(accs=[("fm", m * 128, 128, 0, 44)], row0=c0 + m * 128, epi=epi_resid(x2T, x1T))
                    for m in range(8)]))
    gemm_phase(P, [(aT, 0, 44)], 44, So, blocks, T=To, CB=1024, nwbuf=1, setup=setup_resid)

    norm_phase(P, x2T, outT, g_fin, So, F32)
    return done()


def _col(v, n):
    return np.ascontiguousarray(np.asarray(v, np.float32).reshape(n, 128).T)


def _rep(v):
    v = np.asarray(v, np.float32).reshape(1, -1)
    return np.ascontiguousarray(np.broadcast_to(v, (128, v.shape[1])))


def position_tables(S):
    inv = (10000.0 ** (-np.arange(0, 64, 2, dtype=np.float32) / np.float32(64))).astype(np.float32)
    ang = np.arange(S, dtype=np.float32)[:, None] * inv[None, :]
    c, s = np.cos(ang).astype(np.float32).T, np.sin(ang).astype(np.float32).T
    cos2 = np.ascontiguousarray(np.concatenate([c, c], 0))
    sin2 = np.ascontiguousarray(np.concatenate([-s, s], 0))
    slopes = np.exp2(-8.0 * np.arange(1, NH_D + 1, dtype=np.float32) / NH_D).astype(np.float64)
    per_core = []
    p = np.arange(128)[:, None, None]
    for c_ in range(4):
        ab = np.zeros((128, NH_D, 80), np.float32)
        for h in range(NH_D):
            clampv = slopes[h] * (127 if h == 0 else 255)
            di = np.arange(80)[None, :]
            val = slopes[h] * ((di - 66 - 2 * c_) * 128 + np.arange(128)[:, None])
            ab[:, h, :] = np.minimum(val, clampv)
        kk = np.arange(8)[None, :, None]
        col = np.arange(256)[None, None, :]
        kr = kk * 128 + p
        qr = c_ * 256 + col
        allowed = (kr // 64) <= (qr // 64)
        mm = allowed.astype(np.float32)
        md = np.zeros((NH_D, 128, 8, 256), np.float32)
        for h in range(NH_D):
            corr = np.where(kr > qr, np.exp(-2.0 * slopes[h] * np.maximum(kr - qr, 0)), 1.0)
            md[h] = (allowed * corr).astype(np.float32)
        per_core.append(dict(abias=np.ascontiguousarray(ab.reshape(128, NH_D * 80)), mask_mla=np.ascontiguousarray(mm),
                             mask_diff=md))
    return cos2, sin2, per_core


_CACHE = {}


def kernel(x, attn_norm_g, w_in, b_gate, q_norm_g, w_uq, kv_norm_g, w_ukv,
           lambda_q1, lambda_k1, lambda_q2, lambda_k2, diff_norm_g,
           w_mla_proj, w_diff_proj, w_out, ffn_norm_g, w_ffn_gate, w_ffn_up,
           w_ffn_down, final_norm_g, _debug=False, _upto=99):
    x = np.asarray(x, np.float32)
    B, S, _ = x.shape
    So = S // 4
    nq = So // 256
    key = (S, _debug, _upto)
    if key not in _CACHE:
        _CACHE[key] = build_program(S, debug=_debug, upto=_upto)
    nc, P = _CACHE[key]
    cos2, sin2, per_core = position_tables(S)
    f = lambda a: np.ascontiguousarray(np.asarray(a, np.float32))
    shared = dict(
        w_in=f(w_in[0]), w_uq=f(w_uq[0]), w_ukv=f(w_ukv[0]), w_mla_proj=f(w_mla_proj[0]),
        w_diff_proj=f(w_diff_proj[0]), w_out=f(w_out[0]), w_ffn_gate=f(w_ffn_gate[0]), w_ffn_up=f(w_ffn_up[0]),
        w_ffn_down=f(w_ffn_down[0]),
        g_attn=_col(attn_norm_g[0], 16), g_q=_col(q_norm_g[0], 6), g_kv=_col(kv_norm_g[0], 4),
        g_ffn=_col(ffn_norm_g[0], 16), g_fin=_col(final_norm_g, 16), b_gate=_col(b_gate[0], 32),
        gsub=_rep(diff_norm_g[0]), lq1=_rep(lambda_q1[0]), lk1=_rep(lambda_k1[0]), lq2=_rep(lambda_q2[0]),
        lk2=_rep(lambda_k2[0]), cos_all=cos2, sin_all=sin2, ident=np.eye(128, dtype=np.float32))
    in_maps = []
    own_idx = []
    for core in range(8):
        b, c = core // 4, core % 4
        idx = np.concatenate([np.arange((4 * i + c) * 256, (4 * i + c + 1) * 256) for i in range(nq)])
        own_idx.append(idx)
        xT = np.ascontiguousarray(x[b].T)
        m = dict(shared)
        m.update(xall=xT, xown=np.ascontiguousarray(xT[:, idx]),
                 cos_own=np.ascontiguousarray(cos2[:, idx]), sin_own=np.ascontiguousarray(sin2[:, idx]),
                 abias=per_core[c]["abias"], mask_mla=per_core[c]["mask_mla"], mask_diff=per_core[c]["mask_diff"])
        in_maps.append(m)
    res = run_bass_kernel_spmd(nc, in_maps, core_ids=list(range(8)))
    out = np.empty((B, S, D), np.float32)
    for core in range(8):
        b = core // 4
        out[b, own_idx[core], :] = np.asarray(res.results[core]["outT"], np.float32).T
    if _debug:
        return out, res.results
    return out
```
